# Optimizing a Trainium2 kernel written in Bass

```python
import math
import jax, jax.numpy as jnp
from jax import lax
import numpy as np

D_MODEL = 2048
BATCH = 4
SEQ = 2048
DEPTH = 4

GDN_HEADS = 8
GDN_HEAD_DIM = 128
GDN_WIDTH = GDN_HEADS * GDN_HEAD_DIM
GDN_CONV = 4
GDN_CHUNK = 64
RWKV_HEAD_DIM = 64
RWKV_HEADS = 16
RWKV_WIDTH = RWKV_HEADS * RWKV_HEAD_DIM
RWKV_DECAY_LORA = 96
RWKV_ICLR_LORA = 96
RWKV_SHIFT_COLS = 3 * RWKV_WIDTH + RWKV_DECAY_LORA + RWKV_ICLR_LORA
RWKV_LN_EPS = 64e-5
S5_GROUP = 16
S5_WIDTH = 1024
S5_GROUPS = S5_WIDTH // S5_GROUP
S5_STATE = 64
N_BRANCH = 3
BRANCH_WIDTH = 1024
NORM_EPS = 1e-6

SPLIT_SIZES = (3 * GDN_WIDTH, GDN_WIDTH, GDN_HEADS, GDN_HEADS,
               RWKV_SHIFT_COLS, RWKV_WIDTH,
               S5_WIDTH, S5_WIDTH,
               N_BRANCH * D_MODEL)
IN_COLS = sum(SPLIT_SIZES)

kernel_name = "hybrid_gdn_rwkv7_s5_gated_merge"


def _split_cols(t, sizes):
    out, start = [], 0
    for s in sizes:
        out.append(t[..., start:start + s])
        start += s
    return out


def rms_norm(x, w, eps=NORM_EPS):
    xf = x.astype(jnp.float32)
    y = xf * lax.rsqrt(jnp.mean(xf * xf, axis=-1, keepdims=True) + eps)
    return (y * w.astype(jnp.float32)).astype(x.dtype)


def l2_normalize(x, eps=1e-6):
    return x * lax.rsqrt(jnp.sum(x * x, axis=-1, keepdims=True) + eps)


def causal_depthwise_conv(x, w):
    k_width, ch = w.shape
    return lax.conv_general_dilated(x, w[:, None, :].astype(x.dtype), window_strides=(1,),
                                    padding=[(k_width - 1, 0)],
                                    dimension_numbers=('NWC', 'WIO', 'NWC'),
                                    feature_group_count=ch)


def _to_chunks(t, n):
    bsz, _, h = t.shape[:3]
    t = t.reshape((bsz, n, GDN_CHUNK, h) + t.shape[3:])
    return t.transpose((1, 0, 3, 2) + tuple(range(4, t.ndim)))


def gated_delta_rule_chunked(q, k, v, beta, g):
    bsz, seq, h, dk = q.shape
    dv = v.shape[-1]
    n = seq // GDN_CHUNK
    q, k, v, beta, g = (_to_chunks(t, n) for t in (q, k, v, beta, g))
    g_cum = jnp.cumsum(g, axis=-1)
    idx = jnp.arange(GDN_CHUNK)
    strict = idx[:, None] > idx[None, :]
    incl = idx[:, None] >= idx[None, :]
    diff = g_cum[..., :, None] - g_cum[..., None, :]
    decay_strict = jnp.exp(jnp.where(strict, diff, -jnp.inf))
    decay_incl = jnp.exp(jnp.where(incl, diff, -jnp.inf))
    k_beta = k * beta[..., None]
    lower = jnp.einsum('nbhcd,nbhsd->nbhcs', k_beta, k) * decay_strict
    eye = jnp.eye(GDN_CHUNK, dtype=q.dtype)
    t_mat = lax.linalg.triangular_solve(eye + lower, jnp.broadcast_to(eye, lower.shape),
                                        left_side=True, lower=True)
    u = jnp.einsum('nbhcs,nbhse->nbhce', t_mat, v * beta[..., None])
    w = jnp.einsum('nbhcs,nbhsd->nbhcd', t_mat, k_beta * jnp.exp(g_cum)[..., None])
    attn = jnp.einsum('nbhcd,nbhsd->nbhcs', q, k) * decay_incl
    q_dec = q * jnp.exp(g_cum)[..., None]
    g_last = g_cum[..., -1]
    k_dec = k * jnp.exp(g_last[..., None] - g_cum)[..., None]

    def step(state, xs):
        q_i, k_i, u_i, w_i, a_i, gl_i = xs
        v_new = u_i - jnp.einsum('bhcd,bhde->bhce', w_i, state)
        o_i = (jnp.einsum('bhcd,bhde->bhce', q_i, state)
               + jnp.einsum('bhcs,bhse->bhce', a_i, v_new))
        state = state * jnp.exp(gl_i)[..., None, None] + jnp.einsum('bhcd,bhce->bhde', k_i, v_new)
        return state, o_i

    s0 = jnp.zeros((bsz, h, dk, dv), jnp.float32)
    _, o = lax.scan(step, s0, (q_dec, k_dec, u, w, attn, g_last))
    return o.transpose(1, 0, 3, 2, 4).reshape(bsz, seq, h, dv)


def gdn_branch(qkv, z, b_logit, a_logit, conv_w, a_log, dt_bias, norm_w):
    bsz, seq, _ = qkv.shape
    qkv = jax.nn.silu(causal_depthwise_conv(qkv, conv_w)).astype(jnp.float32)
    q, k, v = jnp.split(qkv, 3, axis=-1)
    hs = (bsz, seq, GDN_HEADS, GDN_HEAD_DIM)
    q = l2_normalize(q.reshape(hs)) * (GDN_HEAD_DIM ** -0.5)
    k = l2_normalize(k.reshape(hs))
    v = v.reshape(hs)
    beta = jax.nn.sigmoid(b_logit.astype(jnp.float32))
    g = -jnp.exp(a_log.astype(jnp.float32)) * jax.nn.softplus(
        a_logit.astype(jnp.float32) + dt_bias.astype(jnp.float32))
    o = gated_delta_rule_chunked(q, k, v, beta, g)
    o = rms_norm(o, norm_w) * jax.nn.silu(z.astype(jnp.float32).reshape(hs))
    return o.reshape(bsz, seq, GDN_WIDTH)


def rwkv7_scan(r, w, k, v, kk, a):
    bsz, seq, h, n = r.shape

    def step(state, xs):
        r_t, w_t, k_t, v_t, kk_t, a_t = xs
        sa = jnp.einsum('bhvk,bhk->bhv', state, -kk_t)
        state = (state * w_t[:, :, None, :] + sa[..., None] * (kk_t * a_t)[:, :, None, :]
                 + v_t[..., None] * k_t[:, :, None, :])
        return state, jnp.einsum('bhvk,bhk->bhv', state, r_t)

    xs = tuple(t.transpose(1, 0, 2, 3) for t in (r, w, k, v, kk, a))
    s0 = jnp.zeros((bsz, h, n, n), jnp.float32)
    _, y = lax.scan(step, s0, xs)
    return y.transpose(1, 0, 2, 3)


def rwkv7_branch(feat, z, mu, w0, w_up, a0, a_up, k_k, k_a, r_k, lnx_w, lnx_b):
    bsz, seq, _ = feat.shape
    f32 = jnp.float32
    feat = feat.astype(f32)
    prev = jnp.pad(feat[:, :-1], ((0, 0), (1, 0), (0, 0)))
    feat = feat + (prev - feat) * mu.astype(f32)
    r, k, v, wl, al = _split_cols(feat, (RWKV_WIDTH,) * 3 + (RWKV_DECAY_LORA, RWKV_ICLR_LORA))
    w_pre = w0.astype(f32) + jnp.tanh(wl) @ w_up.astype(f32)
    decay = jnp.exp(-jnp.exp(-jax.nn.softplus(-w_pre) - 0.5))
    a = jax.nn.sigmoid(a0.astype(f32) + al @ a_up.astype(f32))
    hs = (bsz, seq, RWKV_HEADS, RWKV_HEAD_DIM)
    kk = l2_normalize((k * k_k.astype(f32)).reshape(hs))
    k = k * (1.0 + (a - 1.0) * k_a.astype(f32))
    r, k, v, a, decay = (t.reshape(hs) for t in (r, k, v, a, decay))
    y = rwkv7_scan(r, decay, k, v, kk, a)
    mean = jnp.mean(y, axis=-1, keepdims=True)
    var = jnp.mean(jnp.square(y - mean), axis=-1, keepdims=True)
    y = ((y - mean) * lax.rsqrt(var + RWKV_LN_EPS) * lnx_w.astype(f32).reshape(RWKV_HEADS, RWKV_HEAD_DIM)
         + lnx_b.astype(f32).reshape(RWKV_HEADS, RWKV_HEAD_DIM))
    y = y + jnp.sum(r * k * r_k.astype(f32), axis=-1, keepdims=True) * v
    y = y * jax.nn.silu(z.astype(f32).reshape(hs))
    return y.reshape(bsz, seq, RWKV_WIDTH)


def s5_branch(u, z, a_re, a_im, log_dt, b_re, b_im, c_re, c_im, d_skip, glu_w, glu_b):
    bsz, seq, _ = u.shape
    f32 = jnp.float32
    u = u.astype(f32)
    ug = u.reshape(bsz, seq, S5_GROUPS, S5_GROUP)
    a_re, a_im, b_re, b_im, c_re, c_im = (t.astype(f32) for t in (a_re, a_im, b_re, b_im, c_re, c_im))
    dt = jnp.exp(log_dt.astype(f32))[:, None]
    mag = jnp.exp(a_re * dt)
    ab_re, ab_im = mag * jnp.cos(a_im * dt), mag * jnp.sin(a_im * dt)
    den = a_re * a_re + a_im * a_im
    coef_re = ((ab_re - 1.0) * a_re + ab_im * a_im) / den
    coef_im = (ab_im * a_re - (ab_re - 1.0) * a_im) / den
    bb_re = coef_re[..., None] * b_re - coef_im[..., None] * b_im
    bb_im = coef_re[..., None] * b_im + coef_im[..., None] * b_re
    bu_re = jnp.einsum('gpc,blgc->blgp', bb_re, ug)
    bu_im = jnp.einsum('gpc,blgc->blgp', bb_im, ug)
    shp = (1, seq, S5_GROUPS, S5_STATE)
    a_re_t = jnp.broadcast_to(ab_re, shp)
    a_im_t = jnp.broadcast_to(ab_im, shp)

    def combine(e1, e2):
        a1r, a1i, b1r, b1i = e1
        a2r, a2i, b2r, b2i = e2
        return (a1r * a2r - a1i * a2i, a1r * a2i + a1i * a2r,
                a2r * b1r - a2i * b1i + b2r, a2r * b1i + a2i * b1r + b2i)

    _, _, s_re, s_im = lax.associative_scan(combine, (a_re_t, a_im_t, bu_re, bu_im), axis=1)
    y = jnp.einsum('gcp,blgp->blgc', c_re, s_re) - jnp.einsum('gcp,blgp->blgc', c_im, s_im)
    y = y.reshape(bsz, seq, S5_WIDTH) + d_skip.astype(f32) * u
    y = jax.nn.gelu(y)
    y = y * jax.nn.sigmoid(y @ glu_w.astype(f32) + glu_b.astype(f32))
    return y * jax.nn.silu(z.astype(f32))


def setup_inputs(seed: int = 0) -> dict:
    key = jax.random.key(seed)
    ks = iter(jax.random.split(key, 40))
    f32 = jnp.float32

    def nrm(shape, s):
        return s * jax.random.normal(next(ks), shape, f32)

    def unif(shape, lo, hi):
        return jax.random.uniform(next(ks), shape, f32, lo, hi)

    gdn_dt = jnp.exp(unif((DEPTH, GDN_HEADS), math.log(1e-3), math.log(1e-1)))
    return {
        "x": nrm((BATCH, SEQ, D_MODEL), 1.0),
        "norm_w": 1.0 + nrm((DEPTH, D_MODEL), 0.02),
        "w_in": nrm((DEPTH, D_MODEL, IN_COLS), D_MODEL ** -0.5),
        "gdn_conv_w": nrm((DEPTH, GDN_CONV, 3 * GDN_WIDTH), GDN_CONV ** -0.5),
        "gdn_a_log": jnp.log(unif((DEPTH, GDN_HEADS), 1.0, 16.0)),
        "gdn_dt_bias": jnp.log(jnp.expm1(gdn_dt)),
        "gdn_norm_w": 1.0 + nrm((DEPTH, GDN_HEAD_DIM), 0.02),
        "rwkv_mu": unif((DEPTH, RWKV_SHIFT_COLS), 0.0, 1.0),
        "rwkv_w0": unif((DEPTH, RWKV_WIDTH), -6.5, -1.5),
        "rwkv_w_up": nrm((DEPTH, RWKV_DECAY_LORA, RWKV_WIDTH), 0.5 * RWKV_DECAY_LORA ** -0.5),
        "rwkv_a0": nrm((DEPTH, RWKV_WIDTH), 0.1),
        "rwkv_a_up": nrm((DEPTH, RWKV_ICLR_LORA, RWKV_WIDTH), 0.5 * RWKV_ICLR_LORA ** -0.5),
        "rwkv_k_k": 0.85 + nrm((DEPTH, RWKV_WIDTH), 0.05),
        "rwkv_k_a": 1.0 + nrm((DEPTH, RWKV_WIDTH), 0.05),
        "rwkv_r_k": nrm((DEPTH, RWKV_HEADS, RWKV_HEAD_DIM), 0.1),
        "rwkv_lnx_w": 1.0 + nrm((DEPTH, RWKV_WIDTH), 0.02),
        "rwkv_lnx_b": nrm((DEPTH, RWKV_WIDTH), 0.02),
        "s5_a_re": -0.5 + nrm((DEPTH, S5_GROUPS, S5_STATE), 0.01),
        "s5_a_im": jnp.pi * jnp.arange(S5_STATE, dtype=f32)[None, None, :] + nrm((DEPTH, S5_GROUPS, S5_STATE), 0.01),
        "s5_log_dt": unif((DEPTH, S5_GROUPS), math.log(1e-3), math.log(1e-1)),
        "s5_b_re": nrm((DEPTH, S5_GROUPS, S5_STATE, S5_GROUP), (2 * S5_GROUP) ** -0.5),
        "s5_b_im": nrm((DEPTH, S5_GROUPS, S5_STATE, S5_GROUP), (2 * S5_GROUP) ** -0.5),
        "s5_c_re": nrm((DEPTH, S5_GROUPS, S5_GROUP, S5_STATE), (2 * S5_STATE) ** -0.5),
        "s5_c_im": nrm((DEPTH, S5_GROUPS, S5_GROUP, S5_STATE), (2 * S5_STATE) ** -0.5),
        "s5_d": nrm((DEPTH, S5_WIDTH), 1.0),
        "s5_glu_w": nrm((DEPTH, S5_WIDTH, S5_WIDTH), S5_WIDTH ** -0.5),
        "s5_glu_b": nrm((DEPTH, S5_WIDTH), 0.02),
        "gate_b": nrm((DEPTH, N_BRANCH, D_MODEL), 0.02),
        "w_branch": nrm((DEPTH, N_BRANCH, BRANCH_WIDTH, D_MODEL), BRANCH_WIDTH ** -0.5),
        "w_out": nrm((DEPTH, D_MODEL, D_MODEL), D_MODEL ** -0.5),
        "final_norm_w": 1.0 + nrm((D_MODEL,), 0.02),
    }


def reference(x, norm_w, w_in, gdn_conv_w, gdn_a_log, gdn_dt_bias, gdn_norm_w,
              rwkv_mu, rwkv_w0, rwkv_w_up, rwkv_a0, rwkv_a_up, rwkv_k_k, rwkv_k_a, rwkv_r_k,
              rwkv_lnx_w, rwkv_lnx_b,
              s5_a_re, s5_a_im, s5_log_dt, s5_b_re, s5_b_im, s5_c_re, s5_c_im, s5_d,
              s5_glu_w, s5_glu_b,
              gate_b, w_branch, w_out, final_norm_w):
    bsz, seq, _ = x.shape
    for i in range(DEPTH):
        h = rms_norm(x, norm_w[i])
        proj = h @ w_in[i]
        (g_qkv, g_z, g_b, g_a, r_feat, r_z, s_u, s_z, gate_logits) = _split_cols(proj, SPLIT_SIZES)
        o_a = gdn_branch(g_qkv, g_z, g_b, g_a, gdn_conv_w[i], gdn_a_log[i], gdn_dt_bias[i], gdn_norm_w[i])
        o_b = rwkv7_branch(r_feat, r_z, rwkv_mu[i], rwkv_w0[i], rwkv_w_up[i], rwkv_a0[i], rwkv_a_up[i],
                           rwkv_k_k[i], rwkv_k_a[i], rwkv_r_k[i], rwkv_lnx_w[i], rwkv_lnx_b[i])
        o_c = s5_branch(s_u, s_z, s5_a_re[i], s5_a_im[i], s5_log_dt[i], s5_b_re[i], s5_b_im[i],
                        s5_c_re[i], s5_c_im[i], s5_d[i], s5_glu_w[i], s5_glu_b[i])
        branches = jnp.stack([o_a, o_b, o_c], axis=2)
        branch_proj = jnp.einsum('blnc,ncd->blnd', branches, w_branch[i].astype(jnp.float32))
        gates = jax.nn.sigmoid(gate_logits.astype(jnp.float32).reshape(bsz, seq, N_BRANCH, D_MODEL)
                               + gate_b[i].astype(jnp.float32))
        merged = jnp.sum(gates * branch_proj, axis=2)
        x = x + (merged @ w_out[i].astype(jnp.float32)).astype(x.dtype)
    return rms_norm(x, final_norm_w)
```

```python
import numpy as np
import concourse.bass as bass
import concourse.mybir as mybir
from concourse.bass_utils import run_bass_kernel_spmd
from contextlib import ExitStack

F32 = mybir.dt.float32
BF16 = mybir.dt.bfloat16
I32 = mybir.dt.int32
AF = mybir.ActivationFunctionType
ALU = mybir.AluOpType
AX = mybir.AxisListType

ENGS = ("pe", "act", "dve", "pool", "sp")

D = 2048
KT = 16
NCT = 131
NORM_EPS = 1e-6


class T:
    def __init__(self, name, shape, dtype, space, handle):
        self.name, self.shape, self.dtype, self.space = name, tuple(shape), dtype, space
        self.h = handle
        self.hist = []

    def __getitem__(self, key):
        if not isinstance(key, tuple):
            key = (key,)
        key = key + (slice(None),) * (len(self.shape) - len(key))
        rng = []
        for k, n in zip(key, self.shape):
            if isinstance(k, slice):
                a = 0 if k.start is None else k.start
                b = n if k.stop is None else k.stop
                assert k.step in (None, 1)
            else:
                a, b = k, k + 1
            assert 0 <= a < b <= n, (self.name, key, self.shape)
            rng.append((a, b))
        return V(self, tuple(rng), key)

    def all(self):
        return self[tuple(slice(None) for _ in self.shape)]


class V:
    def __init__(self, t, rng, key, xf=None):
        self.t, self.rng, self.key, self.xf = t, rng, key, xf

    @property
    def ap(self):
        a = self.t.h[self.key]
        if self.xf is not None:
            a = self.xf(a)
        return a

    def x(self, fn):
        old = self.xf
        if old is None:
            return V(self.t, self.rng, self.key, fn)
        return V(self.t, self.rng, self.key, lambda a: fn(old(a)))

    def bc(self, shape):
        return self.x(lambda a: a.to_broadcast(list(shape)))

    @property
    def shape(self):
        return tuple(b - a for a, b in self.rng)


def _overlap(r1, r2):
    for (a, b), (c, d) in zip(r1, r2):
        if b <= c or d <= a:
            return False
    return True


def _contains(r1, r2):
    for (a, b), (c, d) in zip(r1, r2):
        if c < a or d > b:
            return False
    return True


class Op:
    __slots__ = ("eng", "fn", "is_dma", "deps", "needed", "count", "dslot", "dval", "idx", "tag")


class FW:
    NDMA_SEMS = 32

    def __init__(self, nc):
        self.nc = nc
        self.ops = []
        self.stack = ExitStack()
        self.tensors = {}
        self.rings = {}
        self.cursor = 16512
        self.cur_stack = []
        self.uid = 0
        self.dma_i = 0
        self.dma_last = [None] * self.NDMA_SEMS
        self.junk = {e: self.sbuf("junk_" + e, [128, 16], F32) for e in ("act", "dve", "pool")}

    SBUF_LIMIT = 229300

    def sbuf(self, name, shape, dtype=F32):
        esz = {F32: 4, BF16: 2, I32: 4}[dtype]
        nbytes = int(np.prod(shape[1:])) * esz
        nbytes = (nbytes + 63) // 64 * 64
        off = self.cursor
        assert off + nbytes <= self.SBUF_LIMIT, ("SBUF overflow", name, off, nbytes)
        self.cursor = off + nbytes
        self.uid += 1
        h = self.nc.alloc_sbuf_tensor_at("%s_u%d" % (name, self.uid), list(shape), dtype, offset=off)
        t = T(name, shape, dtype, "sbuf", h)
        self.tensors[name] = t
        return t

    def push(self):
        self.cur_stack.append(self.cursor)

    def pop(self):
        self.barrier()
        self.cursor = self.cur_stack.pop()

    def barrier(self):
        ms = []
        for e in ("act", "dve", "pool"):
            j = self.junk[e]
            if e == "act":
                ms.append(self.op(e, "activation", [], [j[:, 0:8]], j[:, 0:8], j[:, 8:16], AF.Copy).idx)
            else:
                ms.append(self.op(e, "memset", [], [j.all()], j.all(), 0.0).idx)
        dl = [o.idx for o in self.dma_last if o is not None]
        for e in ENGS:
            op = Op()
            op.eng, op.fn, op.is_dma, op.tag = e, None, False, "bar"
            op.idx = len(self.ops)
            op.needed = False
            op.count = None
            op.deps = list(ms) + list(dl)
            self.ops.append(op)
        for t in self.tensors.values():
            t.hist = []

    def psum(self, name, shape, dtype=F32):
        h = self.stack.enter_context(self.nc.psum_tensor(name, list(shape), dtype))
        t = T(name, shape, dtype, "psum", h)
        self.tensors[name] = t
        return t

    def dram(self, name, shape, dtype=F32, kind="Internal"):
        h = self.nc.dram_tensor(name, list(shape), dtype, kind=kind).ap()
        t = T(name, shape, dtype, "dram", h)
        self.tensors[name] = t
        return t

    def ring(self, name, n, shape, dtype=F32, space="sbuf"):
        mk = self.sbuf if space == "sbuf" else self.psum
        self.rings[name] = [[mk("%s_%d" % (name, i), shape, dtype) for i in range(n)], 0]

    def next(self, name):
        r = self.rings[name]
        t = r[0][r[1] % len(r[0])]
        r[1] += 1
        return t

    def _record(self, eng, fn, reads, writes, is_dma=False, tag=""):
        op = Op()
        op.eng, op.fn, op.is_dma, op.tag = eng, fn, is_dma, tag
        op.idx = len(self.ops)
        op.needed = False
        op.count = None
        deps = set()

        def reg(v):
            return tuple((0, n) for n in v.t.shape) if v.t.space == "psum" else v.rng
        for v in reads:
            ps_ = v.t.space == "psum"
            for (r, oi, w, e) in v.t.hist:
                if (w or (ps_ and e != eng)) and _overlap(r, reg(v)):
                    deps.add(oi)
        for v in writes:
            for (r, oi, w, e) in v.t.hist:
                if _overlap(r, reg(v)):
                    deps.add(oi)
        for v in writes:
            t = v.t
            t.hist = [h for h in t.hist if not _contains(reg(v), h[0])]
            t.hist.append((reg(v), op.idx, True, eng))
        for v in reads:
            t = v.t
            if not is_dma:
                t.hist = [h for h in t.hist
                          if not ((not h[2]) and h[3] == eng and h[0] == reg(v) and h[1] != op.idx
                               and not self.ops[h[1]].is_dma)]
            t.hist.append((reg(v), op.idx, False, eng))
        deps.discard(op.idx)
        op.deps = [d for d in deps if not (eng == "pe" and self.ops[d].eng == "pe" and not self.ops[d].is_dma
                                           and not is_dma)]
        if is_dma:
            op.dslot = self.dma_i % self.NDMA_SEMS
            op.dval = 16 * (self.dma_i // self.NDMA_SEMS + 1)
            prev = self.dma_last[op.dslot]
            if prev is not None:
                op.deps.append(prev.idx)
            self.dma_last[op.dslot] = op
            self.dma_i += 1
        self.ops.append(op)
        return op

    def op(self, eng, method, reads, writes, *args, **kw):
        def conv(a):
            return a.ap if isinstance(a, V) else a

        def fn(e):
            return getattr(e, method)(*[conv(a) for a in args], **{k: conv(v) for k, v in kw.items()})
        return self._record(eng, fn, reads, writes, tag=method)

    def dma(self, eng, out, in_, **kw):
        def fn(e):
            return e.dma_start(out=out.ap, in_=in_.ap, **kw)
        return self._record(eng, fn, [in_], [out], is_dma=True, tag="dma")

    def matmul(self, out, lhsT, rhs, start=True, stop=True, **kw):
        rd = [lhsT, rhs] + ([] if start else [out])
        return self.op("pe", "matmul", rd, [out], out, lhsT, rhs, start=start, stop=stop, **kw)

    def transpose(self, out, in_, ident):
        return self.op("pe", "transpose", [in_, ident], [out], out, in_, ident)

    def act(self, out, in_, func, bias=None, scale=None, extra_reads=()):
        kw = {}
        rd = [in_] + list(extra_reads)
        if bias is not None:
            kw["bias"] = bias
            if isinstance(bias, V):
                rd.append(bias)
        if scale is not None:
            kw["scale"] = scale
            if isinstance(scale, V):
                rd.append(scale)
        return self.op("act", "activation", rd, [out], out, in_, func, **kw)

    def tt(self, out, in0, in1, op, eng="dve"):
        return self.op(eng, "tensor_tensor", [in0, in1], [out], out, in0, in1, op)

    def ts(self, out, in0, s1, op0, s2=None, op1=None, eng="dve"):
        rd = [in0] + [s for s in (s1, s2) if isinstance(s, V)]
        if op1 is None:
            return self.op(eng, "tensor_scalar", rd, [out], out, in0, s1, None, op0)
        return self.op(eng, "tensor_scalar", rd, [out], out, in0, s1, s2, op0, op1)

    def stt(self, out, in0, scalar, in1, op0, op1, eng="dve"):
        rd = [in0, in1] + ([scalar] if isinstance(scalar, V) else [])
        return self.op(eng, "scalar_tensor_tensor", rd, [out], out, in0, scalar, in1, op0, op1)

    def copy(self, out, in_, eng="dve"):
        if eng == "act":
            return self.op("act", "activation", [in_], [out], out, in_, AF.Copy)
        return self.op(eng, "tensor_copy", [in_], [out], out, in_)

    def memset(self, out, val, eng="dve"):
        return self.op(eng, "memset", [], [out], out, val)

    def scan(self, out, d0, d1, init, op0=ALU.mult, op1=ALU.add):
        rd = [d0, d1] + ([init] if isinstance(init, V) else [])
        return self.op("dve", "tensor_tensor_scan", rd, [out], out, d0, d1, init, op0, op1)

    def emit(self):
        nc = self.nc
        ops = self.ops
        for o in ops:
            for d in o.deps:
                ops[d].needed = True
        cnt = {e: 0 for e in ENGS}
        for o in ops:
            if o.is_dma:
                pass
            elif o.needed:
                cnt[o.eng] += 1
                o.count = cnt[o.eng]
        st = self.stack
        sem_e = {e: st.enter_context(nc.semaphore("sem_" + e)) for e in ENGS if e != "sp"}
        sem_d = [st.enter_context(nc.semaphore("semd%d" % i)) for i in range(self.NDMA_SEMS)]
        per_eng = {e: [o for o in ops if o.eng == e] for e in ENGS}
        final = {}
        for o in ops:
            if o.is_dma:
                final[o.dslot] = o.dval

        def run_engine(ename, e):
            waited = {}

            def wait(key, sem, val):
                if waited.get(key, 0) >= val:
                    return
                e.wait_ge(sem, val)
                waited[key] = val

            for o in per_eng[ename]:
                for d in sorted(o.deps):
                    p = ops[d]
                    if p.is_dma:
                        wait(("d", p.dslot), sem_d[p.dslot], p.dval)
                    else:
                        wait(("e", p.eng), sem_e[p.eng], p.count)
                if o.fn is None:
                    continue
                ins = o.fn(e)
                if o.is_dma:
                    ins.then_inc(sem_d[o.dslot], 16)
                elif o.needed:
                    ins.then_inc(sem_e[o.eng], 1)
            if ename == "sp":
                for slot, val in sorted(final.items()):
                    wait(("d", slot), sem_d[slot], val)

        with nc.Block() as block:
            @block.tensor
            def _(e):
                run_engine("pe", e)

            @block.scalar
            def _(e):
                run_engine("act", e)

            @block.vector
            def _(e):
                run_engine("dve", e)

            @block.gpsimd
            def _(e):
                run_engine("pool", e)

            @block.sync
            def _(e):
                run_engine("sp", e)
        st.close()
        return {e: len(per_eng[e]) for e in ENGS}


def col_tiles():
    tiles = []
    c = 0
    for _ in range(24 + 8):
        tiles.append((c, 128)); c += 128
    tiles.append((c, 16)); c += 16
    for _ in range(24):
        tiles.append((c, 128)); c += 128
    tiles.append((c, 96)); c += 96
    tiles.append((c, 96)); c += 96
    for _ in range(8 + 8 + 8 + 48):
        tiles.append((c, 128)); c += 128
    assert c == 16592 and len(tiles) == NCT
    return tiles


CT_GQKV, CT_GZ, CT_GBA, CT_RR, CT_RK, CT_RV, CT_RWL, CT_RAL, CT_RZ, CT_SU, CT_SZ, CT_GATE = \
    0, 24, 32, 33, 41, 49, 57, 58, 59, 67, 75, 83


def relayout_w_in(w):
    out = np.zeros((NCT, 128, KT, 128), np.float32)
    for j, (c0, n) in enumerate(col_tiles()):
        blk = w[:, c0:c0 + n].reshape(KT, 128, n)
        if j == CT_GBA:
            out[j, :, :, 0:8] = blk[:, :, 0:8].transpose(1, 0, 2)
            out[j, :, :, 32:40] = blk[:, :, 8:16].transpose(1, 0, 2)
            continue
        out[j, :, :, :n] = blk.transpose(1, 0, 2)
    return out.reshape(NCT * 128, KT * 128)


TWO_PI_1 = 6.28125
TWO_PI_2 = float(2 * np.pi - 6.28125)


def make_consts():
    c = {}
    c["c_ones"] = np.ones((128, 128), np.float32)
    c["c_ident"] = np.eye(128, dtype=np.float32)
    c["c_iota"] = np.tile(np.arange(1, 129, dtype=np.float32)[None, :], (128, 1))
    gm = np.zeros((128, 8), np.float32)
    for r in range(128):
        gm[r, r // 16] = 1.0
    c["c_gmask"] = gm
    sw = np.zeros((128, 128), np.float32)
    for m in range(128):
        sw[(m + 64) % 128, m] = 1.0
    c["c_swap"] = sw
    sg = np.ones((128, 1), np.float32)
    sg[:64] = -1.0
    c["c_sgn"] = sg
    NEG = -30000.0
    r_ = np.arange(128)[:, None]
    c_ = np.arange(128)[None, :]
    c["c_mask_su"] = np.where(r_ < c_, 0.0, NEG).astype(np.float32)
    c["c_mask_sl"] = np.where(r_ > c_, 0.0, NEG).astype(np.float32)
    c["c_mask_iu"] = np.where(r_ <= c_, 0.0, NEG).astype(np.float32)
    c["c_m01_su"] = (r_ < c_).astype(np.float32)
    c["c_m01_sl"] = (r_ > c_).astype(np.float32)
    c["c_m01_iu"] = (r_ <= c_).astype(np.float32)
    sel = np.zeros((64, 16, 128), np.float32)
    for h in range(8):
        sel[h, h, :] = 1.0
        sel[32 + h, h, :] = 1.0
        sel[32 + h, 8 + h, :] = 1.0
    c["c_sel"] = sel.reshape(64, 16 * 128)
    bo = np.zeros((128, 128), np.float32)
    bo[:64, :64] = 1.0
    bo[64:, 64:] = 1.0
    c["c_blk64"] = bo
    return c


class Prog:
    def __init__(self, L, NL, debug=False):
        self.L, self.NL, self.debug = L, NL, debug
        self.NB = L // 512
        nc = bass.Bass("TRN2", target_bir_lowering=False)
        self.nc = nc
        self.fw = FW(nc)
        self.evac_i = 0

    def scratch(self, name, shape, dtype=F32):
        return self.fw.dram(name, shape, dtype, kind=("ExternalOutput" if self.debug else "Internal"))

    def ext(self, n, s, dt=F32):
        return self.fw.dram(n, s, dt, kind="ExternalInput")

    def setup(self):
        fw, L, NL = self.fw, self.L, self.NL
        ext = self.ext
        self.xT_in = ext("xT", [D, L])
        self.w_in = ext("w_in", [NL * NCT * 128, KT * 128])
        self.norm_w = ext("norm_w", [128, NL * KT])
        self.cst = {k: ext(k, list(v.shape)) for k, v in make_consts().items()}
        self.s5_pg = ext("s5_pg", [128, NL * 3 * 64])
        self.s5_b = ext("s5_b", [64, NL * 2 * 64 * 16])
        self.s5_c = ext("s5_c", [128, NL * 2 * 64 * 16])
        self.s5_vec = ext("s5_vec", [128, NL * 2 * 8])
        self.s5_glu = ext("s5_glu", [NL * 128, 8 * 1024])
        self.gdn_par = ext("gdn_par", [64, NL * 2])
        self.gdn_vec = ext("gdn_vec", [128, NL * 5 * 24])
        self.rw_vec = ext("rw_vec", [128, NL * 80])
        self.rw_mul = ext("rw_mul", [128, NL * 2])
        self.rw_up = ext("rw_up", [NL * 96, 2048])
        self.gate_b = ext("gate_b", [128, NL * 48])
        self.w_br = ext("w_br", [NL * 48 * 128, 1024])
        self.w_out = ext("w_out", [NL * 16 * 128, 2048])
        self.fnorm_w = ext("fnorm_w", [128, 16])
        self.out = fw.dram("out", [D, L], F32, kind="ExternalOutput")
        self.xT = [self.scratch("xs%d" % i, [D, L]) for i in range(2)]
        self.projT = self.scratch("projT", [NCT * 128, L])
        self.oT = self.scratch("oT", [3 * 1024, L], BF16)
        self.ones = fw.sbuf("ones", [128, 128], F32)
        self.ident = fw.sbuf("ident", [128, 128], F32)
        self.normw = fw.sbuf("normw", [128, NL * KT], F32)
        self.identb = fw.sbuf("identb", [128, 128], BF16)
        self.onesb = fw.sbuf("onesb", [128, 128], BF16)
        fw.dma("pool", self.identb.all(), self.cst["c_ident"].all())
        fw.dma("pool", self.onesb.all(), self.cst["c_ones"].all())
        self.halfpi = fw.sbuf("halfpi", [128, 1], F32)
        fw.memset(self.halfpi.all(), float(np.pi / 2))
        fw.ring("ps", 6, [128, 512], F32, space="psum")
        fw.ring("psacc", 2, [128, 512], F32, space="psum")
        fw.dma("sp", self.ones.all(), self.cst["c_ones"].all())
        fw.dma("sp", self.ident.all(), self.cst["c_ident"].all())
        fw.dma("sp", self.normw.all(), self.norm_w.all())

    def evac(self, out, in_):
        self.evac_i += 1
        if self.evac_i % 2:
            self.fw.copy(out, in_, eng="act")
        else:
            self.fw.copy(out, in_, eng="dve")

    def phase_A(self, li, xsrc):
        fw, L = self.fw, self.L
        fw.push()
        hT = fw.sbuf("hT", [128, KT, L], BF16)
        rstd = fw.sbuf("rstd", [128, L], F32)
        fw.ring("xt", 3, [128, 512], F32)
        fw.ring("sq", 2, [128, 512], F32)
        fw.ring("wt", 3, [128, KT * 128], BF16)
        fw.ring("stage", 2, [128, L], F32)
        for tb in range(self.NB):
            ts_ = slice(tb * 512, (tb + 1) * 512)
            ps = fw.next("ps")
            for kt in range(KT):
                xt = fw.next("xt")
                fw.dma("sp", xt.all(), xsrc[kt * 128:(kt + 1) * 128, ts_])
                sq = fw.next("sq")
                fw.act(sq.all(), xt.all(), AF.Square)
                fw.matmul(ps.all(), self.ones.all(), sq.all(), start=(kt == 0), stop=(kt == KT - 1))
            tmp = fw.next("sq")
            fw.ts(tmp.all(), ps.all(), 1.0 / D, ALU.mult, NORM_EPS, ALU.add)
            fw.act(tmp.all(), tmp.all(), AF.Ln)
            fw.act(rstd[:, ts_], tmp.all(), AF.Exp, scale=-0.5)
        for tb in range(self.NB):
            ts_ = slice(tb * 512, (tb + 1) * 512)
            for kt in range(KT):
                xt = fw.next("xt")
                fw.dma("sp", xt.all(), xsrc[kt * 128:(kt + 1) * 128, ts_])
                fw.stt(hT[:, kt, ts_], xt.all(), self.normw[:, li * KT + kt:li * KT + kt + 1],
                       rstd[:, ts_], ALU.mult, ALU.mult)
        for j in range(NCT):
            wt = fw.next("wt")
            r0 = (li * NCT + j) * 128
            fw.dma("pool", wt.all(), self.w_in[r0:r0 + 128, :])
            st = fw.next("stage")
            for tb in range(self.NB):
                ts_ = slice(tb * 512, (tb + 1) * 512)
                ps = fw.next("ps")
                for kt in range(KT):
                    fw.matmul(ps.all(), wt[:, kt * 128:(kt + 1) * 128], hT[:, kt, ts_],
                              start=(kt == 0), stop=(kt == KT - 1))
                self.evac(st[:, ts_], ps.all())
            fw.dma("sp", self.projT[j * 128:(j + 1) * 128, :], st.all())
        fw.pop()

    def sincos(self, cos_o, sin_o, phi, shape, tagn):
        fw = self.fw
        ki = fw.sbuf("sc_ki_" + tagn, shape, I32)
        kf = fw.sbuf("sc_kf_" + tagn, shape, F32)
        red = fw.sbuf("sc_red_" + tagn, shape, F32)
        s2 = fw.sbuf("sc_s2_" + tagn, shape, F32)
        c2 = fw.sbuf("sc_c2_" + tagn, shape, F32)
        fw.ts(ki.all(), phi, float(1.0 / (2 * np.pi)), ALU.mult)
        fw.copy(kf.all(), ki.all())
        fw.stt(red.all(), kf.all(), -TWO_PI_1, phi, ALU.mult, ALU.add)
        fw.stt(red.all(), kf.all(), -TWO_PI_2, red.all(), ALU.mult, ALU.add)
        fw.act(s2.all(), red.all(), AF.Sin, scale=0.5)
        fw.act(c2.all(), red.all(), AF.Sin, scale=0.5, bias=self.halfpi[0:shape[0], 0:1])
        fw.stt(sin_o, s2.all(), 2.0, c2.all(), ALU.mult, ALU.mult)
        fw.tt(kf.all(), s2.all(), s2.all(), ALU.mult)
        fw.ts(cos_o, kf.all(), -2.0, ALU.mult, 1.0, ALU.add)

    def phase_S5(self, li):
        fw, L, NL = self.fw, self.L, self.NL
        TC = 128
        TB = 256
        fw.push()
        pg = fw.sbuf("s5pg", [128, 3, 64], F32)
        fw.dma("sp", pg.all().x(lambda a: a.rearrange("p a g -> p (a g)")), self.s5_pg[:, li * 192:(li + 1) * 192])
        are, aim, ldt = pg[:, 0, :], pg[:, 1, :], pg[:, 2, :]
        mk = lambda n: fw.sbuf(n, [128, 64], F32)
        dt_, ang, mag, cs, sn, abre, abim, den, cre, cim, t1, t2 = [mk("s5p%d" % i) for i in range(12)]
        fw.act(dt_.all(), ldt, AF.Exp)
        fw.tt(ang.all(), aim, dt_.all(), ALU.mult)
        fw.tt(t1.all(), are, dt_.all(), ALU.mult)
        fw.act(mag.all(), t1.all(), AF.Exp)
        fw.push()
        self.sincos(cs.all(), sn.all(), ang.all(), [128, 64], "p")
        fw.pop()
        fw.tt(abre.all(), mag.all(), cs.all(), ALU.mult)
        fw.tt(abim.all(), mag.all(), sn.all(), ALU.mult)
        fw.tt(den.all(), are, are, ALU.mult)
        fw.tt(t1.all(), aim, aim, ALU.mult)
        fw.tt(den.all(), den.all(), t1.all(), ALU.add)
        fw.op("dve", "reciprocal", [den.all()], [den.all()], den.all(), den.all())
        fw.ts(t1.all(), abre.all(), -1.0, ALU.add)
        fw.tt(cre.all(), t1.all(), are, ALU.mult)
        fw.tt(t2.all(), abim.all(), aim, ALU.mult)
        fw.tt(cre.all(), cre.all(), t2.all(), ALU.add)
        fw.tt(cre.all(), cre.all(), den.all(), ALU.mult)
        fw.tt(cim.all(), abim.all(), are, ALU.mult)
        fw.tt(t2.all(), t1.all(), aim, ALU.mult)
        fw.tt(cim.all(), cim.all(), t2.all(), ALU.subtract)
        fw.tt(cim.all(), cim.all(), den.all(), ALU.mult)
        import os
        stage = float(os.environ.get("S5_STAGE", "9"))
        if stage < 1:
            fw.pop(); return
        L1 = fw.sbuf("s5L1", [128, 32, 128], BF16)
        L2 = fw.sbuf("s5L2", [128, 32, 128], BF16)
        LY1 = fw.sbuf("s5LY1", [128, 8, 8, 64], BF16)
        LY2 = fw.sbuf("s5LY2", [128, 8, 8, 64], BF16)
        CTb = fw.sbuf("s5CT", [128, 64, TC], F32)
        STb = fw.sbuf("s5ST", [128, 64, TC], F32)
        gmask = fw.sbuf("s5gm", [128, 8], F32)
        swp = fw.sbuf("s5sw", [128, 128], F32)
        sgn = fw.sbuf("s5sg", [128, 1], F32)
        vec = fw.sbuf("s5vec", [128, 16], F32)
        gluW = fw.sbuf("s5glu", [128, 8, 1024], BF16)
        fw.dma("sp", gmask.all(), self.cst["c_gmask"].all())
        fw.dma("sp", swp.all(), self.cst["c_swap"].all())
        fw.dma("sp", sgn.all(), self.cst["c_sgn"].all())
        fw.dma("sp", vec.all(), self.s5_vec[:, li * 16:(li + 1) * 16])
        fw.dma("pool", gluW.all().x(lambda a: a.rearrange("p k m -> p (k m)")), self.s5_glu[li * 128:(li + 1) * 128, :])
        if stage < 2:
            fw.pop(); return
        fw.push()
        braw = fw.sbuf("s5braw", [64, 2, 64, 16], F32)
        fw.dma("sp", braw.all().x(lambda a: a.rearrange("p a g c -> p (a g c)")),
               self.s5_b[:, li * 2048:(li + 1) * 2048])
        bbre = fw.sbuf("s5bbre", [64, 64, 16], F32)
        bbim = fw.sbuf("s5bbim", [64, 64, 16], F32)
        tmpb = fw.sbuf("s5tmpb", [64, 64, 16], F32)
        bc3 = lambda v: v.x(lambda a: a.unsqueeze(2).to_broadcast([64, 64, 16]))
        fw.tt(bbre.all(), braw[:, 0, :, :], bc3(cre[0:64, :]), ALU.mult)
        fw.tt(tmpb.all(), braw[:, 1, :, :], bc3(cim[0:64, :]), ALU.mult)
        fw.tt(bbre.all(), bbre.all(), tmpb.all(), ALU.subtract)
        fw.tt(bbim.all(), braw[:, 1, :, :], bc3(cre[0:64, :]), ALU.mult)
        fw.tt(tmpb.all(), braw[:, 0, :, :], bc3(cim[0:64, :]), ALU.mult)
        fw.tt(bbim.all(), bbim.all(), tmpb.all(), ALU.add)
        cat1 = fw.sbuf("s5cat1", [128, 128], F32)
        cat2 = fw.sbuf("s5cat2", [128, 128], F32)
        for ct in range(8 if float(os.environ.get('S5_SUB','9')) > 0.5 else 0):
            ps = fw.next("ps")
            fl = lambda v: v.x(lambda a: a.rearrange("p g c -> p (g c)"))
            fw.matmul(ps[:, 0:64], fl(bbre[:, ct * 8:(ct + 1) * 8, :]), self.ident[0:64, 0:64])
            fw.matmul(ps[:, 64:128], fl(bbim[:, ct * 8:(ct + 1) * 8, :]), self.ident[0:64, 0:64])
            fw.copy(cat1.all(), ps[:, 0:128], eng=os.environ.get("CAT_ENG","act"))
            fw.copy(cat2[:, 0:64], ps[:, 64:128])
            fw.ts(cat2[:, 64:128], ps[:, 0:64], -1.0, ALU.mult)
            for gp in range(8 if float(os.environ.get('S5_SUB','9')) > 1.5 else 0):
                half, slot = gp // 4, ct * 4 + gp % 4
                rows = slice(half * 64, half * 64 + 64)
                fw.ts(L1[rows, slot, :], cat1[rows, :], gmask[rows, gp:gp + 1], ALU.mult)
                fw.ts(L2[rows, slot, :], cat2[rows, :], gmask[rows, gp:gp + 1], ALU.mult)
        fw.pop()
        if stage < 3:
            fw.pop(); return
        fw.push()
        craw = fw.sbuf("s5craw", [128, 2, 8, 8, 16], F32)
        fw.dma("sp", craw.all().x(lambda a: a.rearrange("p a t g c -> p (a t g c)")),
               self.s5_c[:, li * 2048:(li + 1) * 2048])
        fw.memset(LY1.all(), 0.0)
        fw.memset(LY2.all(), 0.0, eng="pool")
        for gp in range(8):
            cs_ = slice((gp % 4) * 16, (gp % 4) * 16 + 16)
            fw.copy(LY1[0:64, :, gp, cs_], craw[0:64, 0, :, gp, :])
            fw.ts(LY1[64:128, :, gp, cs_], craw[64:128, 1, :, gp, :], -1.0, ALU.mult)
            fw.ts(LY2[0:64, :, gp, cs_], craw[0:64, 1, :, gp, :], -1.0, ALU.mult)
            fw.ts(LY2[64:128, :, gp, cs_], craw[64:128, 0, :, gp, :], -1.0, ALU.mult)
        fw.pop()
        if stage < 4:
            fw.pop(); return
        fw.push()
        iota = fw.sbuf("s5iota", [128, TC], F32)
        fw.dma("sp", iota.all(), self.cst["c_iota"].all())
        phi = fw.sbuf("s5phi", [128, 8, TC], F32)
        for gb in range(8):
            fw.tt(phi.all(),
                  ang[:, gb * 8:(gb + 1) * 8].x(lambda a: a.unsqueeze(2).to_broadcast([128, 8, TC])),
                  iota.all().x(lambda a: a.unsqueeze(1).to_broadcast([128, 8, TC])), ALU.mult)
            fw.push()
            self.sincos(CTb[:, gb * 8:(gb + 1) * 8, :], STb[:, gb * 8:(gb + 1) * 8, :], phi.all(), [128, 8, TC], "t")
            fw.pop()
        fw.pop()
        ssgn = fw.sbuf("s5ssgn", [128, 64], F32)
        cend = fw.sbuf("s5cend", [128, 64], F32)
        fw.ts(ssgn.all(), STb[:, :, TC - 1], sgn[:, 0:1], ALU.mult)
        fw.copy(cend.all(), CTb[:, :, TC - 1])
        if stage < 5:
            fw.pop(); return
        if self.debug and li == 0:
            self.dbg_y = self.scratch("dbg_y", [1024, L])
            self.dbg_tab = self.scratch("dbg_tab", [128, 4 * TC])
            self.dbg_par = self.scratch("dbg_par", [128, 4 * 64])
            self.dbg_L = self.scratch("dbg_L", [128, 4 * 128], BF16)
            fw.dma("sp", self.dbg_tab[:, 0:TC], CTb[:, 0, :])
            fw.dma("sp", self.dbg_tab[:, TC:2 * TC], STb[:, 0, :])
            fw.dma("sp", self.dbg_tab[:, 2 * TC:3 * TC], CTb[:, 5, :])
            fw.dma("sp", self.dbg_tab[:, 3 * TC:4 * TC], STb[:, 5, :])
            fw.dma("sp", self.dbg_par[:, 0:64], mag.all())
            fw.dma("sp", self.dbg_par[:, 64:128], ang.all())
            fw.dma("sp", self.dbg_par[:, 128:192], cre.all())
            fw.dma("sp", self.dbg_par[:, 192:256], cim.all())
            fw.dma("sp", self.dbg_L[:, 0:128], L1[:, 0, :])
            fw.dma("sp", self.dbg_L[:, 128:256], L2[:, 0, :])
            fw.dma("sp", self.dbg_L[:, 256:320], LY1[:, 0, 0, :])
            fw.dma("sp", self.dbg_L[:, 320:384], LY2[:, 0, 0, :])
            fw.dma("sp", self.dbg_L[:, 384:512], L1[:, 5, :])
        sprev = fw.sbuf("s5sprev", [128, 64], F32)
        send = fw.sbuf("s5send", [128, 64], F32)
        fw.memset(sprev.all(), 0.0)
        uT = fw.sbuf("s5uT", [128, 8, TB], F32)
        ub = fw.sbuf("s5ub", [128, 8, TB], BF16)
        ysb = fw.sbuf("s5y", [128, 8, TB], F32)
        ygb = fw.sbuf("s5yg", [128, 8, TB], BF16)
        fw.ring("s5t1", 2, [128, TC], F32)
        fw.ring("s5t2", 2, [128, TC], F32)
        fw.ring("s5bt", 2, [128, TC], F32)
        fw.ring("s5st", 3, [128, TC], F32)
        fw.ring("s5z1", 3, [128, TC], BF16)
        fw.ring("s5z2", 3, [128, TC], BF16)
        fw.ring("s5zT", 2, [128, TB], F32)
        fw.ring("s5sg", 2, [128, TB], F32)
        fw.ring("s5oc", 2, [128, TB], BF16)
        for tb in range(L // TB):
            t0 = tb * TB
            for ct in range(8):
                r0 = (CT_SU + ct) * 128
                fw.dma("sp", uT[:, ct, :], self.projT[r0:r0 + 128, t0:t0 + TB])
                fw.copy(ub[:, ct, :], uT[:, ct, :], eng="pool")
            for cc in range(TB // TC):
                csl = slice(cc * TC, (cc + 1) * TC)
                for ct in range(8):
                    psy = fw.next("psacc")
                    for gp in range(8):
                        g = ct * 8 + gp
                        half, slot = gp // 4, ct * 4 + gp % 4
                        rows = slice(half * 64, half * 64 + 64)
                        psx = fw.next("ps")
                        fw.matmul(psx[:, 0:TC], L1[rows, slot, :], ub[rows, ct, csl])
                        fw.matmul(psx[:, TC:2 * TC], L2[rows, slot, :], ub[rows, ct, csl])
                        a1, a2, bt, st = fw.next("s5t1"), fw.next("s5t2"), fw.next("s5bt"), fw.next("s5st")
                        fw.tt(a1.all(), psx[:, 0:TC], CTb[:, g, :], ALU.mult)
                        fw.tt(a2.all(), psx[:, TC:2 * TC], STb[:, g, :], ALU.mult)
                        fw.tt(bt.all(), a1.all(), a2.all(), ALU.add, eng="pool")
                        fw.scan(st.all(), mag[:, g:g + 1].bc([128, TC]), bt.all(), sprev[:, g:g + 1])
                        fw.copy(send[:, g:g + 1], st[:, TC - 1:TC], eng="act")
                        z1, z2 = fw.next("s5z1"), fw.next("s5z2")
                        fw.tt(z1.all(), st.all(), CTb[:, g, :], ALU.mult, eng="pool")
                        fw.tt(z2.all(), st.all(), STb[:, g, :], ALU.mult, eng="pool")
                        if self.debug and li == 0 and tb == 0 and cc == 0 and g == 0:
                            self.dbg_u = self.scratch("dbg_u", [128, 6 * TC])
                            dbt = fw.sbuf("dbgt", [128, 6 * TC], F32)
                            fw.copy(dbt[:, 0:TC], psx[:, 0:TC])
                            fw.copy(dbt[:, TC:2 * TC], psx[:, TC:2 * TC])
                            fw.copy(dbt[:, 2 * TC:3 * TC], bt.all())
                            fw.copy(dbt[:, 3 * TC:4 * TC], st.all())
                            fw.copy(dbt[:, 4 * TC:5 * TC], z1.all())
                            fw.copy(dbt[:, 5 * TC:6 * TC], ub[:, 0, 0:TC])
                            fw.dma("sp", self.dbg_u.all(), dbt.all())
                        orow = slice(half * 64, half * 64 + 64)
                        first, last = (gp % 4 == 0), (gp % 4 == 3)
                        fw.matmul(psy[orow, 0:TC], LY1[:, ct, gp, :], z1.all(), start=first, stop=False)
                        fw.matmul(psy[orow, 0:TC], LY2[:, ct, gp, :], z2.all(), start=False, stop=last)
                    fw.stt(ysb[:, ct, csl], uT[:, ct, csl], vec[:, ct:ct + 1], psy[:, 0:TC], ALU.mult, ALU.add)
                pss = fw.next("ps")
                fw.matmul(pss[:, 0:64], swp.all(), send.all())
                fw.tt(sprev.all(), send.all(), cend.all(), ALU.mult)
                fw.tt(send.all(), pss[:, 0:64], ssgn.all(), ALU.mult)
                fw.tt(sprev.all(), sprev.all(), send.all(), ALU.add)
            if self.debug and li == 0:
                for ct in range(8):
                    fw.dma("sp", self.dbg_y[ct * 128:(ct + 1) * 128, t0:t0 + TB], ysb[:, ct, :])
            for ct in range(8):
                fw.act(ysb[:, ct, :], ysb[:, ct, :], AF.Gelu_apprx_tanh)
                fw.copy(ygb[:, ct, :], ysb[:, ct, :], eng="pool")
            for mt in range(8):
                ps = fw.next("ps")
                for kt in range(8):
                    fw.matmul(ps[:, 0:TB], gluW[:, kt, mt * 128:(mt + 1) * 128], ygb[:, kt, :],
                              start=(kt == 0), stop=(kt == 7))
                sg = fw.next("s5sg")
                fw.act(sg.all(), ps[:, 0:TB], AF.Sigmoid, bias=vec[:, 8 + mt:9 + mt])
                fw.tt(sg.all(), sg.all(), ysb[:, mt, :], ALU.mult)
                zT = fw.next("s5zT")
                r0 = (CT_SZ + mt) * 128
                fw.dma("sp", zT.all(), self.projT[r0:r0 + 128, t0:t0 + TB])
                fw.act(zT.all(), zT.all(), AF.Silu)
                oc = fw.next("s5oc")
                fw.tt(oc.all(), sg.all(), zT.all(), ALU.mult)
                fw.dma("sp", self.oT[2048 + mt * 128:2048 + (mt + 1) * 128, t0:t0 + TB], oc.all())
        fw.pop()


    def neumann(self, A_T, A_, tagn):
        fw = self.fw
        X = fw.next("nm_x")
        fw.tt(X.all(), self.identb.all(), A_T.all(), ALU.subtract)
        P, Q = A_T, A_
        for l in range(1, 7):
            psq = fw.next("ps")
            fw.matmul(psq[:, 0:128], P.all(), Q.all())
            if l < 6:
                fw.matmul(psq[:, 128:256], Q.all(), P.all())
            Qn = fw.next("nm_q")
            fw.copy(Qn.all(), psq[:, 0:128], eng="act")
            if l < 6:
                Pn = fw.next("nm_p")
                fw.copy(Pn.all(), psq[:, 128:256], eng="act")
            psx = fw.next("ps")
            fw.matmul(psx[:, 0:128], Qn.all(), X.all())
            Xn = fw.next("nm_x")
            fw.tt(Xn.all(), X.all(), psx[:, 0:128], ALU.add)
            X = Xn
            Q = Qn
            if l < 6:
                P = Pn
        psr = fw.next("ps")
        fw.matmul(psr[:, 0:128], A_.all(), X.all())
        pstv = psr.all().x(lambda a: a.bitcast(BF16))
        xtv = pstv.x(lambda a: a[:, 512:640])
        fw.transpose(xtv, X.all(), self.identb.all())
        t_ = fw.next("nm_p")
        fw.tt(t_.all(), self.identb.all(), X.all(), ALU.subtract)
        Rb = fw.next("nm_q")
        fw.tt(Rb.all(), t_.all(), psr[:, 0:128], ALU.subtract)
        XT = fw.next("nm_xt")
        fw.copy(XT.all(), xtv)
        psd = fw.next("ps")
        fw.matmul(psd[:, 0:128], XT.all(), Rb.all())
        dX = fw.next("nm_dx")
        fw.copy(dX.all(), psd[:, 0:128], eng="act")
        return X, dX

    def phase_GDN(self, li):
        fw, L, NL = self.fw, self.L, self.NL
        C = 128
        NCH = L // C
        fw.push()
        sel = fw.sbuf("g_sel", [64, 16, 128], F32)
        fw.dma("sp", sel.all().x(lambda a: a.rearrange("p a b -> p (a b)")), self.cst["c_sel"].all())
        msk = {}
        for nm in ("su", "sl", "iu"):
            msk[nm] = fw.sbuf("g_m" + nm, [128, 128], F32)
            fw.dma("sp", msk[nm].all(), self.cst["c_mask_" + nm].all())
        par = fw.sbuf("g_par", [64, 2], F32)
        fw.dma("sp", par.all(), self.gdn_par[:, li * 2:(li + 1) * 2])
        vec = fw.sbuf("g_vec", [128, 5, 24], F32)
        fw.dma("sp", vec.all().x(lambda a: a.rearrange("p a b -> p (a b)")), self.gdn_vec[:, li * 120:(li + 1) * 120])
        onec = fw.sbuf("g_onec", [128, 1], F32)
        fw.memset(onec.all(), 1.0)
        epsc = fw.sbuf("g_epsc", [128, 1], F32)
        fw.memset(epsc.all(), 1e-6)
        nA = fw.sbuf("g_nA", [64, 1], F32)
        fw.act(nA.all(), par[:, 1:2], AF.Exp)
        fw.ts(nA.all(), nA.all(), -1.0, ALU.mult)
        SCX = fw.sbuf("g_scx", [64, NCH, 2, C], F32)
        NSC = fw.sbuf("g_nsc", [64, NCH, C], F32)
        TSS = fw.sbuf("g_tss", [128, NCH, 24], F32)
        QT = fw.sbuf("g_qT", [128, 8, L], BF16)
        KTt = fw.sbuf("g_kT", [128, 8, L], BF16)
        VT = fw.sbuf("g_vT", [128, 8, L], BF16)
        fw.push()
        ba = fw.sbuf("g_ba", [64, L], F32)
        gg = fw.sbuf("g_g", [64, L], F32)
        fw.dma("sp", ba.all(), self.projT[CT_GBA * 128:CT_GBA * 128 + 64, :])
        fw.memset(SCX.all(), 0.0)
        fw.memset(NSC.all(), 0.0)
        scx0 = SCX[:, :, 0, :]
        v3 = lambda v: v.x(lambda a: a.rearrange("p (n c) -> p n c", c=C))
        fw.act(gg[0:8, :], ba[0:8, :], AF.Sigmoid)
        fw.act(SCX[0:8, :, 0, :], v3(gg[0:8, :]), AF.Ln)
        fw.act(gg[32:40, :], ba[32:40, :], AF.Exp, bias=par[32:40, 0:1])
        fw.act(gg[32:40, :], gg[32:40, :], AF.Ln, bias=onec[32:40, 0:1])
        fw.ts(gg[32:40, :], gg[32:40, :], nA[32:40, 0:1], ALU.mult)
        for ch in range(NCH):
            fw.scan(SCX[32:40, ch, 0, :], onec[32:40, 0:1].bc([8, C]), gg[32:40, ch * C:(ch + 1) * C], 0.0)
            fw.act(SCX[32:40, ch, 1, :], SCX[32:40, ch, 0, :], AF.Identity, scale=-1.0, bias=SCX[32:40, ch, 0, C - 1:C])
            fw.ts(NSC[32:40, ch, :], SCX[32:40, ch, 0, :], -1.0, ALU.mult)
        for ch in range(NCH):
            ps = fw.next("ps")
            fw.matmul(ps[:, 0:64], SCX[:, ch, 0, :], self.ident[0:64, 0:64])
            fw.matmul(ps[:, 64:128], SCX[:, ch, 1, :], self.ident[0:64, 0:64])
            tsr = fw.sbuf("g_tsr%d" % (ch % 2), [128, 128], F32) if ch < 2 else fw.tensors["g_tsr%d" % (ch % 2)]
            fw.copy(tsr.all(), ps[:, 0:128])
            fw.act(TSS[:, ch, 0:8], tsr[:, 0:8], AF.Exp)
            fw.tt(tsr[:, 8:16], tsr[:, 0:8], tsr[:, 32:40], ALU.add)
            fw.act(TSS[:, ch, 8:16], tsr[:, 8:16], AF.Exp)
            fw.act(TSS[:, ch, 16:24], tsr[:, 96:104], AF.Exp)
        fw.pop()
        fw.push()
        fw.ring("g_raw", 2, [128, L + 3], F32)
        fw.ring("g_acc", 2, [128, L], F32)
        fw.ring("g_sq", 2, [128, 512], BF16)
        fw.ring("g_rn", 2, [128, 512], F32)
        for j in range(24):
            raw = fw.next("g_raw")
            fw.memset(raw[:, 0:3], 0.0, eng="pool")
            fw.dma("sp", raw[:, 3:L + 3], self.projT[(CT_GQKV + j) * 128:(CT_GQKV + j + 1) * 128, :])
            acc = fw.next("g_acc")
            fw.ts(acc.all(), raw[:, 0:L], vec[:, 0, j:j + 1], ALU.mult)
            for k in range(1, 4):
                fw.stt(acc.all(), raw[:, k:k + L], vec[:, k, j:j + 1], acc.all(), ALU.mult, ALU.add)
            typ, h = j // 8, j % 8
            if typ == 2:
                fw.act(VT[:, h, :], acc.all(), AF.Silu)
                continue
            fw.act(acc.all(), acc.all(), AF.Silu)
            dst = QT if typ == 0 else KTt
            for tb in range(self.NB):
                ts_ = slice(tb * 512, (tb + 1) * 512)
                sq = fw.next("g_sq")
                fw.tt(sq.all(), acc[:, ts_], acc[:, ts_], ALU.mult, eng="pool")
                ps = fw.next("ps")
                fw.matmul(ps.all(), self.onesb.all(), sq.all())
                rn = fw.next("g_rn")
                fw.act(rn.all(), ps.all(), AF.Ln, bias=epsc[:, 0:1])
                fw.act(rn.all(), rn.all(), AF.Exp, scale=-0.5)
                if typ == 0:
                    fw.stt(dst[:, h, ts_], acc[:, ts_], float(128 ** -0.5), rn.all(), ALU.mult, ALU.mult)
                else:
                    fw.tt(dst[:, h, ts_], acc[:, ts_], rn.all(), ALU.mult)
        fw.pop()
        S32 = fw.sbuf("g_S32", [128, 8, 128], F32)
        Sb = fw.sbuf("g_Sb", [128, 8, 128], BF16)
        fw.memset(S32.all(), 0.0)
        fw.memset(Sb.all(), 0.0)
        import os
        GD = F32 if os.environ.get("GDN_F32", "0") == "1" else BF16
        for nm, n_ in (("nm_x", 3), ("nm_q", 2), ("nm_p", 2), ("nm_xt", 2), ("nm_dx", 2), ("g_AT", 2), ("g_A", 2), ("g_at", 2),
                       ("g_ktk", 2), ("g_kd", 2), ("g_vb", 2), ("g_wT", 2), ("g_vn", 2), ("g_qd", 2)):
            fw.ring(nm, n_, [128, 128], GD)
        if GD == F32:
            Sb = S32
        fw.ring("g_D", 3, [128, 128], F32)
        fw.ring("g_gam", 2, [128, 128], F32)
        obuf = fw.sbuf("g_obuf", [128, 8, C], F32)
        osq = fw.sbuf("g_osq", [128, 8, C], BF16)
        zt = fw.sbuf("g_zt", [128, 8, C], F32)
        ogb = fw.sbuf("g_ogb", [128, 8, C], BF16)
        epsn = fw.sbuf("g_epsn", [128, 1], F32)
        fw.memset(epsn.all(), NORM_EPS)
        for ch in range(NCH):
            cs_ = slice(ch * C, (ch + 1) * C)
            for h in range(8):
                selBG, selG = sel[:, h, :], sel[:, 8 + h, :]
                sc, nsc = SCX[:, ch, 0, :], NSC[:, ch, :]
                kT, qT, vT = KTt[:, h, cs_], QT[:, h, cs_], VT[:, h, cs_]
                ps1 = fw.next("ps")
                fw.matmul(ps1[:, 0:128], selBG, sc, start=True, stop=False)
                fw.matmul(ps1[:, 0:128], nsc, selG, start=False, stop=False)
                fw.matmul(ps1[:, 0:128], self.ident.all(), msk["su"].all(), start=False, stop=True)
                fw.matmul(ps1[:, 128:256], sc, selBG, start=True, stop=False)
                fw.matmul(ps1[:, 128:256], selG, nsc, start=False, stop=False)
                fw.matmul(ps1[:, 128:256], self.ident.all(), msk["sl"].all(), start=False, stop=True)
                fw.matmul(ps1[:, 256:384], selG, sc, start=True, stop=False)
                fw.matmul(ps1[:, 256:384], nsc, selG, start=False, stop=False)
                fw.matmul(ps1[:, 256:384], self.ident.all(), msk["iu"].all(), start=False, stop=True)
                fw.matmul(ps1[:, 384:512], selG, sc, start=True, stop=True)
                DT_, D_, DI_ = fw.next("g_D"), fw.next("g_D"), fw.next("g_D")
                gam = fw.next("g_gam")
                fw.act(DT_.all(), ps1[:, 0:128], AF.Exp)
                fw.act(D_.all(), ps1[:, 128:256], AF.Exp)
                fw.act(DI_.all(), ps1[:, 256:384], AF.Exp)
                fw.act(gam.all(), ps1[:, 384:512], AF.Exp)
                ps2 = fw.next("ps")
                fw.matmul(ps2[:, 0:128], kT, kT)
                fw.matmul(ps2[:, 128:256], kT, qT)
                A_T, A_, at = fw.next("g_AT"), fw.next("g_A"), fw.next("g_at")
                fw.tt(A_T.all(), ps2[:, 0:128], DT_.all(), ALU.mult)
                fw.tt(A_.all(), ps2[:, 0:128], D_.all(), ALU.mult)
                fw.tt(at.all(), ps2[:, 128:256], DI_.all(), ALU.mult)
                X, dX = self.neumann(A_T, A_, "g")
                ps3 = fw.next("ps")
                pst = ps3.all().x(lambda a: a.bitcast(BF16))
                fw.transpose(pst.x(lambda a: a[:, 0:128]), kT, self.identb.all())
                fw.transpose(pst.x(lambda a: a[:, 128:256]), vT, self.identb.all())
                ktk, kd, vb = fw.next("g_ktk"), fw.next("g_kd"), fw.next("g_vb")
                kview = pst.x(lambda a: a[:, 0:128])
                vview = pst.x(lambda a: a[:, 128:256])
                fw.ts(ktk.all(), kview, TSS[:, ch, 8 + h:9 + h], ALU.mult)
                fw.ts(kd.all(), kview, TSS[:, ch, 16 + h:17 + h], ALU.mult)
                fw.ts(vb.all(), vview, TSS[:, ch, h:h + 1], ALU.mult)
                ps4 = fw.next("ps")
                fw.matmul(ps4[:, 0:128], ktk.all(), X.all(), start=True, stop=False)
                fw.matmul(ps4[:, 0:128], ktk.all(), dX.all(), start=False, stop=True)
                wT = fw.next("g_wT")
                fw.ts(wT.all(), ps4[:, 0:128], -1.0, ALU.mult)
                qd = fw.next("g_qd")
                fw.tt(qd.all(), qT, gam.all(), ALU.mult)
                ps5 = fw.next("ps")
                fw.matmul(ps5[:, 0:128], X.all(), vb.all(), start=True, stop=False)
                fw.matmul(ps5[:, 0:128], dX.all(), vb.all(), start=False, stop=False)
                fw.matmul(ps5[:, 0:128], wT.all(), Sb[:, h, :], start=False, stop=True)
                vn = fw.next("g_vn")
                fw.copy(vn.all(), ps5[:, 0:128], eng="act")
                ps6 = fw.next("ps")
                fw.matmul(ps6[:, 0:128], Sb[:, h, :], qd.all(), start=True, stop=False)
                fw.matmul(ps6[:, 0:128], vn.all(), at.all(), start=False, stop=True)
                fw.copy(obuf[:, h, :], ps6[:, 0:128], eng="act")
                fw.matmul(ps6[:, 128:256], kd.all(), vn.all())
                fw.stt(S32[:, h, :], S32[:, h, :], gam[:, C - 1:C], ps6[:, 128:256], ALU.mult, ALU.add)
                if GD != F32:
                    fw.copy(Sb[:, h, :], S32[:, h, :], eng="pool")
            fw.tt(osq.all(), obuf.all(), obuf.all(), ALU.mult, eng="pool")
            fw.dma("sp", zt.all(), self.projT[CT_GZ * 128:(CT_GZ + 8) * 128, cs_].x(
                lambda a: a.rearrange("(h p) t -> p h t", p=128)))
            fw.act(zt.all(), zt.all(), AF.Silu)
            for hh in range(2):
                ps = fw.next("ps")
                fw.matmul(ps.all(), self.onesb.all(),
                          osq[:, hh * 4:(hh + 1) * 4, :].x(lambda a: a.rearrange("p h c -> p (h c)")))
                rn = fw.sbuf("g_rn2", [128, 512], F32) if (ch == 0 and hh == 0) else fw.tensors["g_rn2"]
                fw.ts(rn.all(), ps.all(), 1.0 / 128, ALU.mult)
                fw.act(rn.all(), rn.all(), AF.Ln, bias=epsn[:, 0:1])
                fw.act(rn.all(), rn.all(), AF.Exp, scale=-0.5)
                ov = obuf[:, hh * 4:(hh + 1) * 4, :].x(lambda a: a.rearrange("p h c -> p (h c)"))
                zv = zt[:, hh * 4:(hh + 1) * 4, :].x(lambda a: a.rearrange("p h c -> p (h c)"))
                gv_ = ogb[:, hh * 4:(hh + 1) * 4, :].x(lambda a: a.rearrange("p h c -> p (h c)"))
                fw.stt(rn.all(), ov, vec[:, 4, 0:1], rn.all(), ALU.mult, ALU.mult)
                fw.tt(gv_, rn.all(), zv, ALU.mult)
            fw.dma("sp", self.oT[0:1024, cs_].x(lambda a: a.rearrange("(h p) t -> p h t", p=128)), ogb.all())
        fw.pop()


    def phase_RWKV(self, li):
        fw, L, NL = self.fw, self.L, self.NL
        C = 128
        NCH = L // C
        MID = 63
        fw.push()
        rv = fw.sbuf("r_rv", [128, 10, 8], F32)
        fw.dma("sp", rv.all().x(lambda a: a.rearrange("p a b -> p (a b)")), self.rw_vec[:, li * 80:(li + 1) * 80])
        mul = fw.sbuf("r_mul", [128, 2], F32)
        fw.dma("sp", mul.all(), self.rw_mul[:, li * 2:(li + 1) * 2])
        ups = fw.sbuf("r_ups", [96, 2048], BF16)
        fw.dma("pool", ups.all(), self.rw_up[li * 96:(li + 1) * 96, :])
        omka = fw.sbuf("r_omka", [128, 8], F32)
        fw.ts(omka.all(), rv[:, 6, :], -1.0, ALU.mult, 1.0, ALU.add)
        m01 = {}
        for nm in ("su", "sl", "iu"):
            m01[nm] = fw.sbuf("r_m" + nm, [128, 128], F32)
            fw.dma("sp", m01[nm].all(), self.cst["c_m01_" + nm].all())
        blkf = fw.sbuf("r_blkf", [128, 128], F32)
        blkb = fw.sbuf("r_blkb", [128, 128], BF16)
        fw.dma("sp", blkf.all(), self.cst["c_blk64"].all())
        fw.dma("pool", blkb.all(), self.cst["c_blk64"].all())
        onec = fw.sbuf("r_onec", [128, 1], F32)
        fw.memset(onec.all(), 1.0)
        eps6 = fw.sbuf("r_eps6", [128, 1], F32)
        fw.memset(eps6.all(), 1e-6)
        epsl = fw.sbuf("r_epsl", [128, 1], F32)
        fw.memset(epsl.all(), 64e-5)
        S32 = fw.sbuf("r_S32", [128, 8, 64], F32)
        fw.memset(S32.all(), 0.0)
        YB = fw.sbuf("r_YB", [128, 8, C], F32)
        BON = fw.sbuf("r_BON", [128, 8, C], F32)
        ZT = fw.sbuf("r_ZT", [128, 8, C], F32)
        OB = fw.sbuf("r_OB", [128, 8, C], BF16)
        for nm, n_ in (("nm_x", 3), ("nm_q", 2), ("nm_p", 2), ("nm_xt", 2), ("nm_dx", 2), ("r_AT", 2), ("r_A", 2), ("r_BT", 2), ("r_M1", 2),
                       ("r_M2", 2), ("r_rh", 2), ("r_kh", 2), ("r_ah", 2), ("r_kkh", 2), ("r_vb", 2),
                       ("r_khT", 2), ("r_ahT", 2), ("r_vT", 2)):
            fw.ring(nm, n_, [128, 128], BF16)
        for nm in ("r_raw",):
            fw.ring(nm, 4, [128, C + 1], F32)
        for nm in ("r_rs", "r_ks", "r_vs", "r_d", "r_lw", "r_a", "r_kkr", "r_kk", "r_t3", "r_km", "r_at", "r_cs",
                   "r_csp", "r_Wt", "r_Wi", "r_Wp", "r_tmp"):
            fw.ring(nm, 2, [128, C], F32)
        fw.ring("r_sqb", 2, [128, C], BF16)
        fw.ring("r_tw", 2, [128, C], BF16)
        fw.ring("r_al", 2, [128, C], BF16)
        fw.ring("r_col", 4, [128, 4], F32)
        fw.ring("r_Sh", 2, [128, 64], BF16)
        fw.ring("r_rhs", 2, [128, 64], BF16)
        fw.ring("r_nP", 2, [128, 64], BF16)
        fw.ring("r_st", 2, [128, 64], F32)
        fw.ring("r_e512", 3, [128, 512], F32)

        def shifted(ct_row0, ch, mu_ap, out, nrows=128):
            raw = fw.next("r_raw")
            t0 = ch * C
            if ch == 0:
                fw.memset(raw[0:nrows, 0:1], 0.0, eng="pool")
                fw.dma("sp", raw[0:nrows, 1:C + 1], self.projT[ct_row0:ct_row0 + nrows, 0:C])
            else:
                fw.dma("sp", raw[0:nrows, :], self.projT[ct_row0:ct_row0 + nrows, t0 - 1:t0 + C])
            d = fw.next("r_d")
            fw.tt(d[0:nrows, :], raw[0:nrows, 0:C], raw[0:nrows, 1:C + 1], ALU.subtract, eng="pool")
            fw.stt(out, d[0:nrows, :], mu_ap, raw[0:nrows, 1:C + 1], ALU.mult, ALU.add)

        for ch in range(NCH):
            cs_ = slice(ch * C, (ch + 1) * C)
            tw, alb, tmp = fw.next("r_tw"), fw.next("r_al"), fw.next("r_tmp")
            shifted(CT_RWL * 128, ch, mul[0:96, 0:1], tmp[0:96, :], nrows=96)
            fw.act(tw[0:96, :], tmp[0:96, :], AF.Tanh)
            tmp2 = fw.next("r_tmp")
            shifted(CT_RAL * 128, ch, mul[0:96, 1:2], tmp2[0:96, :], nrows=96)
            fw.copy(alb[0:96, :], tmp2[0:96, :], eng="pool")
            fw.dma("sp", ZT.all(), self.projT[CT_RZ * 128:(CT_RZ + 8) * 128, cs_].x(
                lambda a: a.rearrange("(h p) t -> p h t", p=128)))
            fw.act(ZT.all(), ZT.all(), AF.Silu)
            for p in range(8):
                rs, ks, vs = fw.next("r_rs"), fw.next("r_ks"), fw.next("r_vs")
                shifted((CT_RR + p) * 128, ch, rv[:, 0, p:p + 1], rs.all())
                shifted((CT_RK + p) * 128, ch, rv[:, 1, p:p + 1], ks.all())
                shifted((CT_RV + p) * 128, ch, rv[:, 2, p:p + 1], vs.all())
                psl = fw.next("ps")
                fw.matmul(psl[:, 0:C], ups[:, p * 128:(p + 1) * 128], tw[0:96, :])
                fw.matmul(psl[:, C:2 * C], ups[:, 1024 + p * 128:1024 + (p + 1) * 128], alb[0:96, :])
                lw, a_ = fw.next("r_lw"), fw.next("r_a")
                fw.act(lw.all(), psl[:, 0:C], AF.Sigmoid, bias=rv[:, 3, p:p + 1])
                fw.act(a_.all(), psl[:, C:2 * C], AF.Sigmoid, bias=rv[:, 4, p:p + 1])
                fw.ts(lw.all(), lw.all(), float(-np.exp(-0.5)), ALU.mult)
                kkr, sqb, kk = fw.next("r_kkr"), fw.next("r_sqb"), fw.next("r_kk")
                fw.ts(kkr.all(), ks.all(), rv[:, 5, p:p + 1], ALU.mult)
                fw.tt(sqb.all(), kkr.all(), kkr.all(), ALU.mult, eng="pool")
                t3, km, at = fw.next("r_t3"), fw.next("r_km"), fw.next("r_at")
                fw.ts(t3.all(), a_.all(), rv[:, 6, p:p + 1], ALU.mult, omka[:, p:p + 1], ALU.add)
                fw.tt(km.all(), ks.all(), t3.all(), ALU.mult)
                sqb2 = fw.next("r_sqb")
                fw.stt(sqb2.all(), rs.all(), rv[:, 7, p:p + 1], km.all(), ALU.mult, ALU.mult)
                pss = fw.next("ps")
                fw.matmul(pss[:, 0:C], blkb.all(), sqb.all())
                fw.matmul(pss[:, C:2 * C], blkb.all(), sqb2.all())
                rn = fw.next("r_tmp")
                fw.act(rn.all(), pss[:, 0:C], AF.Ln, bias=eps6[:, 0:1])
                fw.act(rn.all(), rn.all(), AF.Exp, scale=-0.5)
                fw.tt(kk.all(), kkr.all(), rn.all(), ALU.mult)
                fw.tt(BON[:, p, :], pss[:, C:2 * C], vs.all(), ALU.mult)
                fw.tt(at.all(), a_.all(), kk.all(), ALU.mult, eng="pool")
                csum, csp = fw.next("r_cs"), fw.next("r_csp")
                fw.scan(csum.all(), onec[:, 0:1].bc([128, C]), lw.all(), 0.0)
                fw.tt(csp.all(), csum.all(), lw.all(), ALU.subtract, eng="pool")
                col = fw.next("r_col")
                fw.ts(col[:, 0:1], csum[:, MID:MID + 1], -1.0, ALU.mult)
                Wt, Wi, Wp = fw.next("r_Wt"), fw.next("r_Wi"), fw.next("r_Wp")
                fw.act(Wt.all(), csum.all(), AF.Exp, bias=col[:, 0:1])
                fw.act(Wi.all(), csum.all(), AF.Exp, scale=-1.0, bias=csum[:, MID:MID + 1])
                fw.act(Wp.all(), csp.all(), AF.Exp, bias=col[:, 0:1])
                fw.op("dve", "reciprocal", [Wp[:, 0:1]], [col[:, 1:2]], col[:, 1:2], Wp[:, 0:1])
                rh, kh, ah, kkh, vb = fw.next("r_rh"), fw.next("r_kh"), fw.next("r_ah"), fw.next("r_kkh"), fw.next("r_vb")
                fw.tt(rh.all(), rs.all(), Wt.all(), ALU.mult)
                fw.tt(kh.all(), km.all(), Wi.all(), ALU.mult)
                fw.tt(ah.all(), at.all(), Wi.all(), ALU.mult, eng="pool")
                fw.tt(kkh.all(), kk.all(), Wp.all(), ALU.mult, eng="pool")
                fw.copy(vb.all(), vs.all(), eng="pool")
                pst_ = fw.next("ps")
                pst = pst_.all().x(lambda a: a.bitcast(BF16))
                fw.transpose(pst.x(lambda a: a[:, 0:128]), kh.all(), self.identb.all())
                fw.transpose(pst.x(lambda a: a[:, 128:256]), ah.all(), self.identb.all())
                fw.transpose(pst.x(lambda a: a[:, 256:384]), vb.all(), self.identb.all())
                khT, ahT, vT = fw.next("r_khT"), fw.next("r_ahT"), fw.next("r_vT")
                fw.copy(khT.all(), pst.x(lambda a: a[:, 0:128]), eng="act")
                fw.copy(ahT.all(), pst.x(lambda a: a[:, 128:256]), eng="act")
                fw.copy(vT.all(), pst.x(lambda a: a[:, 256:384]), eng="act")
                Sh = fw.next("r_Sh")
                fw.ts(Sh.all(), S32[:, p, :], col[:, 1:2], ALU.mult)
                psY = fw.next("psacc")
                psS = fw.next("psacc")
                for hh in range(2):
                    rows = slice(hh * 64, hh * 64 + 64)
                    pa = fw.next("ps")
                    fw.matmul(pa[:, 0:128], ah[rows, :], kkh[rows, :])
                    fw.matmul(pa[:, 128:256], kkh[rows, :], ah[rows, :])
                    fw.matmul(pa[:, 256:384], kh[rows, :], kkh[rows, :])
                    A_T, A_, BT = fw.next("r_AT"), fw.next("r_A"), fw.next("r_BT")
                    fw.tt(A_T.all(), pa[:, 0:128], m01["su"].all(), ALU.mult)
                    fw.tt(A_.all(), pa[:, 128:256], m01["sl"].all(), ALU.mult)
                    fw.tt(BT.all(), pa[:, 256:384], m01["su"].all(), ALU.mult)
                    pb = fw.next("ps")
                    fw.matmul(pb[:, 0:128], kh[rows, :], rh[rows, :])
                    fw.matmul(pb[:, 128:256], ah[rows, :], rh[rows, :])
                    M1, M2 = fw.next("r_M1"), fw.next("r_M2")
                    fw.tt(M1.all(), pb[:, 0:128], m01["iu"].all(), ALU.mult)
                    fw.tt(M2.all(), pb[:, 128:256], m01["iu"].all(), ALU.mult)
                    X, dX = self.neumann(A_T, A_, "r")
                    pr = fw.next("ps")
                    fw.matmul(pr[:, 0:64], kkh[rows, :], Sh[rows, :], start=True, stop=False)
                    fw.matmul(pr[:, 0:64], BT.all(), vT[:, rows], start=False, stop=True)
                    rhs = fw.next("r_rhs")
                    fw.copy(rhs.all(), pr[:, 0:64], eng="act")
                    fw.matmul(pr[:, 64:128], X.all(), rhs.all(), start=True, stop=False)
                    fw.matmul(pr[:, 64:128], dX.all(), rhs.all(), start=False, stop=True)
                    nP = fw.next("r_nP")
                    fw.ts(nP.all(), pr[:, 64:128], -1.0, ALU.mult)
                    fw.matmul(psY[rows, 0:C], Sh[rows, :], rh[rows, :], start=True, stop=False)
                    fw.matmul(psY[rows, 0:C], vT[:, rows], M1.all(), start=False, stop=False)
                    fw.matmul(psY[rows, 0:C], nP.all(), M2.all(), start=False, stop=True)
                    fw.matmul(psS[rows, 0:64], khT[:, rows], vT[:, rows], start=True, stop=False)
                    fw.matmul(psS[rows, 0:64], ahT[:, rows], nP.all(), start=False, stop=True)
                fw.copy(YB[:, p, :], psY[:, 0:C], eng="act")
                st = fw.next("r_st")
                fw.stt(st.all(), S32[:, p, :], col[:, 1:2], psS[:, 0:64], ALU.mult, ALU.add)
                fw.ts(S32[:, p, :], st.all(), Wt[:, C - 1:C], ALU.mult)
            for hf in range(2):
                fl = lambda v: v.x(lambda a: a.rearrange("p h c -> p (h c)"))
                yv = fl(YB[:, hf * 4:(hf + 1) * 4, :])
                pm = fw.next("ps")
                fw.matmul(pm.all(), blkf.all(), yv)
                yc = fw.next("r_e512")
                fw.stt(yc.all(), pm.all(), -1.0 / 64, yv, ALU.mult, ALU.add)
                sq = fw.next("r_e512")
                fw.tt(sq.all(), yc.all(), yc.all(), ALU.mult, eng="pool")
                pv = fw.next("ps")
                fw.matmul(pv.all(), blkf.all(), sq.all())
                rs_ = fw.next("r_e512")
                fw.ts(rs_.all(), pv.all(), 1.0 / 64, ALU.mult)
                fw.act(rs_.all(), rs_.all(), AF.Ln, bias=epsl[:, 0:1])
                fw.act(rs_.all(), rs_.all(), AF.Exp, scale=-0.5)
                fw.tt(yc.all(), yc.all(), rs_.all(), ALU.mult)
                for q in range(4):
                    p = hf * 4 + q
                    fw.ts(YB[:, p, :], yc[:, q * C:(q + 1) * C], rv[:, 8, p:p + 1], ALU.mult, rv[:, 9, p:p + 1], ALU.add)
                fw.tt(yv, yv, fl(BON[:, hf * 4:(hf + 1) * 4, :]), ALU.add)
                fw.tt(fl(OB[:, hf * 4:(hf + 1) * 4, :]), yv, fl(ZT[:, hf * 4:(hf + 1) * 4, :]), ALU.mult)
            fw.dma("sp", self.oT[1024:2048, cs_].x(lambda a: a.rearrange("(h p) t -> p h t", p=128)), OB.all())
        fw.pop()


    def phase_C(self, li, xsrc, xdst):
        fw, L, NL = self.fw, self.L, self.NL
        fw.push()
        gb = fw.sbuf("c_gb", [128, 48], F32)
        fw.dma("sp", gb.all(), self.gate_b[:, li * 48:(li + 1) * 48])
        oall = fw.sbuf("c_o", [128, 24, 512], BF16)
        mg = fw.sbuf("c_mg", [128, 16, 512], BF16)
        fw.ring("c_wb", 3, [128, 1024], BF16)
        fw.ring("c_wo", 2, [128, 2048], BF16)
        fw.ring("c_gl", 3, [128, 512], F32)
        fw.ring("c_acc", 2, [128, 512], F32)
        fw.ring("c_tmp", 2, [128, 512], F32)
        fw.ring("c_x", 3, [128, 512], F32)
        for tb in range(self.NB):
            ts_ = slice(tb * 512, (tb + 1) * 512)
            fw.dma("sp", oall.all(), self.oT[:, ts_].x(lambda a: a.rearrange("(j p) t -> p j t", p=128)))
            for dt in range(16):
                acc = fw.next("c_acc")
                for n in range(3):
                    wb = fw.next("c_wb")
                    r0 = ((li * 3 + n) * 16 + dt) * 128
                    fw.dma("pool", wb.all(), self.w_br[r0:r0 + 128, :])
                    gl = fw.next("c_gl")
                    g0 = (CT_GATE + n * 16 + dt) * 128
                    fw.dma("sp", gl.all(), self.projT[g0:g0 + 128, ts_])
                    fw.act(gl.all(), gl.all(), AF.Sigmoid, bias=gb[:, n * 16 + dt:n * 16 + dt + 1])
                    ps = fw.next("ps")
                    for kt in range(8):
                        fw.matmul(ps.all(), wb[:, kt * 128:(kt + 1) * 128], oall[:, n * 8 + kt, :],
                                  start=(kt == 0), stop=(kt == 7))
                    if n == 0:
                        fw.tt(acc.all(), ps.all(), gl.all(), ALU.mult)
                    elif n == 1:
                        tmp = fw.next("c_tmp")
                        fw.tt(tmp.all(), ps.all(), gl.all(), ALU.mult)
                        fw.tt(acc.all(), acc.all(), tmp.all(), ALU.add, eng="pool")
                    else:
                        tmp = fw.next("c_tmp")
                        fw.tt(tmp.all(), ps.all(), gl.all(), ALU.mult)
                        fw.tt(mg[:, dt, :], acc.all(), tmp.all(), ALU.add, eng="pool")
            for dt in range(16):
                wo = fw.next("c_wo")
                r0 = (li * 16 + dt) * 128
                fw.dma("pool", wo.all(), self.w_out[r0:r0 + 128, :])
                xt = fw.next("c_x")
                fw.dma("sp", xt.all(), xsrc[dt * 128:(dt + 1) * 128, ts_])
                ps = fw.next("ps")
                for kt in range(16):
                    fw.matmul(ps.all(), wo[:, kt * 128:(kt + 1) * 128], mg[:, kt, :], start=(kt == 0), stop=(kt == 15))
                fw.tt(xt.all(), xt.all(), ps.all(), ALU.add)
                fw.dma("sp", xdst[dt * 128:(dt + 1) * 128, ts_], xt.all())
        fw.pop()

    def phase_final(self, xsrc):
        fw, L = self.fw, self.L
        fw.push()
        fnw = fw.sbuf("f_w", [128, 16], F32)
        fw.dma("sp", fnw.all(), self.fnorm_w.all())
        rstd = fw.sbuf("f_rstd", [128, 512], F32)
        fw.ring("f_xt", 3, [128, 512], F32)
        fw.ring("f_sq", 2, [128, 512], F32)
        for tb in range(self.NB):
            ts_ = slice(tb * 512, (tb + 1) * 512)
            ps = fw.next("ps")
            for kt in range(KT):
                xt = fw.next("f_xt")
                fw.dma("sp", xt.all(), xsrc[kt * 128:(kt + 1) * 128, ts_])
                sq = fw.next("f_sq")
                fw.act(sq.all(), xt.all(), AF.Square)
                fw.matmul(ps.all(), self.ones.all(), sq.all(), start=(kt == 0), stop=(kt == KT - 1))
            tmp = fw.next("f_sq")
            fw.ts(tmp.all(), ps.all(), 1.0 / D, ALU.mult, NORM_EPS, ALU.add)
            fw.act(tmp.all(), tmp.all(), AF.Ln)
            fw.act(rstd.all(), tmp.all(), AF.Exp, scale=-0.5)
            for kt in range(KT):
                xt = fw.next("f_xt")
                fw.dma("sp", xt.all(), xsrc[kt * 128:(kt + 1) * 128, ts_])
                fw.stt(xt.all(), xt.all(), fnw[:, kt:kt + 1], rstd.all(), ALU.mult, ALU.mult)
                fw.dma("sp", self.out[kt * 128:(kt + 1) * 128, ts_], xt.all())
        fw.pop()


def host_inputs(inputs, b, L, NL):
    f = np.float32
    m = {}
    m["xT"] = np.ascontiguousarray(inputs["x"][b, :L].T)
    m["w_in"] = np.concatenate([relayout_w_in(inputs["w_in"][i]) for i in range(NL)], axis=0)
    m["norm_w"] = np.ascontiguousarray(inputs["norm_w"][:NL].reshape(NL * KT, 128).T)
    m.update(make_consts())
    pg = np.zeros((128, NL, 3, 64), f)
    for i in range(NL):
        for h in range(2):
            pg[h * 64:(h + 1) * 64, i, 0] = inputs["s5_a_re"][i].T
            pg[h * 64:(h + 1) * 64, i, 1] = inputs["s5_a_im"][i].T
            pg[h * 64:(h + 1) * 64, i, 2] = inputs["s5_log_dt"][i][None, :]
    m["s5_pg"] = pg.reshape(128, -1)
    sb = np.zeros((64, NL, 2, 64, 16), f)
    sc = np.zeros((128, NL, 2, 64, 16), f)
    for i in range(NL):
        sb[:, i, 0] = inputs["s5_b_re"][i].transpose(1, 0, 2)
        sb[:, i, 1] = inputs["s5_b_im"][i].transpose(1, 0, 2)
        for h in range(2):
            sc[h * 64:(h + 1) * 64, i, 0] = inputs["s5_c_re"][i].transpose(2, 0, 1)
            sc[h * 64:(h + 1) * 64, i, 1] = inputs["s5_c_im"][i].transpose(2, 0, 1)
    m["s5_b"] = sb.reshape(64, -1)
    m["s5_c"] = sc.reshape(128, -1)
    vec = np.zeros((128, NL, 2, 8), f)
    for i in range(NL):
        vec[:, i, 0] = inputs["s5_d"][i].reshape(8, 128).T
        vec[:, i, 1] = inputs["s5_glu_b"][i].reshape(8, 128).T
    m["s5_vec"] = vec.reshape(128, -1)
    gp_ = np.zeros((64, NL, 2), f)
    gv = np.zeros((128, NL, 5, 24), f)
    for i in range(NL):
        gp_[32:40, i, 0] = inputs["gdn_dt_bias"][i]
        gp_[32:40, i, 1] = inputs["gdn_a_log"][i]
        gv[:, i, 0:4, :] = inputs["gdn_conv_w"][i].reshape(4, 24, 128).transpose(2, 0, 1)
        gv[:, i, 4, 0] = inputs["gdn_norm_w"][i]
    m["gdn_par"] = gp_.reshape(64, -1)
    m["gdn_vec"] = gv.reshape(128, -1)
    rv = np.zeros((128, NL, 10, 8), f)
    mul = np.zeros((128, NL, 2), f)
    for i in range(NL):
        mu = inputs["rwkv_mu"][i]
        t8 = lambda a: a.reshape(8, 128).T
        rv[:, i, 0] = t8(mu[0:1024]); rv[:, i, 1] = t8(mu[1024:2048]); rv[:, i, 2] = t8(mu[2048:3072])
        mul[0:96, i, 0] = mu[3072:3168]; mul[0:96, i, 1] = mu[3168:3264]
        rv[:, i, 3] = t8(inputs["rwkv_w0"][i]); rv[:, i, 4] = t8(inputs["rwkv_a0"][i])
        rv[:, i, 5] = t8(inputs["rwkv_k_k"][i]); rv[:, i, 6] = t8(inputs["rwkv_k_a"][i])
        rv[:, i, 7] = t8(inputs["rwkv_r_k"][i].reshape(1024))
        rv[:, i, 8] = t8(inputs["rwkv_lnx_w"][i]); rv[:, i, 9] = t8(inputs["rwkv_lnx_b"][i])
    m["rw_vec"] = rv.reshape(128, -1)
    m["rw_mul"] = mul.reshape(128, -1)
    m["rw_up"] = np.ascontiguousarray(np.concatenate(
        [np.concatenate([inputs["rwkv_w_up"][i], inputs["rwkv_a_up"][i]], axis=1) for i in range(NL)], axis=0))
    m["gate_b"] = np.ascontiguousarray(inputs["gate_b"][:NL].reshape(NL, 3, 16, 128).transpose(3, 0, 1, 2).reshape(128, -1))
    m["w_br"] = np.ascontiguousarray(
        inputs["w_branch"][:NL].reshape(NL, 3, 8, 128, 16, 128).transpose(0, 1, 4, 3, 2, 5).reshape(NL * 3 * 16 * 128, 1024))
    m["w_out"] = np.ascontiguousarray(
        inputs["w_out"][:NL].reshape(NL, 16, 128, 16, 128).transpose(0, 3, 2, 1, 4).reshape(NL * 16 * 128, 2048))
    m["fnorm_w"] = np.ascontiguousarray(inputs["final_norm_w"].reshape(16, 128).T)
    m["s5_glu"] = np.concatenate(
        [inputs["s5_glu_w"][i].reshape(8, 128, 1024).transpose(1, 0, 2).reshape(128, 8 * 1024) for i in range(NL)], axis=0)
    return m


def build(L, NL, debug=False, phases=("A", "S5", "GDN", "RWKV", "C", "F")):
    p = Prog(L, NL, debug)
    p.setup()
    xsrc = p.xT_in
    for li in range(NL):
        xdst = p.xT[li % 2]
        if "A" in phases:
            p.phase_A(li, xsrc)
        if "S5" in phases:
            p.phase_S5(li)
        if "GDN" in phases:
            p.phase_GDN(li)
        if "RWKV" in phases:
            p.phase_RWKV(li)
        if "C" in phases:
            p.phase_C(li, xsrc, xdst)
            xsrc = xdst
    if "F" in phases:
        p.phase_final(xsrc)
    stats = p.fw.emit()
    print("ops per engine:", stats)
    return p


_CACHE = {}


def kernel(**inputs):
    L, NL, B = 2048, 4, 4
    inputs = {k: np.asarray(v) for k, v in inputs.items()}
    if "prog" not in _CACHE:
        _CACHE["prog"] = build(L, NL)
    p = _CACHE["prog"]
    m0 = host_inputs(inputs, 0, L, NL)
    in_maps = [m0]
    for b in range(1, B):
        mb = dict(m0)
        mb["xT"] = np.ascontiguousarray(inputs["x"][b, :L].T)
        in_maps.append(mb)
    res = run_bass_kernel_spmd(p.nc, in_maps, core_ids=list(range(B)))
    out = np.stack([np.ascontiguousarray(res.results[b]["out"].T) for b in range(B)], axis=0)
    return out.astype(np.float32)
```

```python
import numpy as np
import concourse.bass as bass
import concourse.mybir as mybir
from concourse.bass_utils import run_bass_kernel_spmd
from contextlib import ExitStack

F32 = mybir.dt.float32
BF16 = mybir.dt.bfloat16
I32 = mybir.dt.int32
AF = mybir.ActivationFunctionType
ALU = mybir.AluOpType
AX = mybir.AxisListType

ENGS = ("pe", "act", "dve", "pool", "sp")

D = 2048
KT = 16
NCT = 131
NORM_EPS = 1e-6


class T:
    def __init__(self, name, shape, dtype, space, handle):
        self.name, self.shape, self.dtype, self.space = name, tuple(shape), dtype, space
        self.h = handle
        self.hist = []

    def __getitem__(self, key):
        if not isinstance(key, tuple):
            key = (key,)
        key = key + (slice(None),) * (len(self.shape) - len(key))
        rng = []
        for k, n in zip(key, self.shape):
            if isinstance(k, slice):
                a = 0 if k.start is None else k.start
                b = n if k.stop is None else k.stop
                assert k.step in (None, 1)
            else:
                a, b = k, k + 1
            assert 0 <= a < b <= n, (self.name, key, self.shape)
            rng.append((a, b))
        return V(self, tuple(rng), key)

    def all(self):
        return self[tuple(slice(None) for _ in self.shape)]


class V:
    def __init__(self, t, rng, key, xf=None):
        self.t, self.rng, self.key, self.xf = t, rng, key, xf

    @property
    def ap(self):
        a = self.t.h[self.key]
        if self.xf is not None:
            a = self.xf(a)
        return a

    def x(self, fn):
        old = self.xf
        if old is None:
            return V(self.t, self.rng, self.key, fn)
        return V(self.t, self.rng, self.key, lambda a: fn(old(a)))

    def bc(self, shape):
        return self.x(lambda a: a.to_broadcast(list(shape)))

    @property
    def shape(self):
        return tuple(b - a for a, b in self.rng)


def _overlap(r1, r2):
    for (a, b), (c, d) in zip(r1, r2):
        if b <= c or d <= a:
            return False
    return True


def _contains(r1, r2):
    for (a, b), (c, d) in zip(r1, r2):
        if c < a or d > b:
            return False
    return True


class Op:
    __slots__ = ("eng", "fn", "is_dma", "deps", "needed", "count", "dslot", "dval", "idx", "tag")


class FW:
    NDMA_SEMS = 32

    def __init__(self, nc):
        self.nc = nc
        self.ops = []
        self.stack = ExitStack()
        self.tensors = {}
        self.rings = {}
        self.cursor = 16512
        self.cur_stack = []
        self.uid = 0
        self.dma_i = 0
        self.dma_last = [None] * self.NDMA_SEMS
        self.junk = {e: self.sbuf("junk_" + e, [128, 16], F32) for e in ("act", "dve", "pool")}

    SBUF_LIMIT = 229300

    def sbuf(self, name, shape, dtype=F32):
        esz = {F32: 4, BF16: 2, I32: 4}[dtype]
        nbytes = int(np.prod(shape[1:])) * esz
        nbytes = (nbytes + 63) // 64 * 64
        off = self.cursor
        assert off + nbytes <= self.SBUF_LIMIT, ("SBUF overflow", name, off, nbytes)
        self.cursor = off + nbytes
        self.uid += 1
        h = self.nc.alloc_sbuf_tensor_at("%s_u%d" % (name, self.uid), list(shape), dtype, offset=off)
        t = T(name, shape, dtype, "sbuf", h)
        self.tensors[name] = t
        return t

    def push(self):
        self.cur_stack.append(self.cursor)

    def pop(self):
        self.barrier()
        self.cursor = self.cur_stack.pop()

    def barrier(self):
        ms = []
        for e in ("act", "dve", "pool"):
            j = self.junk[e]
            if e == "act":
                ms.append(self.op(e, "activation", [], [j[:, 0:8]], j[:, 0:8], j[:, 8:16], AF.Copy).idx)
            else:
                ms.append(self.op(e, "memset", [], [j.all()], j.all(), 0.0).idx)
        dl = [o.idx for o in self.dma_last if o is not None]
        for e in ENGS:
            op = Op()
            op.eng, op.fn, op.is_dma, op.tag = e, None, False, "bar"
            op.idx = len(self.ops)
            op.needed = False
            op.count = None
            op.deps = list(ms) + list(dl)
            self.ops.append(op)
        for t in self.tensors.values():
            t.hist = []

    def psum(self, name, shape, dtype=F32):
        h = self.stack.enter_context(self.nc.psum_tensor(name, list(shape), dtype))
        t = T(name, shape, dtype, "psum", h)
        self.tensors[name] = t
        return t

    def dram(self, name, shape, dtype=F32, kind="Internal"):
        h = self.nc.dram_tensor(name, list(shape), dtype, kind=kind).ap()
        t = T(name, shape, dtype, "dram", h)
        self.tensors[name] = t
        return t

    def ring(self, name, n, shape, dtype=F32, space="sbuf"):
        mk = self.sbuf if space == "sbuf" else self.psum
        self.rings[name] = [[mk("%s_%d" % (name, i), shape, dtype) for i in range(n)], 0]

    def next(self, name):
        r = self.rings[name]
        t = r[0][r[1] % len(r[0])]
        r[1] += 1
        return t

    def _record(self, eng, fn, reads, writes, is_dma=False, tag=""):
        op = Op()
        op.eng, op.fn, op.is_dma, op.tag = eng, fn, is_dma, tag
        op.idx = len(self.ops)
        op.needed = False
        op.count = None
        deps = set()

        def reg(v):
            return tuple((0, n) for n in v.t.shape) if v.t.space == "psum" else v.rng
        for v in reads:
            ps_ = v.t.space == "psum"
            for (r, oi, w, e) in v.t.hist:
                if (w or (ps_ and e != eng)) and _overlap(r, reg(v)):
                    deps.add(oi)
        for v in writes:
            for (r, oi, w, e) in v.t.hist:
                if _overlap(r, reg(v)):
                    deps.add(oi)
        for v in writes:
            t = v.t
            t.hist = [h for h in t.hist if not _contains(reg(v), h[0])]
            t.hist.append((reg(v), op.idx, True, eng))
        for v in reads:
            t = v.t
            if not is_dma:
                t.hist = [h for h in t.hist
                          if not ((not h[2]) and h[3] == eng and h[0] == reg(v) and h[1] != op.idx
                               and not self.ops[h[1]].is_dma)]
            t.hist.append((reg(v), op.idx, False, eng))
        deps.discard(op.idx)
        op.deps = [d for d in deps if not (eng == "pe" and self.ops[d].eng == "pe" and not self.ops[d].is_dma
                                           and not is_dma)]
        if is_dma:
            op.dslot = self.dma_i % self.NDMA_SEMS
            op.dval = 16 * (self.dma_i // self.NDMA_SEMS + 1)
            prev = self.dma_last[op.dslot]
            if prev is not None:
                op.deps.append(prev.idx)
            self.dma_last[op.dslot] = op
            self.dma_i += 1
        self.ops.append(op)
        return op

    def op(self, eng, method, reads, writes, *args, **kw):
        def conv(a):
            return a.ap if isinstance(a, V) else a

        def fn(e):
            return getattr(e, method)(*[conv(a) for a in args], **{k: conv(v) for k, v in kw.items()})
        return self._record(eng, fn, reads, writes, tag=method)

    def dma(self, eng, out, in_, **kw):
        def fn(e):
            return e.dma_start(out=out.ap, in_=in_.ap, **kw)
        return self._record(eng, fn, [in_], [out], is_dma=True, tag="dma")

    def matmul(self, out, lhsT, rhs, start=True, stop=True, **kw):
        rd = [lhsT, rhs] + ([] if start else [out])
        return self.op("pe", "matmul", rd, [out], out, lhsT, rhs, start=start, stop=stop, **kw)

    def transpose(self, out, in_, ident):
        return self.op("pe", "transpose", [in_, ident], [out], out, in_, ident)

    def act(self, out, in_, func, bias=None, scale=None, extra_reads=()):
        kw = {}
        rd = [in_] + list(extra_reads)
        if bias is not None:
            kw["bias"] = bias
            if isinstance(bias, V):
                rd.append(bias)
        if scale is not None:
            kw["scale"] = scale
            if isinstance(scale, V):
                rd.append(scale)
        return self.op("act", "activation", rd, [out], out, in_, func, **kw)

    def tt(self, out, in0, in1, op, eng="dve"):
        return self.op(eng, "tensor_tensor", [in0, in1], [out], out, in0, in1, op)

    def ts(self, out, in0, s1, op0, s2=None, op1=None, eng="dve"):
        rd = [in0] + [s for s in (s1, s2) if isinstance(s, V)]
        if op1 is None:
            return self.op(eng, "tensor_scalar", rd, [out], out, in0, s1, None, op0)
        return self.op(eng, "tensor_scalar", rd, [out], out, in0, s1, s2, op0, op1)

    def stt(self, out, in0, scalar, in1, op0, op1, eng="dve"):
        rd = [in0, in1] + ([scalar] if isinstance(scalar, V) else [])
        return self.op(eng, "scalar_tensor_tensor", rd, [out], out, in0, scalar, in1, op0, op1)

    def copy(self, out, in_, eng="dve"):
        if eng == "act":
            return self.op("act", "activation", [in_], [out], out, in_, AF.Copy)
        return self.op(eng, "tensor_copy", [in_], [out], out, in_)

    def memset(self, out, val, eng="dve"):
        return self.op(eng, "memset", [], [out], out, val)

    def scan(self, out, d0, d1, init, op0=ALU.mult, op1=ALU.add):
        rd = [d0, d1] + ([init] if isinstance(init, V) else [])
        return self.op("dve", "tensor_tensor_scan", rd, [out], out, d0, d1, init, op0, op1)

    def emit(self):
        nc = self.nc
        ops = self.ops
        for o in ops:
            for d in o.deps:
                ops[d].needed = True
        cnt = {e: 0 for e in ENGS}
        for o in ops:
            if o.is_dma:
                pass
            elif o.needed:
                cnt[o.eng] += 1
                o.count = cnt[o.eng]
        st = self.stack
        sem_e = {e: st.enter_context(nc.semaphore("sem_" + e)) for e in ENGS if e != "sp"}
        sem_d = [st.enter_context(nc.semaphore("semd%d" % i)) for i in range(self.NDMA_SEMS)]
        per_eng = {e: [o for o in ops if o.eng == e] for e in ENGS}
        final = {}
        for o in ops:
            if o.is_dma:
                final[o.dslot] = o.dval

        def run_engine(ename, e):
            waited = {}

            def wait(key, sem, val):
                if waited.get(key, 0) >= val:
                    return
                e.wait_ge(sem, val)
                waited[key] = val

            for o in per_eng[ename]:
                for d in sorted(o.deps):
                    p = ops[d]
                    if p.is_dma:
                        wait(("d", p.dslot), sem_d[p.dslot], p.dval)
                    else:
                        wait(("e", p.eng), sem_e[p.eng], p.count)
                if o.fn is None:
                    continue
                ins = o.fn(e)
                if o.is_dma:
                    ins.then_inc(sem_d[o.dslot], 16)
                elif o.needed:
                    ins.then_inc(sem_e[o.eng], 1)
            if ename == "sp":
                for slot, val in sorted(final.items()):
                    wait(("d", slot), sem_d[slot], val)

        with nc.Block() as block:
            @block.tensor
            def _(e):
                run_engine("pe", e)

            @block.scalar
            def _(e):
                run_engine("act", e)

            @block.vector
            def _(e):
                run_engine("dve", e)

            @block.gpsimd
            def _(e):
                run_engine("pool", e)

            @block.sync
            def _(e):
                run_engine("sp", e)
        st.close()
        return {e: len(per_eng[e]) for e in ENGS}


def col_tiles():
    tiles = []
    c = 0
    for _ in range(24 + 8):
        tiles.append((c, 128)); c += 128
    tiles.append((c, 16)); c += 16
    for _ in range(24):
        tiles.append((c, 128)); c += 128
    tiles.append((c, 96)); c += 96
    tiles.append((c, 96)); c += 96
    for _ in range(8 + 8 + 8 + 48):
        tiles.append((c, 128)); c += 128
    assert c == 16592 and len(tiles) == NCT
    return tiles


CT_GQKV, CT_GZ, CT_GBA, CT_RR, CT_RK, CT_RV, CT_RWL, CT_RAL, CT_RZ, CT_SU, CT_SZ, CT_GATE = \
    0, 24, 32, 33, 41, 49, 57, 58, 59, 67, 75, 83


def relayout_w_in(w):
    out = np.zeros((NCT, 128, KT, 128), np.float32)
    for j, (c0, n) in enumerate(col_tiles()):
        blk = w[:, c0:c0 + n].reshape(KT, 128, n)
        if j == CT_GBA:
            out[j, :, :, 0:8] = blk[:, :, 0:8].transpose(1, 0, 2)
            out[j, :, :, 32:40] = blk[:, :, 8:16].transpose(1, 0, 2)
            continue
        out[j, :, :, :n] = blk.transpose(1, 0, 2)
    return out.reshape(NCT * 128, KT * 128)


TWO_PI_1 = 6.28125
TWO_PI_2 = float(2 * np.pi - 6.28125)


def make_consts():
    c = {}
    c["c_ones"] = np.ones((128, 128), np.float32)
    c["c_ident"] = np.eye(128, dtype=np.float32)
    c["c_iota"] = np.tile(np.arange(1, 129, dtype=np.float32)[None, :], (128, 1))
    gm = np.zeros((128, 8), np.float32)
    for r in range(128):
        gm[r, r // 16] = 1.0
    c["c_gmask"] = gm
    sw = np.zeros((128, 128), np.float32)
    for m in range(128):
        sw[(m + 64) % 128, m] = 1.0
    c["c_swap"] = sw
    sg = np.ones((128, 1), np.float32)
    sg[:64] = -1.0
    c["c_sgn"] = sg
    NEG = -30000.0
    r_ = np.arange(128)[:, None]
    c_ = np.arange(128)[None, :]
    c["c_mask_su"] = np.where(r_ < c_, 0.0, NEG).astype(np.float32)
    c["c_mask_sl"] = np.where(r_ > c_, 0.0, NEG).astype(np.float32)
    c["c_mask_iu"] = np.where(r_ <= c_, 0.0, NEG).astype(np.float32)
    c["c_m01_su"] = (r_ < c_).astype(np.float32)
    c["c_m01_sl"] = (r_ > c_).astype(np.float32)
    c["c_m01_iu"] = (r_ <= c_).astype(np.float32)
    sel = np.zeros((64, 16, 128), np.float32)
    for h in range(8):
        sel[h, h, :] = 1.0
        sel[32 + h, h, :] = 1.0
        sel[32 + h, 8 + h, :] = 1.0
    c["c_sel"] = sel.reshape(64, 16 * 128)
    bo = np.zeros((128, 128), np.float32)
    bo[:64, :64] = 1.0
    bo[64:, 64:] = 1.0
    c["c_blk64"] = bo
    return c


class Prog:
    def __init__(self, L, NL, debug=False):
        self.L, self.NL, self.debug = L, NL, debug
        self.NB = L // 512
        nc = bass.Bass("TRN2", target_bir_lowering=False)
        self.nc = nc
        self.fw = FW(nc)
        self.evac_i = 0

    def scratch(self, name, shape, dtype=F32):
        return self.fw.dram(name, shape, dtype, kind=("ExternalOutput" if self.debug else "Internal"))

    def ext(self, n, s, dt=F32):
        return self.fw.dram(n, s, dt, kind="ExternalInput")

    def setup(self):
        fw, L, NL = self.fw, self.L, self.NL
        ext = self.ext
        self.xT_in = ext("xT", [D, L])
        self.w_in = ext("w_in", [NL * NCT * 128, KT * 128])
        self.norm_w = ext("norm_w", [128, NL * KT])
        self.cst = {k: ext(k, list(v.shape)) for k, v in make_consts().items()}
        self.s5_pg = ext("s5_pg", [128, NL * 3 * 64])
        self.s5_b = ext("s5_b", [64, NL * 2 * 64 * 16])
        self.s5_c = ext("s5_c", [128, NL * 2 * 64 * 16])
        self.s5_vec = ext("s5_vec", [128, NL * 2 * 8])
        self.s5_glu = ext("s5_glu", [NL * 128, 8 * 1024])
        self.gdn_par = ext("gdn_par", [64, NL * 2])
        self.gdn_vec = ext("gdn_vec", [128, NL * 5 * 24])
        self.rw_vec = ext("rw_vec", [128, NL * 80])
        self.rw_mul = ext("rw_mul", [128, NL * 2])
        self.rw_up = ext("rw_up", [NL * 96, 2048])
        self.gate_b = ext("gate_b", [128, NL * 48])
        self.w_br = ext("w_br", [NL * 48 * 128, 1024])
        self.w_out = ext("w_out", [NL * 16 * 128, 2048])
        self.fnorm_w = ext("fnorm_w", [128, 16])
        self.out = fw.dram("out", [D, L], F32, kind="ExternalOutput")
        self.xT = [self.scratch("xs%d" % i, [D, L]) for i in range(2)]
        self.projT = self.scratch("projT", [NCT * 128, L])
        self.oT = self.scratch("oT", [3 * 1024, L], BF16)
        self.ones = fw.sbuf("ones", [128, 128], F32)
        self.ident = fw.sbuf("ident", [128, 128], F32)
        self.normw = fw.sbuf("normw", [128, NL * KT], F32)
        self.identb = fw.sbuf("identb", [128, 128], BF16)
        self.onesb = fw.sbuf("onesb", [128, 128], BF16)
        fw.dma("pool", self.identb.all(), self.cst["c_ident"].all())
        fw.dma("pool", self.onesb.all(), self.cst["c_ones"].all())
        self.halfpi = fw.sbuf("halfpi", [128, 1], F32)
        fw.memset(self.halfpi.all(), float(np.pi / 2))
        fw.ring("ps", 6, [128, 512], F32, space="psum")
        fw.ring("psacc", 2, [128, 512], F32, space="psum")
        fw.dma("sp", self.ones.all(), self.cst["c_ones"].all())
        fw.dma("sp", self.ident.all(), self.cst["c_ident"].all())
        fw.dma("sp", self.normw.all(), self.norm_w.all())

    def evac(self, out, in_):
        self.evac_i += 1
        if self.evac_i % 2:
            self.fw.copy(out, in_, eng="act")
        else:
            self.fw.copy(out, in_, eng="dve")

    def phase_A(self, li, xsrc):
        fw, L = self.fw, self.L
        fw.push()
        hT = fw.sbuf("hT", [128, KT, L], BF16)
        rstd = fw.sbuf("rstd", [128, L], F32)
        fw.ring("xt", 3, [128, 512], F32)
        fw.ring("sq", 2, [128, 512], F32)
        fw.ring("wt", 3, [128, KT * 128], BF16)
        fw.ring("stage", 2, [128, L], F32)
        for tb in range(self.NB):
            ts_ = slice(tb * 512, (tb + 1) * 512)
            ps = fw.next("ps")
            for kt in range(KT):
                xt = fw.next("xt")
                fw.dma("sp", xt.all(), xsrc[kt * 128:(kt + 1) * 128, ts_])
                sq = fw.next("sq")
                fw.act(sq.all(), xt.all(), AF.Square)
                fw.matmul(ps.all(), self.ones.all(), sq.all(), start=(kt == 0), stop=(kt == KT - 1))
            tmp = fw.next("sq")
            fw.ts(tmp.all(), ps.all(), 1.0 / D, ALU.mult, NORM_EPS, ALU.add)
            fw.act(tmp.all(), tmp.all(), AF.Ln)
            fw.act(rstd[:, ts_], tmp.all(), AF.Exp, scale=-0.5)
        for tb in range(self.NB):
            ts_ = slice(tb * 512, (tb + 1) * 512)
            for kt in range(KT):
                xt = fw.next("xt")
                fw.dma("sp", xt.all(), xsrc[kt * 128:(kt + 1) * 128, ts_])
                fw.stt(hT[:, kt, ts_], xt.all(), self.normw[:, li * KT + kt:li * KT + kt + 1],
                       rstd[:, ts_], ALU.mult, ALU.mult)
        for j in range(NCT):
            wt = fw.next("wt")
            r0 = (li * NCT + j) * 128
            fw.dma("pool", wt.all(), self.w_in[r0:r0 + 128, :])
            st = fw.next("stage")
            for tb in range(self.NB):
                ts_ = slice(tb * 512, (tb + 1) * 512)
                ps = fw.next("ps")
                for kt in range(KT):
                    fw.matmul(ps.all(), wt[:, kt * 128:(kt + 1) * 128], hT[:, kt, ts_],
                              start=(kt == 0), stop=(kt == KT - 1))
                self.evac(st[:, ts_], ps.all())
            fw.dma("sp", self.projT[j * 128:(j + 1) * 128, :], st.all())
        fw.pop()

    def sincos(self, cos_o, sin_o, phi, shape, tagn):
        fw = self.fw
        ki = fw.sbuf("sc_ki_" + tagn, shape, I32)
        kf = fw.sbuf("sc_kf_" + tagn, shape, F32)
        red = fw.sbuf("sc_red_" + tagn, shape, F32)
        s2 = fw.sbuf("sc_s2_" + tagn, shape, F32)
        c2 = fw.sbuf("sc_c2_" + tagn, shape, F32)
        fw.ts(ki.all(), phi, float(1.0 / (2 * np.pi)), ALU.mult)
        fw.copy(kf.all(), ki.all())
        fw.stt(red.all(), kf.all(), -TWO_PI_1, phi, ALU.mult, ALU.add)
        fw.stt(red.all(), kf.all(), -TWO_PI_2, red.all(), ALU.mult, ALU.add)
        fw.act(s2.all(), red.all(), AF.Sin, scale=0.5)
        fw.act(c2.all(), red.all(), AF.Sin, scale=0.5, bias=self.halfpi[0:shape[0], 0:1])
        fw.stt(sin_o, s2.all(), 2.0, c2.all(), ALU.mult, ALU.mult)
        fw.tt(kf.all(), s2.all(), s2.all(), ALU.mult)
        fw.ts(cos_o, kf.all(), -2.0, ALU.mult, 1.0, ALU.add)

    def phase_S5(self, li):
        fw, L, NL = self.fw, self.L, self.NL
        TC = 128
        TB = 256
        fw.push()
        pg = fw.sbuf("s5pg", [128, 3, 64], F32)
        fw.dma("sp", pg.all().x(lambda a: a.rearrange("p a g -> p (a g)")), self.s5_pg[:, li * 192:(li + 1) * 192])
        are, aim, ldt = pg[:, 0, :], pg[:, 1, :], pg[:, 2, :]
        mk = lambda n: fw.sbuf(n, [128, 64], F32)
        dt_, ang, mag, cs, sn, abre, abim, den, cre, cim, t1, t2 = [mk("s5p%d" % i) for i in range(12)]
        fw.act(dt_.all(), ldt, AF.Exp)
        fw.tt(ang.all(), aim, dt_.all(), ALU.mult)
        fw.tt(t1.all(), are, dt_.all(), ALU.mult)
        fw.act(mag.all(), t1.all(), AF.Exp)
        fw.push()
        self.sincos(cs.all(), sn.all(), ang.all(), [128, 64], "p")
        fw.pop()
        fw.tt(abre.all(), mag.all(), cs.all(), ALU.mult)
        fw.tt(abim.all(), mag.all(), sn.all(), ALU.mult)
        fw.tt(den.all(), are, are, ALU.mult)
        fw.tt(t1.all(), aim, aim, ALU.mult)
        fw.tt(den.all(), den.all(), t1.all(), ALU.add)
        fw.op("dve", "reciprocal", [den.all()], [den.all()], den.all(), den.all())
        fw.ts(t1.all(), abre.all(), -1.0, ALU.add)
        fw.tt(cre.all(), t1.all(), are, ALU.mult)
        fw.tt(t2.all(), abim.all(), aim, ALU.mult)
        fw.tt(cre.all(), cre.all(), t2.all(), ALU.add)
        fw.tt(cre.all(), cre.all(), den.all(), ALU.mult)
        fw.tt(cim.all(), abim.all(), are, ALU.mult)
        fw.tt(t2.all(), t1.all(), aim, ALU.mult)
        fw.tt(cim.all(), cim.all(), t2.all(), ALU.subtract)
        fw.tt(cim.all(), cim.all(), den.all(), ALU.mult)
        import os
        stage = float(os.environ.get("S5_STAGE", "9"))
        if stage < 1:
            fw.pop(); return
        L1 = fw.sbuf("s5L1", [128, 32, 128], BF16)
        L2 = fw.sbuf("s5L2", [128, 32, 128], BF16)
        LY1 = fw.sbuf("s5LY1", [128, 8, 8, 64], BF16)
        LY2 = fw.sbuf("s5LY2", [128, 8, 8, 64], BF16)
        CTb = fw.sbuf("s5CT", [128, 64, TC], F32)
        STb = fw.sbuf("s5ST", [128, 64, TC], F32)
        gmask = fw.sbuf("s5gm", [128, 8], F32)
        swp = fw.sbuf("s5sw", [128, 128], F32)
        sgn = fw.sbuf("s5sg", [128, 1], F32)
        vec = fw.sbuf("s5vec", [128, 16], F32)
        gluW = fw.sbuf("s5glu", [128, 8, 1024], BF16)
        fw.dma("sp", gmask.all(), self.cst["c_gmask"].all())
        fw.dma("sp", swp.all(), self.cst["c_swap"].all())
        fw.dma("sp", sgn.all(), self.cst["c_sgn"].all())
        fw.dma("sp", vec.all(), self.s5_vec[:, li * 16:(li + 1) * 16])
        fw.dma("pool", gluW.all().x(lambda a: a.rearrange("p k m -> p (k m)")), self.s5_glu[li * 128:(li + 1) * 128, :])
        if stage < 2:
            fw.pop(); return
        fw.push()
        braw = fw.sbuf("s5braw", [64, 2, 64, 16], F32)
        fw.dma("sp", braw.all().x(lambda a: a.rearrange("p a g c -> p (a g c)")),
               self.s5_b[:, li * 2048:(li + 1) * 2048])
        bbre = fw.sbuf("s5bbre", [64, 64, 16], F32)
        bbim = fw.sbuf("s5bbim", [64, 64, 16], F32)
        tmpb = fw.sbuf("s5tmpb", [64, 64, 16], F32)
        bc3 = lambda v: v.x(lambda a: a.unsqueeze(2).to_broadcast([64, 64, 16]))
        fw.tt(bbre.all(), braw[:, 0, :, :], bc3(cre[0:64, :]), ALU.mult)
        fw.tt(tmpb.all(), braw[:, 1, :, :], bc3(cim[0:64, :]), ALU.mult)
        fw.tt(bbre.all(), bbre.all(), tmpb.all(), ALU.subtract)
        fw.tt(bbim.all(), braw[:, 1, :, :], bc3(cre[0:64, :]), ALU.mult)
        fw.tt(tmpb.all(), braw[:, 0, :, :], bc3(cim[0:64, :]), ALU.mult)
        fw.tt(bbim.all(), bbim.all(), tmpb.all(), ALU.add)
        cat1 = fw.sbuf("s5cat1", [128, 128], F32)
        cat2 = fw.sbuf("s5cat2", [128, 128], F32)
        for ct in range(8 if float(os.environ.get('S5_SUB','9')) > 0.5 else 0):
            ps = fw.next("ps")
            fl = lambda v: v.x(lambda a: a.rearrange("p g c -> p (g c)"))
            fw.matmul(ps[:, 0:64], fl(bbre[:, ct * 8:(ct + 1) * 8, :]), self.ident[0:64, 0:64])
            fw.matmul(ps[:, 64:128], fl(bbim[:, ct * 8:(ct + 1) * 8, :]), self.ident[0:64, 0:64])
            fw.copy(cat1.all(), ps[:, 0:128], eng=os.environ.get("CAT_ENG","act"))
            fw.copy(cat2[:, 0:64], ps[:, 64:128])
            fw.ts(cat2[:, 64:128], ps[:, 0:64], -1.0, ALU.mult)
            for gp in range(8 if float(os.environ.get('S5_SUB','9')) > 1.5 else 0):
                half, slot = gp // 4, ct * 4 + gp % 4
                rows = slice(half * 64, half * 64 + 64)
                fw.ts(L1[rows, slot, :], cat1[rows, :], gmask[rows, gp:gp + 1], ALU.mult)
                fw.ts(L2[rows, slot, :], cat2[rows, :], gmask[rows, gp:gp + 1], ALU.mult)
        fw.pop()
        if stage < 3:
            fw.pop(); return
        fw.push()
        craw = fw.sbuf("s5craw", [128, 2, 8, 8, 16], F32)
        fw.dma("sp", craw.all().x(lambda a: a.rearrange("p a t g c -> p (a t g c)")),
               self.s5_c[:, li * 2048:(li + 1) * 2048])
        fw.memset(LY1.all(), 0.0)
        fw.memset(LY2.all(), 0.0, eng="pool")
        for gp in range(8):
            cs_ = slice((gp % 4) * 16, (gp % 4) * 16 + 16)
            fw.copy(LY1[0:64, :, gp, cs_], craw[0:64, 0, :, gp, :])
            fw.ts(LY1[64:128, :, gp, cs_], craw[64:128, 1, :, gp, :], -1.0, ALU.mult)
            fw.ts(LY2[0:64, :, gp, cs_], craw[0:64, 1, :, gp, :], -1.0, ALU.mult)
            fw.ts(LY2[64:128, :, gp, cs_], craw[64:128, 0, :, gp, :], -1.0, ALU.mult)
        fw.pop()
        if stage < 4:
            fw.pop(); return
        fw.push()
        iota = fw.sbuf("s5iota", [128, TC], F32)
        fw.dma("sp", iota.all(), self.cst["c_iota"].all())
        phi = fw.sbuf("s5phi", [128, 8, TC], F32)
        for gb in range(8):
            fw.tt(phi.all(),
                  ang[:, gb * 8:(gb + 1) * 8].x(lambda a: a.unsqueeze(2).to_broadcast([128, 8, TC])),
                  iota.all().x(lambda a: a.unsqueeze(1).to_broadcast([128, 8, TC])), ALU.mult)
            fw.push()
            self.sincos(CTb[:, gb * 8:(gb + 1) * 8, :], STb[:, gb * 8:(gb + 1) * 8, :], phi.all(), [128, 8, TC], "t")
            fw.pop()
        fw.pop()
        ssgn = fw.sbuf("s5ssgn", [128, 64], F32)
        cend = fw.sbuf("s5cend", [128, 64], F32)
        fw.ts(ssgn.all(), STb[:, :, TC - 1], sgn[:, 0:1], ALU.mult)
        fw.copy(cend.all(), CTb[:, :, TC - 1])
        if stage < 5:
            fw.pop(); return
        if self.debug and li == 0:
            self.dbg_y = self.scratch("dbg_y", [1024, L])
            self.dbg_tab = self.scratch("dbg_tab", [128, 4 * TC])
            self.dbg_par = self.scratch("dbg_par", [128, 4 * 64])
            self.dbg_L = self.scratch("dbg_L", [128, 4 * 128], BF16)
            fw.dma("sp", self.dbg_tab[:, 0:TC], CTb[:, 0, :])
            fw.dma("sp", self.dbg_tab[:, TC:2 * TC], STb[:, 0, :])
            fw.dma("sp", self.dbg_tab[:, 2 * TC:3 * TC], CTb[:, 5, :])
            fw.dma("sp", self.dbg_tab[:, 3 * TC:4 * TC], STb[:, 5, :])
            fw.dma("sp", self.dbg_par[:, 0:64], mag.all())
            fw.dma("sp", self.dbg_par[:, 64:128], ang.all())
            fw.dma("sp", self.dbg_par[:, 128:192], cre.all())
            fw.dma("sp", self.dbg_par[:, 192:256], cim.all())
            fw.dma("sp", self.dbg_L[:, 0:128], L1[:, 0, :])
            fw.dma("sp", self.dbg_L[:, 128:256], L2[:, 0, :])
            fw.dma("sp", self.dbg_L[:, 256:320], LY1[:, 0, 0, :])
            fw.dma("sp", self.dbg_L[:, 320:384], LY2[:, 0, 0, :])
            fw.dma("sp", self.dbg_L[:, 384:512], L1[:, 5, :])
        sprev = fw.sbuf("s5sprev", [128, 64], F32)
        send = fw.sbuf("s5send", [128, 64], F32)
        fw.memset(sprev.all(), 0.0)
        uT = fw.sbuf("s5uT", [128, 8, TB], F32)
        ub = fw.sbuf("s5ub", [128, 8, TB], BF16)
        ysb = fw.sbuf("s5y", [128, 8, TB], F32)
        ygb = fw.sbuf("s5yg", [128, 8, TB], BF16)
        fw.ring("s5t1", 2, [128, TC], F32)
        fw.ring("s5t2", 2, [128, TC], F32)
        fw.ring("s5bt", 2, [128, TC], F32)
        fw.ring("s5st", 3, [128, TC], F32)
        fw.ring("s5z1", 3, [128, TC], BF16)
        fw.ring("s5z2", 3, [128, TC], BF16)
        fw.ring("s5zT", 2, [128, TB], F32)
        fw.ring("s5sg", 2, [128, TB], F32)
        fw.ring("s5oc", 2, [128, TB], BF16)
        for tb in range(L // TB):
            t0 = tb * TB
            for ct in range(8):
                r0 = (CT_SU + ct) * 128
                fw.dma("sp", uT[:, ct, :], self.projT[r0:r0 + 128, t0:t0 + TB])
                fw.copy(ub[:, ct, :], uT[:, ct, :], eng="pool")
            for cc in range(TB // TC):
                csl = slice(cc * TC, (cc + 1) * TC)
                for ct in range(8):
                    psy = fw.next("psacc")
                    for gp in range(8):
                        g = ct * 8 + gp
                        half, slot = gp // 4, ct * 4 + gp % 4
                        rows = slice(half * 64, half * 64 + 64)
                        psx = fw.next("ps")
                        fw.matmul(psx[:, 0:TC], L1[rows, slot, :], ub[rows, ct, csl])
                        fw.matmul(psx[:, TC:2 * TC], L2[rows, slot, :], ub[rows, ct, csl])
                        a1, a2, bt, st = fw.next("s5t1"), fw.next("s5t2"), fw.next("s5bt"), fw.next("s5st")
                        fw.tt(a1.all(), psx[:, 0:TC], CTb[:, g, :], ALU.mult)
                        fw.tt(a2.all(), psx[:, TC:2 * TC], STb[:, g, :], ALU.mult)
                        fw.tt(bt.all(), a1.all(), a2.all(), ALU.add, eng="pool")
                        fw.scan(st.all(), mag[:, g:g + 1].bc([128, TC]), bt.all(), sprev[:, g:g + 1])
                        fw.copy(send[:, g:g + 1], st[:, TC - 1:TC], eng="act")
                        z1, z2 = fw.next("s5z1"), fw.next("s5z2")
                        fw.tt(z1.all(), st.all(), CTb[:, g, :], ALU.mult, eng="pool")
                        fw.tt(z2.all(), st.all(), STb[:, g, :], ALU.mult, eng="pool")
                        if self.debug and li == 0 and tb == 0 and cc == 0 and g == 0:
                            self.dbg_u = self.scratch("dbg_u", [128, 6 * TC])
                            dbt = fw.sbuf("dbgt", [128, 6 * TC], F32)
                            fw.copy(dbt[:, 0:TC], psx[:, 0:TC])
                            fw.copy(dbt[:, TC:2 * TC], psx[:, TC:2 * TC])
                            fw.copy(dbt[:, 2 * TC:3 * TC], bt.all())
                            fw.copy(dbt[:, 3 * TC:4 * TC], st.all())
                            fw.copy(dbt[:, 4 * TC:5 * TC], z1.all())
                            fw.copy(dbt[:, 5 * TC:6 * TC], ub[:, 0, 0:TC])
                            fw.dma("sp", self.dbg_u.all(), dbt.all())
                        orow = slice(half * 64, half * 64 + 64)
                        first, last = (gp % 4 == 0), (gp % 4 == 3)
                        fw.matmul(psy[orow, 0:TC], LY1[:, ct, gp, :], z1.all(), start=first, stop=False)
                        fw.matmul(psy[orow, 0:TC], LY2[:, ct, gp, :], z2.all(), start=False, stop=last)
                    fw.stt(ysb[:, ct, csl], uT[:, ct, csl], vec[:, ct:ct + 1], psy[:, 0:TC], ALU.mult, ALU.add)
                pss = fw.next("ps")
                fw.matmul(pss[:, 0:64], swp.all(), send.all())
                fw.tt(sprev.all(), send.all(), cend.all(), ALU.mult)
                fw.tt(send.all(), pss[:, 0:64], ssgn.all(), ALU.mult)
                fw.tt(sprev.all(), sprev.all(), send.all(), ALU.add)
            if self.debug and li == 0:
                for ct in range(8):
                    fw.dma("sp", self.dbg_y[ct * 128:(ct + 1) * 128, t0:t0 + TB], ysb[:, ct, :])
            for ct in range(8):
                fw.act(ysb[:, ct, :], ysb[:, ct, :], AF.Gelu_apprx_tanh)
                fw.copy(ygb[:, ct, :], ysb[:, ct, :], eng="pool")
            for mt in range(8):
                ps = fw.next("ps")
                for kt in range(8):
                    fw.matmul(ps[:, 0:TB], gluW[:, kt, mt * 128:(mt + 1) * 128], ygb[:, kt, :],
                              start=(kt == 0), stop=(kt == 7))
                sg = fw.next("s5sg")
                fw.act(sg.all(), ps[:, 0:TB], AF.Sigmoid, bias=vec[:, 8 + mt:9 + mt])
                fw.tt(sg.all(), sg.all(), ysb[:, mt, :], ALU.mult)
                zT = fw.next("s5zT")
                r0 = (CT_SZ + mt) * 128
                fw.dma("sp", zT.all(), self.projT[r0:r0 + 128, t0:t0 + TB])
                fw.act(zT.all(), zT.all(), AF.Silu)
                oc = fw.next("s5oc")
                fw.tt(oc.all(), sg.all(), zT.all(), ALU.mult)
                fw.dma("sp", self.oT[2048 + mt * 128:2048 + (mt + 1) * 128, t0:t0 + TB], oc.all())
        fw.pop()


    def neumann(self, A_T, A_, tagn):
        fw = self.fw
        X = fw.next("nm_x")
        fw.tt(X.all(), self.identb.all(), A_T.all(), ALU.subtract)
        P, Q = A_T, A_
        for l in range(1, 7):
            psq = fw.next("ps")
            fw.matmul(psq[:, 0:128], P.all(), Q.all())
            if l < 6:
                fw.matmul(psq[:, 128:256], Q.all(), P.all())
            Qn = fw.next("nm_q")
            fw.copy(Qn.all(), psq[:, 0:128], eng="act")
            if l < 6:
                Pn = fw.next("nm_p")
                fw.copy(Pn.all(), psq[:, 128:256], eng="act")
            psx = fw.next("ps")
            fw.matmul(psx[:, 0:128], Qn.all(), X.all())
            Xn = fw.next("nm_x")
            fw.tt(Xn.all(), X.all(), psx[:, 0:128], ALU.add)
            X = Xn
            Q = Qn
            if l < 6:
                P = Pn
        psr = fw.next("ps")
        fw.matmul(psr[:, 0:128], A_.all(), X.all())
        pstv = psr.all().x(lambda a: a.bitcast(BF16))
        xtv = pstv.x(lambda a: a[:, 512:640])
        fw.transpose(xtv, X.all(), self.identb.all())
        t_ = fw.next("nm_p")
        fw.tt(t_.all(), self.identb.all(), X.all(), ALU.subtract)
        Rb = fw.next("nm_q")
        fw.tt(Rb.all(), t_.all(), psr[:, 0:128], ALU.subtract)
        XT = fw.next("nm_xt")
        fw.copy(XT.all(), xtv)
        psd = fw.next("ps")
        fw.matmul(psd[:, 0:128], XT.all(), Rb.all())
        dX = fw.next("nm_dx")
        fw.copy(dX.all(), psd[:, 0:128], eng="act")
        return X, dX

    def nb_alloc(self, n, tagn):
        fw = self.fw
        mk = lambda nm: [fw.sbuf("%s_%s%d" % (tagn, nm, i), [128, n, 128], BF16) for i in range(2)]
        return dict(P=mk("nbP"), Q=mk("nbQ"), X=mk("nbX"),
                    dX=fw.sbuf(tagn + "_nbdX", [128, n, 128], BF16), XT=fw.sbuf(tagn + "_nbXT", [128, n, 128], BF16))

    def neumann_batch(self, AT, A_, n, nbuf):
        fw = self.fw
        G = 4
        ng = n // G
        Pb, Qb, Xb, dXt, XTt = nbuf["P"], nbuf["Q"], nbuf["X"], nbuf["dX"], nbuf["XT"]
        fl = lambda v: v.x(lambda a: a.rearrange("p g c -> p (g c)"))
        idb = self.identb.all().x(lambda a: a.unsqueeze(1).to_broadcast([128, G, 128]))
        for g in range(ng):
            gs = slice(g * G, (g + 1) * G)
            fw.tt(Xb[0][:, gs, :], idb, AT[:, gs, :], ALU.subtract)
        P, Q, X = AT, A_, Xb[0]
        xi = 0
        for l in range(1, 7):
            Qn, Pn, Xn = Qb[l % 2], Pb[l % 2], Xb[(xi + 1) % 2]
            for g in range(ng):
                gs = slice(g * G, (g + 1) * G)
                psQ = fw.next("ps")
                for k in range(G):
                    fw.matmul(psQ[:, k * 128:(k + 1) * 128], P[:, g * G + k, :], Q[:, g * G + k, :])
                fw.copy(fl(Qn[:, gs, :]), psQ.all(), eng="act")
                if l < 6:
                    psP = fw.next("ps")
                    for k in range(G):
                        fw.matmul(psP[:, k * 128:(k + 1) * 128], Q[:, g * G + k, :], P[:, g * G + k, :])
                    fw.copy(fl(Pn[:, gs, :]), psP.all(), eng="act")
                psX = fw.next("ps")
                for k in range(G):
                    fw.matmul(psX[:, k * 128:(k + 1) * 128], Qn[:, g * G + k, :], X[:, g * G + k, :])
                fw.tt(fl(Xn[:, gs, :]), fl(X[:, gs, :]), psX.all(), ALU.add)
            P, Q, X = Pn, Qn, Xn
            xi += 1
        tmpR = Pb[0]
        Rb = Pb[1]
        for g in range(ng):
            gs = slice(g * G, (g + 1) * G)
            psr = fw.next("ps")
            for k in range(G):
                fw.matmul(psr[:, k * 128:(k + 1) * 128], A_[:, g * G + k, :], X[:, g * G + k, :])
            pst_ = fw.next("ps")
            pstv = pst_.all().x(lambda a: a.bitcast(BF16))
            for k in range(G):
                fw.transpose(pstv.x(lambda a, k=k: a[:, k * 128:(k + 1) * 128]), X[:, g * G + k, :], self.identb.all())
            fw.tt(tmpR[:, gs, :], idb, X[:, gs, :], ALU.subtract)
            fw.tt(fl(Rb[:, gs, :]), fl(tmpR[:, gs, :]), psr.all(), ALU.subtract)
            fw.copy(fl(XTt[:, gs, :]), pstv.x(lambda a: a[:, 0:512]), eng="act")
            psd = fw.next("ps")
            for k in range(G):
                fw.matmul(psd[:, k * 128:(k + 1) * 128], XTt[:, g * G + k, :], Rb[:, g * G + k, :])
            fw.copy(fl(dXt[:, gs, :]), psd.all(), eng="act")
        return X, dXt

    def phase_GDN(self, li):
        fw, L, NL = self.fw, self.L, self.NL
        C = 128
        NCH = L // C
        fw.push()
        sel = fw.sbuf("g_sel", [64, 16, 128], F32)
        fw.dma("sp", sel.all().x(lambda a: a.rearrange("p a b -> p (a b)")), self.cst["c_sel"].all())
        msk = {}
        for nm in ("su", "sl", "iu"):
            msk[nm] = fw.sbuf("g_m" + nm, [128, 128], F32)
            fw.dma("sp", msk[nm].all(), self.cst["c_mask_" + nm].all())
        par = fw.sbuf("g_par", [64, 2], F32)
        fw.dma("sp", par.all(), self.gdn_par[:, li * 2:(li + 1) * 2])
        vec = fw.sbuf("g_vec", [128, 5, 24], F32)
        fw.dma("sp", vec.all().x(lambda a: a.rearrange("p a b -> p (a b)")), self.gdn_vec[:, li * 120:(li + 1) * 120])
        onec = fw.sbuf("g_onec", [128, 1], F32)
        fw.memset(onec.all(), 1.0)
        epsc = fw.sbuf("g_epsc", [128, 1], F32)
        fw.memset(epsc.all(), 1e-6)
        nA = fw.sbuf("g_nA", [64, 1], F32)
        fw.act(nA.all(), par[:, 1:2], AF.Exp)
        fw.ts(nA.all(), nA.all(), -1.0, ALU.mult)
        SCX = fw.sbuf("g_scx", [64, NCH, 2, C], F32)
        NSC = fw.sbuf("g_nsc", [64, NCH, C], F32)
        TSS = fw.sbuf("g_tss", [128, NCH, 24], F32)
        QT = fw.sbuf("g_qT", [128, 8, L], BF16)
        KTt = fw.sbuf("g_kT", [128, 8, L], BF16)
        VT = fw.sbuf("g_vT", [128, 8, L], BF16)
        fw.push()
        ba = fw.sbuf("g_ba", [64, L], F32)
        gg = fw.sbuf("g_g", [64, L], F32)
        fw.dma("sp", ba.all(), self.projT[CT_GBA * 128:CT_GBA * 128 + 64, :])
        fw.memset(SCX.all(), 0.0)
        fw.memset(NSC.all(), 0.0)
        scx0 = SCX[:, :, 0, :]
        v3 = lambda v: v.x(lambda a: a.rearrange("p (n c) -> p n c", c=C))
        fw.act(gg[0:8, :], ba[0:8, :], AF.Sigmoid)
        fw.act(SCX[0:8, :, 0, :], v3(gg[0:8, :]), AF.Ln)
        fw.act(gg[32:40, :], ba[32:40, :], AF.Exp, bias=par[32:40, 0:1])
        fw.act(gg[32:40, :], gg[32:40, :], AF.Ln, bias=onec[32:40, 0:1])
        fw.ts(gg[32:40, :], gg[32:40, :], nA[32:40, 0:1], ALU.mult)
        for ch in range(NCH):
            fw.scan(SCX[32:40, ch, 0, :], onec[32:40, 0:1].bc([8, C]), gg[32:40, ch * C:(ch + 1) * C], 0.0)
            fw.act(SCX[32:40, ch, 1, :], SCX[32:40, ch, 0, :], AF.Identity, scale=-1.0, bias=SCX[32:40, ch, 0, C - 1:C])
            fw.ts(NSC[32:40, ch, :], SCX[32:40, ch, 0, :], -1.0, ALU.mult)
        for ch in range(NCH):
            ps = fw.next("ps")
            fw.matmul(ps[:, 0:64], SCX[:, ch, 0, :], self.ident[0:64, 0:64])
            fw.matmul(ps[:, 64:128], SCX[:, ch, 1, :], self.ident[0:64, 0:64])
            tsr = fw.sbuf("g_tsr%d" % (ch % 2), [128, 128], F32) if ch < 2 else fw.tensors["g_tsr%d" % (ch % 2)]
            fw.copy(tsr.all(), ps[:, 0:128])
            fw.act(TSS[:, ch, 0:8], tsr[:, 0:8], AF.Exp)
            fw.tt(tsr[:, 8:16], tsr[:, 0:8], tsr[:, 32:40], ALU.add)
            fw.act(TSS[:, ch, 8:16], tsr[:, 8:16], AF.Exp)
            fw.act(TSS[:, ch, 16:24], tsr[:, 96:104], AF.Exp)
        fw.pop()
        fw.push()
        fw.ring("g_raw", 2, [128, L + 3], F32)
        fw.ring("g_acc", 2, [128, L], F32)
        fw.ring("g_sq", 2, [128, 512], BF16)
        fw.ring("g_rn", 2, [128, 512], F32)
        for j in range(24):
            raw = fw.next("g_raw")
            fw.memset(raw[:, 0:3], 0.0, eng="pool")
            fw.dma("sp", raw[:, 3:L + 3], self.projT[(CT_GQKV + j) * 128:(CT_GQKV + j + 1) * 128, :])
            acc = fw.next("g_acc")
            fw.ts(acc.all(), raw[:, 0:L], vec[:, 0, j:j + 1], ALU.mult)
            for k in range(1, 4):
                fw.stt(acc.all(), raw[:, k:k + L], vec[:, k, j:j + 1], acc.all(), ALU.mult, ALU.add)
            typ, h = j // 8, j % 8
            if typ == 2:
                fw.act(VT[:, h, :], acc.all(), AF.Silu)
                continue
            fw.act(acc.all(), acc.all(), AF.Silu)
            dst = QT if typ == 0 else KTt
            for tb in range(self.NB):
                ts_ = slice(tb * 512, (tb + 1) * 512)
                sq = fw.next("g_sq")
                fw.tt(sq.all(), acc[:, ts_], acc[:, ts_], ALU.mult, eng="pool")
                ps = fw.next("ps")
                fw.matmul(ps.all(), self.onesb.all(), sq.all())
                rn = fw.next("g_rn")
                fw.act(rn.all(), ps.all(), AF.Ln, bias=epsc[:, 0:1])
                fw.act(rn.all(), rn.all(), AF.Exp, scale=-0.5)
                if typ == 0:
                    fw.stt(dst[:, h, ts_], acc[:, ts_], float(128 ** -0.5), rn.all(), ALU.mult, ALU.mult)
                else:
                    fw.tt(dst[:, h, ts_], acc[:, ts_], rn.all(), ALU.mult)
        fw.pop()
        S32 = fw.sbuf("g_S32", [128, 8, 128], F32)
        Sb = fw.sbuf("g_Sb", [128, 8, 128], BF16)
        fw.memset(S32.all(), 0.0)
        fw.memset(Sb.all(), 0.0)
        import os
        GD = F32 if os.environ.get("GDN_F32", "0") == "1" else BF16
        for nm, n_ in (("nm_x", 3), ("nm_q", 2), ("nm_p", 2), ("nm_xt", 2), ("nm_dx", 2), ("g_AT", 2), ("g_A", 2), ("g_at", 2),
                       ("g_ktk", 2), ("g_kd", 2), ("g_vb", 2), ("g_wT", 2), ("g_vn", 2), ("g_qd", 2)):
            fw.ring(nm, n_, [128, 128], GD)
        if GD == F32:
            Sb = S32
        fw.ring("g_D", 3, [128, 128], F32)
        fw.ring("g_gam", 2, [128, 128], F32)
        obuf = fw.sbuf("g_obuf", [128, 8, C], F32)
        osq = fw.sbuf("g_osq", [128, 8, C], BF16)
        zt = fw.sbuf("g_zt", [128, 8, C], F32)
        ogb = fw.sbuf("g_ogb", [128, 8, C], BF16)
        epsn = fw.sbuf("g_epsn", [128, 1], F32)
        fw.memset(epsn.all(), NORM_EPS)
        nbuf_g = self.nb_alloc(8, "g")
        ATg = fw.sbuf("g_ATg", [128, 8, 128], BF16)
        Ag = fw.sbuf("g_Ag", [128, 8, 128], BF16)
        fw.ring("g_at8", 9, [128, 128], BF16)
        fw.ring("g_ktk8", 9, [128, 128], BF16)
        fw.ring("g_kd8", 9, [128, 128], BF16)
        fw.ring("g_vb8", 9, [128, 128], BF16)
        fw.ring("g_qd8", 9, [128, 128], BF16)
        fw.ring("g_gl8", 9, [128, 1], F32)
        for ch in range(NCH):
            cs_ = slice(ch * C, (ch + 1) * C)
            per = []
            for h in range(8):
                selBG, selG = sel[:, h, :], sel[:, 8 + h, :]
                sc, nsc = SCX[:, ch, 0, :], NSC[:, ch, :]
                kT, qT, vT = KTt[:, h, cs_], QT[:, h, cs_], VT[:, h, cs_]
                ps1 = fw.next("ps")
                fw.matmul(ps1[:, 0:128], selBG, sc, start=True, stop=False)
                fw.matmul(ps1[:, 0:128], nsc, selG, start=False, stop=False)
                fw.matmul(ps1[:, 0:128], self.ident.all(), msk["su"].all(), start=False, stop=True)
                fw.matmul(ps1[:, 128:256], sc, selBG, start=True, stop=False)
                fw.matmul(ps1[:, 128:256], selG, nsc, start=False, stop=False)
                fw.matmul(ps1[:, 128:256], self.ident.all(), msk["sl"].all(), start=False, stop=True)
                fw.matmul(ps1[:, 256:384], selG, sc, start=True, stop=False)
                fw.matmul(ps1[:, 256:384], nsc, selG, start=False, stop=False)
                fw.matmul(ps1[:, 256:384], self.ident.all(), msk["iu"].all(), start=False, stop=True)
                fw.matmul(ps1[:, 384:512], selG, sc, start=True, stop=True)
                DT_, D_, DI_ = fw.next("g_D"), fw.next("g_D"), fw.next("g_D")
                gam = fw.next("g_gam")
                fw.act(DT_.all(), ps1[:, 0:128], AF.Exp)
                fw.act(D_.all(), ps1[:, 128:256], AF.Exp)
                fw.act(DI_.all(), ps1[:, 256:384], AF.Exp)
                fw.act(gam.all(), ps1[:, 384:512], AF.Exp)
                ps2 = fw.next("ps")
                fw.matmul(ps2[:, 0:128], kT, kT)
                fw.matmul(ps2[:, 128:256], kT, qT)
                at = fw.next("g_at8")
                fw.tt(ATg[:, h, :], ps2[:, 0:128], DT_.all(), ALU.mult)
                fw.tt(Ag[:, h, :], ps2[:, 0:128], D_.all(), ALU.mult)
                fw.tt(at.all(), ps2[:, 128:256], DI_.all(), ALU.mult)
                ps3 = fw.next("ps")
                pst = ps3.all().x(lambda a: a.bitcast(BF16))
                fw.transpose(pst.x(lambda a: a[:, 0:128]), kT, self.identb.all())
                fw.transpose(pst.x(lambda a: a[:, 128:256]), vT, self.identb.all())
                ktk, kd, vb = fw.next("g_ktk8"), fw.next("g_kd8"), fw.next("g_vb8")
                kview = pst.x(lambda a: a[:, 0:128])
                vview = pst.x(lambda a: a[:, 128:256])
                fw.ts(ktk.all(), kview, TSS[:, ch, 8 + h:9 + h], ALU.mult)
                fw.ts(kd.all(), kview, TSS[:, ch, 16 + h:17 + h], ALU.mult)
                fw.ts(vb.all(), vview, TSS[:, ch, h:h + 1], ALU.mult)
                qd = fw.next("g_qd8")
                fw.tt(qd.all(), qT, gam.all(), ALU.mult, eng="pool")
                gl = fw.next("g_gl8")
                fw.copy(gl.all(), gam[:, C - 1:C], eng="pool")
                per.append((at, ktk, kd, vb, qd, gl))
            Xg, dXg = self.neumann_batch(ATg, Ag, 8, nbuf_g)
            for h in range(8):
                at, ktk, kd, vb, qd, gl = per[h]
                ps4 = fw.next("ps")
                fw.matmul(ps4[:, 0:128], ktk.all(), Xg[:, h, :], start=True, stop=False)
                fw.matmul(ps4[:, 0:128], ktk.all(), dXg[:, h, :], start=False, stop=True)
                wT = fw.next("g_wT")
                fw.ts(wT.all(), ps4[:, 0:128], -1.0, ALU.mult)
                ps5 = fw.next("ps")
                fw.matmul(ps5[:, 0:128], Xg[:, h, :], vb.all(), start=True, stop=False)
                fw.matmul(ps5[:, 0:128], dXg[:, h, :], vb.all(), start=False, stop=False)
                fw.matmul(ps5[:, 0:128], wT.all(), Sb[:, h, :], start=False, stop=True)
                vn = fw.next("g_vn")
                fw.copy(vn.all(), ps5[:, 0:128], eng="act")
                ps6 = fw.next("ps")
                fw.matmul(ps6[:, 0:128], Sb[:, h, :], qd.all(), start=True, stop=False)
                fw.matmul(ps6[:, 0:128], vn.all(), at.all(), start=False, stop=True)
                fw.matmul(ps6[:, 128:256], kd.all(), vn.all())
                fw.copy(obuf[:, h, :], ps6[:, 0:128], eng="act")
                fw.stt(S32[:, h, :], S32[:, h, :], gl[:, 0:1], ps6[:, 128:256], ALU.mult, ALU.add)
                if GD != F32:
                    fw.copy(Sb[:, h, :], S32[:, h, :], eng="pool")
            fw.tt(osq.all(), obuf.all(), obuf.all(), ALU.mult, eng="pool")
            fw.dma("sp", zt.all(), self.projT[CT_GZ * 128:(CT_GZ + 8) * 128, cs_].x(
                lambda a: a.rearrange("(h p) t -> p h t", p=128)))
            fw.act(zt.all(), zt.all(), AF.Silu)
            for hh in range(2):
                ps = fw.next("ps")
                fw.matmul(ps.all(), self.onesb.all(),
                          osq[:, hh * 4:(hh + 1) * 4, :].x(lambda a: a.rearrange("p h c -> p (h c)")))
                rn = fw.sbuf("g_rn2", [128, 512], F32) if (ch == 0 and hh == 0) else fw.tensors["g_rn2"]
                fw.ts(rn.all(), ps.all(), 1.0 / 128, ALU.mult)
                fw.act(rn.all(), rn.all(), AF.Ln, bias=epsn[:, 0:1])
                fw.act(rn.all(), rn.all(), AF.Exp, scale=-0.5)
                ov = obuf[:, hh * 4:(hh + 1) * 4, :].x(lambda a: a.rearrange("p h c -> p (h c)"))
                zv = zt[:, hh * 4:(hh + 1) * 4, :].x(lambda a: a.rearrange("p h c -> p (h c)"))
                gv_ = ogb[:, hh * 4:(hh + 1) * 4, :].x(lambda a: a.rearrange("p h c -> p (h c)"))
                fw.stt(rn.all(), ov, vec[:, 4, 0:1], rn.all(), ALU.mult, ALU.mult)
                fw.tt(gv_, rn.all(), zv, ALU.mult)
            fw.dma("sp", self.oT[0:1024, cs_].x(lambda a: a.rearrange("(h p) t -> p h t", p=128)), ogb.all())
        fw.pop()


    def phase_RWKV(self, li):
        fw, L, NL = self.fw, self.L, self.NL
        C = 128
        NCH = L // C
        MID = 63
        fw.push()
        rv = fw.sbuf("r_rv", [128, 10, 8], F32)
        fw.dma("sp", rv.all().x(lambda a: a.rearrange("p a b -> p (a b)")), self.rw_vec[:, li * 80:(li + 1) * 80])
        mul = fw.sbuf("r_mul", [128, 2], F32)
        fw.dma("sp", mul.all(), self.rw_mul[:, li * 2:(li + 1) * 2])
        ups = fw.sbuf("r_ups", [96, 2048], BF16)
        fw.dma("pool", ups.all(), self.rw_up[li * 96:(li + 1) * 96, :])
        omka = fw.sbuf("r_omka", [128, 8], F32)
        fw.ts(omka.all(), rv[:, 6, :], -1.0, ALU.mult, 1.0, ALU.add)
        m01 = {}
        for nm in ("su", "sl", "iu"):
            m01[nm] = fw.sbuf("r_m" + nm, [128, 128], F32)
            fw.dma("sp", m01[nm].all(), self.cst["c_m01_" + nm].all())
        blkf = fw.sbuf("r_blkf", [128, 128], F32)
        blkb = fw.sbuf("r_blkb", [128, 128], BF16)
        fw.dma("sp", blkf.all(), self.cst["c_blk64"].all())
        fw.dma("pool", blkb.all(), self.cst["c_blk64"].all())
        onec = fw.sbuf("r_onec", [128, 1], F32)
        fw.memset(onec.all(), 1.0)
        eps6 = fw.sbuf("r_eps6", [128, 1], F32)
        fw.memset(eps6.all(), 1e-6)
        epsl = fw.sbuf("r_epsl", [128, 1], F32)
        fw.memset(epsl.all(), 64e-5)
        S32 = fw.sbuf("r_S32", [128, 8, 64], F32)
        fw.memset(S32.all(), 0.0)
        YB = fw.sbuf("r_YB", [128, 8, C], F32)
        BON = fw.sbuf("r_BON", [128, 8, C], F32)
        ZT = fw.sbuf("r_ZT", [128, 8, C], F32)
        OB = fw.sbuf("r_OB", [128, 8, C], BF16)
        for nm, n_ in (("r_BT", 17), ("r_M1", 17),
                       ("r_M2", 17), ("r_rh", 9), ("r_kh", 9), ("r_ah", 2), ("r_kkh", 9), ("r_vb", 2),
                       ("r_khT", 9), ("r_ahT", 9), ("r_vT", 9)):
            fw.ring(nm, n_, [128, 128], BF16)
        nbuf_r = self.nb_alloc(16, "r")
        ATr = fw.sbuf("r_ATr", [128, 16, 128], BF16)
        Ar = fw.sbuf("r_Ar", [128, 16, 128], BF16)
        for nm in ("r_raw",):
            fw.ring(nm, 4, [128, C + 1], F32)
        for nm in ("r_rs", "r_ks", "r_vs", "r_d", "r_lw", "r_a", "r_kkr", "r_kk", "r_t3", "r_km", "r_at", "r_cs",
                   "r_csp", "r_Wi", "r_Wp", "r_tmp"):
            fw.ring(nm, 2, [128, C], F32)
        fw.ring("r_Wt", 9, [128, C], F32)
        fw.ring("r_sqb", 2, [128, C], BF16)
        fw.ring("r_tw", 2, [128, C], BF16)
        fw.ring("r_al", 2, [128, C], BF16)
        fw.ring("r_col", 9, [128, 4], F32)
        fw.ring("r_Sh", 2, [128, 64], BF16)
        fw.ring("r_rhs", 2, [128, 64], BF16)
        fw.ring("r_nP", 2, [128, 64], BF16)
        fw.ring("r_st", 2, [128, 64], F32)
        fw.ring("r_e512", 3, [128, 512], F32)

        def shifted(ct_row0, ch, mu_ap, out, nrows=128):
            raw = fw.next("r_raw")
            t0 = ch * C
            if ch == 0:
                fw.memset(raw[0:nrows, 0:1], 0.0, eng="pool")
                fw.dma("sp", raw[0:nrows, 1:C + 1], self.projT[ct_row0:ct_row0 + nrows, 0:C])
            else:
                fw.dma("sp", raw[0:nrows, :], self.projT[ct_row0:ct_row0 + nrows, t0 - 1:t0 + C])
            d = fw.next("r_d")
            fw.tt(d[0:nrows, :], raw[0:nrows, 0:C], raw[0:nrows, 1:C + 1], ALU.subtract, eng="pool")
            fw.stt(out, d[0:nrows, :], mu_ap, raw[0:nrows, 1:C + 1], ALU.mult, ALU.add)

        for ch in range(NCH):
            cs_ = slice(ch * C, (ch + 1) * C)
            tw, alb, tmp = fw.next("r_tw"), fw.next("r_al"), fw.next("r_tmp")
            shifted(CT_RWL * 128, ch, mul[0:96, 0:1], tmp[0:96, :], nrows=96)
            fw.act(tw[0:96, :], tmp[0:96, :], AF.Tanh)
            tmp2 = fw.next("r_tmp")
            shifted(CT_RAL * 128, ch, mul[0:96, 1:2], tmp2[0:96, :], nrows=96)
            fw.copy(alb[0:96, :], tmp2[0:96, :], eng="pool")
            fw.dma("sp", ZT.all(), self.projT[CT_RZ * 128:(CT_RZ + 8) * 128, cs_].x(
                lambda a: a.rearrange("(h p) t -> p h t", p=128)))
            fw.act(ZT.all(), ZT.all(), AF.Silu)
            perp, perh = [], []
            for p in range(8):
                rs, ks, vs = fw.next("r_rs"), fw.next("r_ks"), fw.next("r_vs")
                shifted((CT_RR + p) * 128, ch, rv[:, 0, p:p + 1], rs.all())
                shifted((CT_RK + p) * 128, ch, rv[:, 1, p:p + 1], ks.all())
                shifted((CT_RV + p) * 128, ch, rv[:, 2, p:p + 1], vs.all())
                psl = fw.next("ps")
                fw.matmul(psl[:, 0:C], ups[:, p * 128:(p + 1) * 128], tw[0:96, :])
                fw.matmul(psl[:, C:2 * C], ups[:, 1024 + p * 128:1024 + (p + 1) * 128], alb[0:96, :])
                lw, a_ = fw.next("r_lw"), fw.next("r_a")
                fw.act(lw.all(), psl[:, 0:C], AF.Sigmoid, bias=rv[:, 3, p:p + 1])
                fw.act(a_.all(), psl[:, C:2 * C], AF.Sigmoid, bias=rv[:, 4, p:p + 1])
                fw.ts(lw.all(), lw.all(), float(-np.exp(-0.5)), ALU.mult)
                kkr, sqb, kk = fw.next("r_kkr"), fw.next("r_sqb"), fw.next("r_kk")
                fw.ts(kkr.all(), ks.all(), rv[:, 5, p:p + 1], ALU.mult)
                fw.tt(sqb.all(), kkr.all(), kkr.all(), ALU.mult, eng="pool")
                t3, km, at = fw.next("r_t3"), fw.next("r_km"), fw.next("r_at")
                fw.ts(t3.all(), a_.all(), rv[:, 6, p:p + 1], ALU.mult, omka[:, p:p + 1], ALU.add)
                fw.tt(km.all(), ks.all(), t3.all(), ALU.mult)
                sqb2 = fw.next("r_sqb")
                fw.stt(sqb2.all(), rs.all(), rv[:, 7, p:p + 1], km.all(), ALU.mult, ALU.mult)
                pss = fw.next("ps")
                fw.matmul(pss[:, 0:C], blkb.all(), sqb.all())
                fw.matmul(pss[:, C:2 * C], blkb.all(), sqb2.all())
                rn = fw.next("r_tmp")
                fw.act(rn.all(), pss[:, 0:C], AF.Ln, bias=eps6[:, 0:1])
                fw.act(rn.all(), rn.all(), AF.Exp, scale=-0.5)
                fw.tt(kk.all(), kkr.all(), rn.all(), ALU.mult)
                fw.tt(BON[:, p, :], pss[:, C:2 * C], vs.all(), ALU.mult)
                fw.tt(at.all(), a_.all(), kk.all(), ALU.mult, eng="pool")
                csum, csp = fw.next("r_cs"), fw.next("r_csp")
                fw.scan(csum.all(), onec[:, 0:1].bc([128, C]), lw.all(), 0.0)
                fw.tt(csp.all(), csum.all(), lw.all(), ALU.subtract, eng="pool")
                col = fw.next("r_col")
                fw.ts(col[:, 0:1], csum[:, MID:MID + 1], -1.0, ALU.mult)
                Wt, Wi, Wp = fw.next("r_Wt"), fw.next("r_Wi"), fw.next("r_Wp")
                fw.act(Wt.all(), csum.all(), AF.Exp, bias=col[:, 0:1])
                fw.act(Wi.all(), csum.all(), AF.Exp, scale=-1.0, bias=csum[:, MID:MID + 1])
                fw.act(Wp.all(), csp.all(), AF.Exp, bias=col[:, 0:1])
                fw.op("dve", "reciprocal", [Wp[:, 0:1]], [col[:, 1:2]], col[:, 1:2], Wp[:, 0:1])
                rh, kh, ah, kkh, vb = fw.next("r_rh"), fw.next("r_kh"), fw.next("r_ah"), fw.next("r_kkh"), fw.next("r_vb")
                fw.tt(rh.all(), rs.all(), Wt.all(), ALU.mult)
                fw.tt(kh.all(), km.all(), Wi.all(), ALU.mult)
                fw.tt(ah.all(), at.all(), Wi.all(), ALU.mult, eng="pool")
                fw.tt(kkh.all(), kk.all(), Wp.all(), ALU.mult, eng="pool")
                fw.copy(vb.all(), vs.all(), eng="pool")
                pst_ = fw.next("ps")
                pst = pst_.all().x(lambda a: a.bitcast(BF16))
                fw.transpose(pst.x(lambda a: a[:, 0:128]), kh.all(), self.identb.all())
                fw.transpose(pst.x(lambda a: a[:, 128:256]), ah.all(), self.identb.all())
                fw.transpose(pst.x(lambda a: a[:, 256:384]), vb.all(), self.identb.all())
                khT, ahT, vT = fw.next("r_khT"), fw.next("r_ahT"), fw.next("r_vT")
                fw.copy(khT.all(), pst.x(lambda a: a[:, 0:128]), eng="act")
                fw.copy(ahT.all(), pst.x(lambda a: a[:, 128:256]), eng="act")
                fw.copy(vT.all(), pst.x(lambda a: a[:, 256:384]), eng="act")
                for hh in range(2):
                    rows = slice(hh * 64, hh * 64 + 64)
                    hidx = p * 2 + hh
                    pa = fw.next("ps")
                    fw.matmul(pa[:, 0:128], ah[rows, :], kkh[rows, :])
                    fw.matmul(pa[:, 128:256], kkh[rows, :], ah[rows, :])
                    fw.matmul(pa[:, 256:384], kh[rows, :], kkh[rows, :])
                    BT = fw.next("r_BT")
                    fw.tt(ATr[:, hidx, :], pa[:, 0:128], m01["su"].all(), ALU.mult)
                    fw.tt(Ar[:, hidx, :], pa[:, 128:256], m01["sl"].all(), ALU.mult)
                    fw.tt(BT.all(), pa[:, 256:384], m01["su"].all(), ALU.mult)
                    pb = fw.next("ps")
                    fw.matmul(pb[:, 0:128], kh[rows, :], rh[rows, :])
                    fw.matmul(pb[:, 128:256], ah[rows, :], rh[rows, :])
                    M1, M2 = fw.next("r_M1"), fw.next("r_M2")
                    fw.tt(M1.all(), pb[:, 0:128], m01["iu"].all(), ALU.mult)
                    fw.tt(M2.all(), pb[:, 128:256], m01["iu"].all(), ALU.mult)
                    perh.append((BT, M1, M2))
                perp.append((rh, kh, kkh, vT, khT, ahT, col, Wt))
            Xr, dXr = self.neumann_batch(ATr, Ar, 16, nbuf_r)
            for p in range(8):
                rh, kh, kkh, vT, khT, ahT, col, Wt = perp[p]
                Sh = fw.next("r_Sh")
                fw.ts(Sh.all(), S32[:, p, :], col[:, 1:2], ALU.mult)
                psY = fw.next("psacc")
                psS = fw.next("psacc")
                for hh in range(2):
                    rows = slice(hh * 64, hh * 64 + 64)
                    hidx = p * 2 + hh
                    BT, M1, M2 = perh[hidx]
                    pr = fw.next("ps")
                    fw.matmul(pr[:, 0:64], kkh[rows, :], Sh[rows, :], start=True, stop=False)
                    fw.matmul(pr[:, 0:64], BT.all(), vT[:, rows], start=False, stop=True)
                    rhs = fw.next("r_rhs")
                    fw.copy(rhs.all(), pr[:, 0:64], eng="act")
                    fw.matmul(pr[:, 64:128], Xr[:, hidx, :], rhs.all(), start=True, stop=False)
                    fw.matmul(pr[:, 64:128], dXr[:, hidx, :], rhs.all(), start=False, stop=True)
                    nP = fw.next("r_nP")
                    fw.ts(nP.all(), pr[:, 64:128], -1.0, ALU.mult)
                    fw.matmul(psY[rows, 0:C], Sh[rows, :], rh[rows, :], start=True, stop=False)
                    fw.matmul(psY[rows, 0:C], vT[:, rows], M1.all(), start=False, stop=False)
                    fw.matmul(psY[rows, 0:C], nP.all(), M2.all(), start=False, stop=True)
                    fw.matmul(psS[rows, 0:64], khT[:, rows], vT[:, rows], start=True, stop=False)
                    fw.matmul(psS[rows, 0:64], ahT[:, rows], nP.all(), start=False, stop=True)
                fw.copy(YB[:, p, :], psY[:, 0:C], eng="act")
                st = fw.next("r_st")
                fw.stt(st.all(), S32[:, p, :], col[:, 1:2], psS[:, 0:64], ALU.mult, ALU.add)
                fw.ts(S32[:, p, :], st.all(), Wt[:, C - 1:C], ALU.mult)
            for hf in range(2):
                fl = lambda v: v.x(lambda a: a.rearrange("p h c -> p (h c)"))
                yv = fl(YB[:, hf * 4:(hf + 1) * 4, :])
                pm = fw.next("ps")
                fw.matmul(pm.all(), blkf.all(), yv)
                yc = fw.next("r_e512")
                fw.stt(yc.all(), pm.all(), -1.0 / 64, yv, ALU.mult, ALU.add)
                sq = fw.next("r_e512")
                fw.tt(sq.all(), yc.all(), yc.all(), ALU.mult, eng="pool")
                pv = fw.next("ps")
                fw.matmul(pv.all(), blkf.all(), sq.all())
                rs_ = fw.next("r_e512")
                fw.ts(rs_.all(), pv.all(), 1.0 / 64, ALU.mult)
                fw.act(rs_.all(), rs_.all(), AF.Ln, bias=epsl[:, 0:1])
                fw.act(rs_.all(), rs_.all(), AF.Exp, scale=-0.5)
                fw.tt(yc.all(), yc.all(), rs_.all(), ALU.mult)
                for q in range(4):
                    p = hf * 4 + q
                    fw.ts(YB[:, p, :], yc[:, q * C:(q + 1) * C], rv[:, 8, p:p + 1], ALU.mult, rv[:, 9, p:p + 1], ALU.add)
                fw.tt(yv, yv, fl(BON[:, hf * 4:(hf + 1) * 4, :]), ALU.add)
                fw.tt(fl(OB[:, hf * 4:(hf + 1) * 4, :]), yv, fl(ZT[:, hf * 4:(hf + 1) * 4, :]), ALU.mult)
            fw.dma("sp", self.oT[1024:2048, cs_].x(lambda a: a.rearrange("(h p) t -> p h t", p=128)), OB.all())
        fw.pop()


    def phase_C(self, li, xsrc, xdst):
        fw, L, NL = self.fw, self.L, self.NL
        fw.push()
        gb = fw.sbuf("c_gb", [128, 48], F32)
        fw.dma("sp", gb.all(), self.gate_b[:, li * 48:(li + 1) * 48])
        oall = fw.sbuf("c_o", [128, 24, 512], BF16)
        mg = fw.sbuf("c_mg", [128, 16, 512], BF16)
        fw.ring("c_wb", 3, [128, 1024], BF16)
        fw.ring("c_wo", 2, [128, 2048], BF16)
        fw.ring("c_gl", 3, [128, 512], F32)
        fw.ring("c_acc", 2, [128, 512], F32)
        fw.ring("c_tmp", 2, [128, 512], F32)
        fw.ring("c_x", 3, [128, 512], F32)
        for tb in range(self.NB):
            ts_ = slice(tb * 512, (tb + 1) * 512)
            fw.dma("sp", oall.all(), self.oT[:, ts_].x(lambda a: a.rearrange("(j p) t -> p j t", p=128)))
            for dt in range(16):
                acc = fw.next("c_acc")
                for n in range(3):
                    wb = fw.next("c_wb")
                    r0 = ((li * 3 + n) * 16 + dt) * 128
                    fw.dma("pool", wb.all(), self.w_br[r0:r0 + 128, :])
                    gl = fw.next("c_gl")
                    g0 = (CT_GATE + n * 16 + dt) * 128
                    fw.dma("sp", gl.all(), self.projT[g0:g0 + 128, ts_])
                    fw.act(gl.all(), gl.all(), AF.Sigmoid, bias=gb[:, n * 16 + dt:n * 16 + dt + 1])
                    ps = fw.next("ps")
                    for kt in range(8):
                        fw.matmul(ps.all(), wb[:, kt * 128:(kt + 1) * 128], oall[:, n * 8 + kt, :],
                                  start=(kt == 0), stop=(kt == 7))
                    if n == 0:
                        fw.tt(acc.all(), ps.all(), gl.all(), ALU.mult)
                    elif n == 1:
                        tmp = fw.next("c_tmp")
                        fw.tt(tmp.all(), ps.all(), gl.all(), ALU.mult)
                        fw.tt(acc.all(), acc.all(), tmp.all(), ALU.add, eng="pool")
                    else:
                        tmp = fw.next("c_tmp")
                        fw.tt(tmp.all(), ps.all(), gl.all(), ALU.mult)
                        fw.tt(mg[:, dt, :], acc.all(), tmp.all(), ALU.add, eng="pool")
            for dt in range(16):
                wo = fw.next("c_wo")
                r0 = (li * 16 + dt) * 128
                fw.dma("pool", wo.all(), self.w_out[r0:r0 + 128, :])
                xt = fw.next("c_x")
                fw.dma("sp", xt.all(), xsrc[dt * 128:(dt + 1) * 128, ts_])
                ps = fw.next("ps")
                for kt in range(16):
                    fw.matmul(ps.all(), wo[:, kt * 128:(kt + 1) * 128], mg[:, kt, :], start=(kt == 0), stop=(kt == 15))
                fw.tt(xt.all(), xt.all(), ps.all(), ALU.add)
                fw.dma("sp", xdst[dt * 128:(dt + 1) * 128, ts_], xt.all())
        fw.pop()

    def phase_final(self, xsrc):
        fw, L = self.fw, self.L
        fw.push()
        fnw = fw.sbuf("f_w", [128, 16], F32)
        fw.dma("sp", fnw.all(), self.fnorm_w.all())
        rstd = fw.sbuf("f_rstd", [128, 512], F32)
        fw.ring("f_xt", 3, [128, 512], F32)
        fw.ring("f_sq", 2, [128, 512], F32)
        for tb in range(self.NB):
            ts_ = slice(tb * 512, (tb + 1) * 512)
            ps = fw.next("ps")
            for kt in range(KT):
                xt = fw.next("f_xt")
                fw.dma("sp", xt.all(), xsrc[kt * 128:(kt + 1) * 128, ts_])
                sq = fw.next("f_sq")
                fw.act(sq.all(), xt.all(), AF.Square)
                fw.matmul(ps.all(), self.ones.all(), sq.all(), start=(kt == 0), stop=(kt == KT - 1))
            tmp = fw.next("f_sq")
            fw.ts(tmp.all(), ps.all(), 1.0 / D, ALU.mult, NORM_EPS, ALU.add)
            fw.act(tmp.all(), tmp.all(), AF.Ln)
            fw.act(rstd.all(), tmp.all(), AF.Exp, scale=-0.5)
            for kt in range(KT):
                xt = fw.next("f_xt")
                fw.dma("sp", xt.all(), xsrc[kt * 128:(kt + 1) * 128, ts_])
                fw.stt(xt.all(), xt.all(), fnw[:, kt:kt + 1], rstd.all(), ALU.mult, ALU.mult)
                fw.dma("sp", self.out[kt * 128:(kt + 1) * 128, ts_], xt.all())
        fw.pop()


def host_inputs(inputs, b, L, NL):
    f = np.float32
    m = {}
    m["xT"] = np.ascontiguousarray(inputs["x"][b, :L].T)
    m["w_in"] = np.concatenate([relayout_w_in(inputs["w_in"][i]) for i in range(NL)], axis=0)
    m["norm_w"] = np.ascontiguousarray(inputs["norm_w"][:NL].reshape(NL * KT, 128).T)
    m.update(make_consts())
    pg = np.zeros((128, NL, 3, 64), f)
    for i in range(NL):
        for h in range(2):
            pg[h * 64:(h + 1) * 64, i, 0] = inputs["s5_a_re"][i].T
            pg[h * 64:(h + 1) * 64, i, 1] = inputs["s5_a_im"][i].T
            pg[h * 64:(h + 1) * 64, i, 2] = inputs["s5_log_dt"][i][None, :]
    m["s5_pg"] = pg.reshape(128, -1)
    sb = np.zeros((64, NL, 2, 64, 16), f)
    sc = np.zeros((128, NL, 2, 64, 16), f)
    for i in range(NL):
        sb[:, i, 0] = inputs["s5_b_re"][i].transpose(1, 0, 2)
        sb[:, i, 1] = inputs["s5_b_im"][i].transpose(1, 0, 2)
        for h in range(2):
            sc[h * 64:(h + 1) * 64, i, 0] = inputs["s5_c_re"][i].transpose(2, 0, 1)
            sc[h * 64:(h + 1) * 64, i, 1] = inputs["s5_c_im"][i].transpose(2, 0, 1)
    m["s5_b"] = sb.reshape(64, -1)
    m["s5_c"] = sc.reshape(128, -1)
    vec = np.zeros((128, NL, 2, 8), f)
    for i in range(NL):
        vec[:, i, 0] = inputs["s5_d"][i].reshape(8, 128).T
        vec[:, i, 1] = inputs["s5_glu_b"][i].reshape(8, 128).T
    m["s5_vec"] = vec.reshape(128, -1)
    gp_ = np.zeros((64, NL, 2), f)
    gv = np.zeros((128, NL, 5, 24), f)
    for i in range(NL):
        gp_[32:40, i, 0] = inputs["gdn_dt_bias"][i]
        gp_[32:40, i, 1] = inputs["gdn_a_log"][i]
        gv[:, i, 0:4, :] = inputs["gdn_conv_w"][i].reshape(4, 24, 128).transpose(2, 0, 1)
        gv[:, i, 4, 0] = inputs["gdn_norm_w"][i]
    m["gdn_par"] = gp_.reshape(64, -1)
    m["gdn_vec"] = gv.reshape(128, -1)
    rv = np.zeros((128, NL, 10, 8), f)
    mul = np.zeros((128, NL, 2), f)
    for i in range(NL):
        mu = inputs["rwkv_mu"][i]
        t8 = lambda a: a.reshape(8, 128).T
        rv[:, i, 0] = t8(mu[0:1024]); rv[:, i, 1] = t8(mu[1024:2048]); rv[:, i, 2] = t8(mu[2048:3072])
        mul[0:96, i, 0] = mu[3072:3168]; mul[0:96, i, 1] = mu[3168:3264]
        rv[:, i, 3] = t8(inputs["rwkv_w0"][i]); rv[:, i, 4] = t8(inputs["rwkv_a0"][i])
        rv[:, i, 5] = t8(inputs["rwkv_k_k"][i]); rv[:, i, 6] = t8(inputs["rwkv_k_a"][i])
        rv[:, i, 7] = t8(inputs["rwkv_r_k"][i].reshape(1024))
        rv[:, i, 8] = t8(inputs["rwkv_lnx_w"][i]); rv[:, i, 9] = t8(inputs["rwkv_lnx_b"][i])
    m["rw_vec"] = rv.reshape(128, -1)
    m["rw_mul"] = mul.reshape(128, -1)
    m["rw_up"] = np.ascontiguousarray(np.concatenate(
        [np.concatenate([inputs["rwkv_w_up"][i], inputs["rwkv_a_up"][i]], axis=1) for i in range(NL)], axis=0))
    m["gate_b"] = np.ascontiguousarray(inputs["gate_b"][:NL].reshape(NL, 3, 16, 128).transpose(3, 0, 1, 2).reshape(128, -1))
    m["w_br"] = np.ascontiguousarray(
        inputs["w_branch"][:NL].reshape(NL, 3, 8, 128, 16, 128).transpose(0, 1, 4, 3, 2, 5).reshape(NL * 3 * 16 * 128, 1024))
    m["w_out"] = np.ascontiguousarray(
        inputs["w_out"][:NL].reshape(NL, 16, 128, 16, 128).transpose(0, 3, 2, 1, 4).reshape(NL * 16 * 128, 2048))
    m["fnorm_w"] = np.ascontiguousarray(inputs["final_norm_w"].reshape(16, 128).T)
    m["s5_glu"] = np.concatenate(
        [inputs["s5_glu_w"][i].reshape(8, 128, 1024).transpose(1, 0, 2).reshape(128, 8 * 1024) for i in range(NL)], axis=0)
    return m


def build(L, NL, debug=False, phases=("A", "S5", "GDN", "RWKV", "C", "F")):
    p = Prog(L, NL, debug)
    p.setup()
    xsrc = p.xT_in
    for li in range(NL):
        xdst = p.xT[li % 2]
        if "A" in phases:
            p.phase_A(li, xsrc)
        if "S5" in phases:
            p.phase_S5(li)
        if "GDN" in phases:
            p.phase_GDN(li)
        if "RWKV" in phases:
            p.phase_RWKV(li)
        if "C" in phases:
            p.phase_C(li, xsrc, xdst)
            xsrc = xdst
    if "F" in phases:
        p.phase_final(xsrc)
    stats = p.fw.emit()
    print("ops per engine:", stats)
    return p


_CACHE = {}


def kernel(**inputs):
    L, NL, B = 2048, 4, 4
    inputs = {k: np.asarray(v) for k, v in inputs.items()}
    if "prog" not in _CACHE:
        _CACHE["prog"] = build(L, NL)
    p = _CACHE["prog"]
    m0 = host_inputs(inputs, 0, L, NL)
    in_maps = [m0]
    for b in range(1, B):
        mb = dict(m0)
        mb["xT"] = np.ascontiguousarray(inputs["x"][b, :L].T)
        in_maps.append(mb)
    res = run_bass_kernel_spmd(p.nc, in_maps, core_ids=list(range(B)))
    out = np.stack([np.ascontiguousarray(res.results[b]["out"].T) for b in range(B)], axis=0)
    return out.astype(np.float32)
```

```python
import numpy as np
import concourse.bass as bass
import concourse.mybir as mybir
from concourse.bass_utils import run_bass_kernel_spmd
from contextlib import ExitStack

F32 = mybir.dt.float32
BF16 = mybir.dt.bfloat16
I32 = mybir.dt.int32
AF = mybir.ActivationFunctionType
ALU = mybir.AluOpType
AX = mybir.AxisListType

ENGS = ("pe", "act", "dve", "pool", "sp")

D = 2048
KT = 16
NCT = 131
NORM_EPS = 1e-6


class T:
    def __init__(self, name, shape, dtype, space, handle):
        self.name, self.shape, self.dtype, self.space = name, tuple(shape), dtype, space
        self.h = handle
        self.hist = []

    def __getitem__(self, key):
        if not isinstance(key, tuple):
            key = (key,)
        key = key + (slice(None),) * (len(self.shape) - len(key))
        rng = []
        for k, n in zip(key, self.shape):
            if isinstance(k, slice):
                a = 0 if k.start is None else k.start
                b = n if k.stop is None else k.stop
                assert k.step in (None, 1)
            else:
                a, b = k, k + 1
            assert 0 <= a < b <= n, (self.name, key, self.shape)
            rng.append((a, b))
        return V(self, tuple(rng), key)

    def all(self):
        return self[tuple(slice(None) for _ in self.shape)]


class V:
    def __init__(self, t, rng, key, xf=None):
        self.t, self.rng, self.key, self.xf = t, rng, key, xf

    @property
    def ap(self):
        a = self.t.h[self.key]
        if self.xf is not None:
            a = self.xf(a)
        return a

    def x(self, fn):
        old = self.xf
        if old is None:
            return V(self.t, self.rng, self.key, fn)
        return V(self.t, self.rng, self.key, lambda a: fn(old(a)))

    def bc(self, shape):
        return self.x(lambda a: a.to_broadcast(list(shape)))

    @property
    def shape(self):
        return tuple(b - a for a, b in self.rng)


def _overlap(r1, r2):
    for (a, b), (c, d) in zip(r1, r2):
        if b <= c or d <= a:
            return False
    return True


def _contains(r1, r2):
    for (a, b), (c, d) in zip(r1, r2):
        if c < a or d > b:
            return False
    return True


class Op:
    __slots__ = ("eng", "fn", "is_dma", "deps", "needed", "count", "dslot", "dval", "idx", "tag")


class FW:
    NDMA_SEMS = 32

    def __init__(self, nc):
        self.nc = nc
        self.ops = []
        self.stack = ExitStack()
        self.tensors = {}
        self.rings = {}
        self.cursor = 16512
        self.cur_stack = []
        self.uid = 0
        self.dma_i = 0
        self.dma_last = [None] * self.NDMA_SEMS
        self.junk = {e: self.sbuf("junk_" + e, [128, 16], F32) for e in ("act", "dve", "pool")}

    SBUF_LIMIT = 229300

    def sbuf(self, name, shape, dtype=F32):
        esz = {F32: 4, BF16: 2, I32: 4}[dtype]
        nbytes = int(np.prod(shape[1:])) * esz
        nbytes = (nbytes + 63) // 64 * 64
        off = self.cursor
        assert off + nbytes <= self.SBUF_LIMIT, ("SBUF overflow", name, off, nbytes)
        self.cursor = off + nbytes
        self.uid += 1
        h = self.nc.alloc_sbuf_tensor_at("%s_u%d" % (name, self.uid), list(shape), dtype, offset=off)
        t = T(name, shape, dtype, "sbuf", h)
        self.tensors[name] = t
        return t

    def push(self):
        self.cur_stack.append(self.cursor)

    def pop(self):
        self.barrier()
        self.cursor = self.cur_stack.pop()

    def barrier(self):
        ms = []
        for e in ("act", "dve", "pool"):
            j = self.junk[e]
            if e == "act":
                ms.append(self.op(e, "activation", [], [j[:, 0:8]], j[:, 0:8], j[:, 8:16], AF.Copy).idx)
            else:
                ms.append(self.op(e, "memset", [], [j.all()], j.all(), 0.0).idx)
        dl = [o.idx for o in self.dma_last if o is not None]
        for e in ENGS:
            op = Op()
            op.eng, op.fn, op.is_dma, op.tag = e, None, False, "bar"
            op.idx = len(self.ops)
            op.needed = False
            op.count = None
            op.deps = list(ms) + list(dl)
            self.ops.append(op)
        for t in self.tensors.values():
            t.hist = []

    def psum(self, name, shape, dtype=F32):
        h = self.stack.enter_context(self.nc.psum_tensor(name, list(shape), dtype))
        t = T(name, shape, dtype, "psum", h)
        self.tensors[name] = t
        return t

    def dram(self, name, shape, dtype=F32, kind="Internal"):
        h = self.nc.dram_tensor(name, list(shape), dtype, kind=kind).ap()
        t = T(name, shape, dtype, "dram", h)
        self.tensors[name] = t
        return t

    def ring(self, name, n, shape, dtype=F32, space="sbuf"):
        mk = self.sbuf if space == "sbuf" else self.psum
        self.rings[name] = [[mk("%s_%d" % (name, i), shape, dtype) for i in range(n)], 0]

    def next(self, name):
        r = self.rings[name]
        t = r[0][r[1] % len(r[0])]
        r[1] += 1
        return t

    def _record(self, eng, fn, reads, writes, is_dma=False, tag=""):
        op = Op()
        op.eng, op.fn, op.is_dma, op.tag = eng, fn, is_dma, tag
        op.idx = len(self.ops)
        op.needed = False
        op.count = None
        deps = set()

        def reg(v):
            return tuple((0, n) for n in v.t.shape) if v.t.space == "psum" else v.rng
        for v in reads:
            ps_ = v.t.space == "psum"
            for (r, oi, w, e) in v.t.hist:
                if (w or (ps_ and e != eng)) and _overlap(r, reg(v)):
                    deps.add(oi)
        for v in writes:
            for (r, oi, w, e) in v.t.hist:
                if _overlap(r, reg(v)):
                    deps.add(oi)
        for v in writes:
            t = v.t
            t.hist = [h for h in t.hist if not _contains(reg(v), h[0])]
            t.hist.append((reg(v), op.idx, True, eng))
        for v in reads:
            t = v.t
            if not is_dma:
                t.hist = [h for h in t.hist
                          if not ((not h[2]) and h[3] == eng and h[0] == reg(v) and h[1] != op.idx
                               and not self.ops[h[1]].is_dma)]
            t.hist.append((reg(v), op.idx, False, eng))
        deps.discard(op.idx)
        op.deps = [d for d in deps if not (eng == "pe" and self.ops[d].eng == "pe" and not self.ops[d].is_dma
                                           and not is_dma)]
        if is_dma:
            op.dslot = self.dma_i % self.NDMA_SEMS
            op.dval = 16 * (self.dma_i // self.NDMA_SEMS + 1)
            prev = self.dma_last[op.dslot]
            if prev is not None:
                op.deps.append(prev.idx)
            self.dma_last[op.dslot] = op
            self.dma_i += 1
        self.ops.append(op)
        return op

    def op(self, eng, method, reads, writes, *args, **kw):
        def conv(a):
            return a.ap if isinstance(a, V) else a

        def fn(e):
            return getattr(e, method)(*[conv(a) for a in args], **{k: conv(v) for k, v in kw.items()})
        return self._record(eng, fn, reads, writes, tag=method)

    def dma(self, eng, out, in_, **kw):
        def fn(e):
            return e.dma_start(out=out.ap, in_=in_.ap, **kw)
        return self._record(eng, fn, [in_], [out], is_dma=True, tag="dma")

    def matmul(self, out, lhsT, rhs, start=True, stop=True, **kw):
        rd = [lhsT, rhs] + ([] if start else [out])
        return self.op("pe", "matmul", rd, [out], out, lhsT, rhs, start=start, stop=stop, **kw)

    def transpose(self, out, in_, ident):
        return self.op("pe", "transpose", [in_, ident], [out], out, in_, ident)

    def act(self, out, in_, func, bias=None, scale=None, extra_reads=()):
        kw = {}
        rd = [in_] + list(extra_reads)
        if bias is not None:
            kw["bias"] = bias
            if isinstance(bias, V):
                rd.append(bias)
        if scale is not None:
            kw["scale"] = scale
            if isinstance(scale, V):
                rd.append(scale)
        return self.op("act", "activation", rd, [out], out, in_, func, **kw)

    def tt(self, out, in0, in1, op, eng="dve"):
        return self.op(eng, "tensor_tensor", [in0, in1], [out], out, in0, in1, op)

    def ts(self, out, in0, s1, op0, s2=None, op1=None, eng="dve"):
        rd = [in0] + [s for s in (s1, s2) if isinstance(s, V)]
        if op1 is None:
            return self.op(eng, "tensor_scalar", rd, [out], out, in0, s1, None, op0)
        return self.op(eng, "tensor_scalar", rd, [out], out, in0, s1, s2, op0, op1)

    def stt(self, out, in0, scalar, in1, op0, op1, eng="dve"):
        rd = [in0, in1] + ([scalar] if isinstance(scalar, V) else [])
        return self.op(eng, "scalar_tensor_tensor", rd, [out], out, in0, scalar, in1, op0, op1)

    def copy(self, out, in_, eng="dve"):
        if eng == "act":
            return self.op("act", "activation", [in_], [out], out, in_, AF.Copy)
        return self.op(eng, "tensor_copy", [in_], [out], out, in_)

    def memset(self, out, val, eng="dve"):
        return self.op(eng, "memset", [], [out], out, val)

    def scan(self, out, d0, d1, init, op0=ALU.mult, op1=ALU.add):
        rd = [d0, d1] + ([init] if isinstance(init, V) else [])
        return self.op("dve", "tensor_tensor_scan", rd, [out], out, d0, d1, init, op0, op1)

    def emit(self):
        nc = self.nc
        ops = self.ops
        for o in ops:
            for d in o.deps:
                ops[d].needed = True
        cnt = {e: 0 for e in ENGS}
        for o in ops:
            if o.is_dma:
                pass
            elif o.needed:
                cnt[o.eng] += 1
                o.count = cnt[o.eng]
        st = self.stack
        sem_e = {e: st.enter_context(nc.semaphore("sem_" + e)) for e in ENGS if e != "sp"}
        sem_d = [st.enter_context(nc.semaphore("semd%d" % i)) for i in range(self.NDMA_SEMS)]
        per_eng = {e: [o for o in ops if o.eng == e] for e in ENGS}
        final = {}
        for o in ops:
            if o.is_dma:
                final[o.dslot] = o.dval

        def run_engine(ename, e):
            waited = {}

            def wait(key, sem, val):
                if waited.get(key, 0) >= val:
                    return
                e.wait_ge(sem, val)
                waited[key] = val

            for o in per_eng[ename]:
                for d in sorted(o.deps):
                    p = ops[d]
                    if p.is_dma:
                        wait(("d", p.dslot), sem_d[p.dslot], p.dval)
                    else:
                        wait(("e", p.eng), sem_e[p.eng], p.count)
                if o.fn is None:
                    continue
                ins = o.fn(e)
                if o.is_dma:
                    ins.then_inc(sem_d[o.dslot], 16)
                elif o.needed:
                    ins.then_inc(sem_e[o.eng], 1)
            if ename == "sp":
                for slot, val in sorted(final.items()):
                    wait(("d", slot), sem_d[slot], val)

        with nc.Block() as block:
            @block.tensor
            def _(e):
                run_engine("pe", e)

            @block.scalar
            def _(e):
                run_engine("act", e)

            @block.vector
            def _(e):
                run_engine("dve", e)

            @block.gpsimd
            def _(e):
                run_engine("pool", e)

            @block.sync
            def _(e):
                run_engine("sp", e)
        st.close()
        return {e: len(per_eng[e]) for e in ENGS}


def col_tiles():
    tiles = []
    c = 0
    for _ in range(24 + 8):
        tiles.append((c, 128)); c += 128
    tiles.append((c, 16)); c += 16
    for _ in range(24):
        tiles.append((c, 128)); c += 128
    tiles.append((c, 96)); c += 96
    tiles.append((c, 96)); c += 96
    for _ in range(8 + 8 + 8 + 48):
        tiles.append((c, 128)); c += 128
    assert c == 16592 and len(tiles) == NCT
    return tiles


CT_GQKV, CT_GZ, CT_GBA, CT_RR, CT_RK, CT_RV, CT_RWL, CT_RAL, CT_RZ, CT_SU, CT_SZ, CT_GATE = \
    0, 24, 32, 33, 41, 49, 57, 58, 59, 67, 75, 83


def relayout_w_in(w):
    out = np.zeros((NCT, 128, KT, 128), np.float32)
    for j, (c0, n) in enumerate(col_tiles()):
        blk = w[:, c0:c0 + n].reshape(KT, 128, n)
        if j == CT_GBA:
            out[j, :, :, 0:8] = blk[:, :, 0:8].transpose(1, 0, 2)
            out[j, :, :, 32:40] = blk[:, :, 8:16].transpose(1, 0, 2)
            continue
        out[j, :, :, :n] = blk.transpose(1, 0, 2)
    return out.reshape(NCT * 128, KT * 128)


TWO_PI_1 = 6.28125
TWO_PI_2 = float(2 * np.pi - 6.28125)


def make_consts():
    c = {}
    c["c_ones"] = np.ones((128, 128), np.float32)
    c["c_ident"] = np.eye(128, dtype=np.float32)
    c["c_iota"] = np.tile(np.arange(1, 129, dtype=np.float32)[None, :], (128, 1))
    gm = np.zeros((128, 8), np.float32)
    for r in range(128):
        gm[r, r // 16] = 1.0
    c["c_gmask"] = gm
    sw = np.zeros((128, 128), np.float32)
    for m in range(128):
        sw[(m + 64) % 128, m] = 1.0
    c["c_swap"] = sw
    sg = np.ones((128, 1), np.float32)
    sg[:64] = -1.0
    c["c_sgn"] = sg
    NEG = -30000.0
    r_ = np.arange(128)[:, None]
    c_ = np.arange(128)[None, :]
    c["c_mask_su"] = np.where(r_ < c_, 0.0, NEG).astype(np.float32)
    c["c_mask_sl"] = np.where(r_ > c_, 0.0, NEG).astype(np.float32)
    c["c_mask_iu"] = np.where(r_ <= c_, 0.0, NEG).astype(np.float32)
    c["c_m01_su"] = (r_ < c_).astype(np.float32)
    c["c_m01_sl"] = (r_ > c_).astype(np.float32)
    c["c_m01_iu"] = (r_ <= c_).astype(np.float32)
    sel = np.zeros((64, 16, 128), np.float32)
    for h in range(8):
        sel[h, h, :] = 1.0
        sel[32 + h, h, :] = 1.0
        sel[32 + h, 8 + h, :] = 1.0
    c["c_sel"] = sel.reshape(64, 16 * 128)
    bo = np.zeros((128, 128), np.float32)
    bo[:64, :64] = 1.0
    bo[64:, 64:] = 1.0
    c["c_blk64"] = bo
    return c


class Prog:
    def __init__(self, L, NL, debug=False):
        self.L, self.NL, self.debug = L, NL, debug
        self.NB = L // 512
        nc = bass.Bass("TRN2", target_bir_lowering=False)
        self.nc = nc
        self.fw = FW(nc)
        self.evac_i = 0

    def scratch(self, name, shape, dtype=F32):
        return self.fw.dram(name, shape, dtype, kind=("ExternalOutput" if self.debug else "Internal"))

    def ext(self, n, s, dt=F32):
        return self.fw.dram(n, s, dt, kind="ExternalInput")

    def setup(self):
        fw, L, NL = self.fw, self.L, self.NL
        ext = self.ext
        self.xT_in = ext("xT", [D, L])
        self.w_in = ext("w_in", [NL * NCT * 128, KT * 128])
        self.norm_w = ext("norm_w", [128, NL * KT])
        self.cst = {k: ext(k, list(v.shape)) for k, v in make_consts().items()}
        self.s5_pg = ext("s5_pg", [128, NL * 3 * 64])
        self.s5_b = ext("s5_b", [64, NL * 2 * 64 * 16])
        self.s5_c = ext("s5_c", [128, NL * 2 * 64 * 16])
        self.s5_vec = ext("s5_vec", [128, NL * 2 * 8])
        self.s5_glu = ext("s5_glu", [NL * 128, 8 * 1024])
        self.gdn_par = ext("gdn_par", [64, NL * 2])
        self.gdn_vec = ext("gdn_vec", [128, NL * 5 * 24])
        self.rw_vec = ext("rw_vec", [128, NL * 80])
        self.rw_mul = ext("rw_mul", [128, NL * 2])
        self.rw_up = ext("rw_up", [NL * 96, 2048])
        self.gate_b = ext("gate_b", [128, NL * 48])
        self.w_br = ext("w_br", [NL * 48 * 128, 1024])
        self.w_out = ext("w_out", [NL * 16 * 128, 2048])
        self.fnorm_w = ext("fnorm_w", [128, 16])
        self.out = fw.dram("out", [D, L], F32, kind="ExternalOutput")
        self.xT = [self.scratch("xs%d" % i, [D, L]) for i in range(2)]
        self.projT = self.scratch("projT", [NCT * 128, L])
        self.oT = self.scratch("oT", [3 * 1024, L], BF16)
        self.ones = fw.sbuf("ones", [128, 128], F32)
        self.ident = fw.sbuf("ident", [128, 128], F32)
        self.normw = fw.sbuf("normw", [128, NL * KT], F32)
        self.identb = fw.sbuf("identb", [128, 128], BF16)
        self.onesb = fw.sbuf("onesb", [128, 128], BF16)
        fw.dma("pool", self.identb.all(), self.cst["c_ident"].all())
        fw.dma("pool", self.onesb.all(), self.cst["c_ones"].all())
        self.halfpi = fw.sbuf("halfpi", [128, 1], F32)
        fw.memset(self.halfpi.all(), float(np.pi / 2))
        fw.ring("ps", 6, [128, 512], F32, space="psum")
        fw.ring("psacc", 2, [128, 512], F32, space="psum")
        fw.dma("sp", self.ones.all(), self.cst["c_ones"].all())
        fw.dma("sp", self.ident.all(), self.cst["c_ident"].all())
        fw.dma("sp", self.normw.all(), self.norm_w.all())

    def evac(self, out, in_):
        self.evac_i += 1
        if self.evac_i % 2:
            self.fw.copy(out, in_, eng="act")
        else:
            self.fw.copy(out, in_, eng="dve")

    def phase_A(self, li, xsrc):
        fw, L = self.fw, self.L
        fw.push()
        hT = fw.sbuf("hT", [128, KT, L], BF16)
        rstd = fw.sbuf("rstd", [128, L], F32)
        fw.ring("xt", 3, [128, 512], F32)
        fw.ring("sq", 2, [128, 512], F32)
        fw.ring("wt", 3, [128, KT * 128], BF16)
        fw.ring("stage", 2, [128, L], F32)
        for tb in range(self.NB):
            ts_ = slice(tb * 512, (tb + 1) * 512)
            ps = fw.next("ps")
            for kt in range(KT):
                xt = fw.next("xt")
                fw.dma("sp", xt.all(), xsrc[kt * 128:(kt + 1) * 128, ts_])
                sq = fw.next("sq")
                fw.act(sq.all(), xt.all(), AF.Square)
                fw.matmul(ps.all(), self.ones.all(), sq.all(), start=(kt == 0), stop=(kt == KT - 1))
            tmp = fw.next("sq")
            fw.ts(tmp.all(), ps.all(), 1.0 / D, ALU.mult, NORM_EPS, ALU.add)
            fw.act(tmp.all(), tmp.all(), AF.Ln)
            fw.act(rstd[:, ts_], tmp.all(), AF.Exp, scale=-0.5)
        for tb in range(self.NB):
            ts_ = slice(tb * 512, (tb + 1) * 512)
            for kt in range(KT):
                xt = fw.next("xt")
                fw.dma("sp", xt.all(), xsrc[kt * 128:(kt + 1) * 128, ts_])
                fw.stt(hT[:, kt, ts_], xt.all(), self.normw[:, li * KT + kt:li * KT + kt + 1],
                       rstd[:, ts_], ALU.mult, ALU.mult)
        for j in range(NCT):
            wt = fw.next("wt")
            r0 = (li * NCT + j) * 128
            fw.dma("pool", wt.all(), self.w_in[r0:r0 + 128, :])
            st = fw.next("stage")
            for tb in range(self.NB):
                ts_ = slice(tb * 512, (tb + 1) * 512)
                ps = fw.next("ps")
                for kt in range(KT):
                    fw.matmul(ps.all(), wt[:, kt * 128:(kt + 1) * 128], hT[:, kt, ts_],
                              start=(kt == 0), stop=(kt == KT - 1))
                self.evac(st[:, ts_], ps.all())
            fw.dma("sp", self.projT[j * 128:(j + 1) * 128, :], st.all())
        fw.pop()

    def sincos(self, cos_o, sin_o, phi, shape, tagn):
        fw = self.fw
        ki = fw.sbuf("sc_ki_" + tagn, shape, I32)
        kf = fw.sbuf("sc_kf_" + tagn, shape, F32)
        red = fw.sbuf("sc_red_" + tagn, shape, F32)
        s2 = fw.sbuf("sc_s2_" + tagn, shape, F32)
        c2 = fw.sbuf("sc_c2_" + tagn, shape, F32)
        fw.ts(ki.all(), phi, float(1.0 / (2 * np.pi)), ALU.mult)
        fw.copy(kf.all(), ki.all())
        fw.stt(red.all(), kf.all(), -TWO_PI_1, phi, ALU.mult, ALU.add)
        fw.stt(red.all(), kf.all(), -TWO_PI_2, red.all(), ALU.mult, ALU.add)
        fw.act(s2.all(), red.all(), AF.Sin, scale=0.5)
        fw.act(c2.all(), red.all(), AF.Sin, scale=0.5, bias=self.halfpi[0:shape[0], 0:1])
        fw.stt(sin_o, s2.all(), 2.0, c2.all(), ALU.mult, ALU.mult)
        fw.tt(kf.all(), s2.all(), s2.all(), ALU.mult)
        fw.ts(cos_o, kf.all(), -2.0, ALU.mult, 1.0, ALU.add)

    def phase_S5(self, li):
        fw, L, NL = self.fw, self.L, self.NL
        TC = 128
        TB = 256
        fw.push()
        pg = fw.sbuf("s5pg", [128, 3, 64], F32)
        fw.dma("sp", pg.all().x(lambda a: a.rearrange("p a g -> p (a g)")), self.s5_pg[:, li * 192:(li + 1) * 192])
        are, aim, ldt = pg[:, 0, :], pg[:, 1, :], pg[:, 2, :]
        mk = lambda n: fw.sbuf(n, [128, 64], F32)
        dt_, ang, mag, cs, sn, abre, abim, den, cre, cim, t1, t2 = [mk("s5p%d" % i) for i in range(12)]
        fw.act(dt_.all(), ldt, AF.Exp)
        fw.tt(ang.all(), aim, dt_.all(), ALU.mult)
        fw.tt(t1.all(), are, dt_.all(), ALU.mult)
        fw.act(mag.all(), t1.all(), AF.Exp)
        fw.push()
        self.sincos(cs.all(), sn.all(), ang.all(), [128, 64], "p")
        fw.pop()
        fw.tt(abre.all(), mag.all(), cs.all(), ALU.mult)
        fw.tt(abim.all(), mag.all(), sn.all(), ALU.mult)
        fw.tt(den.all(), are, are, ALU.mult)
        fw.tt(t1.all(), aim, aim, ALU.mult)
        fw.tt(den.all(), den.all(), t1.all(), ALU.add)
        fw.op("dve", "reciprocal", [den.all()], [den.all()], den.all(), den.all())
        fw.ts(t1.all(), abre.all(), -1.0, ALU.add)
        fw.tt(cre.all(), t1.all(), are, ALU.mult)
        fw.tt(t2.all(), abim.all(), aim, ALU.mult)
        fw.tt(cre.all(), cre.all(), t2.all(), ALU.add)
        fw.tt(cre.all(), cre.all(), den.all(), ALU.mult)
        fw.tt(cim.all(), abim.all(), are, ALU.mult)
        fw.tt(t2.all(), t1.all(), aim, ALU.mult)
        fw.tt(cim.all(), cim.all(), t2.all(), ALU.subtract)
        fw.tt(cim.all(), cim.all(), den.all(), ALU.mult)
        import os
        stage = float(os.environ.get("S5_STAGE", "9"))
        if stage < 1:
            fw.pop(); return
        L1 = fw.sbuf("s5L1", [128, 32, 128], BF16)
        L2 = fw.sbuf("s5L2", [128, 32, 128], BF16)
        LY1 = fw.sbuf("s5LY1", [128, 8, 8, 64], BF16)
        LY2 = fw.sbuf("s5LY2", [128, 8, 8, 64], BF16)
        CTb = fw.sbuf("s5CT", [128, 64, TC], F32)
        STb = fw.sbuf("s5ST", [128, 64, TC], F32)
        gmask = fw.sbuf("s5gm", [128, 8], F32)
        swp = fw.sbuf("s5sw", [128, 128], F32)
        sgn = fw.sbuf("s5sg", [128, 1], F32)
        vec = fw.sbuf("s5vec", [128, 16], F32)
        gluW = fw.sbuf("s5glu", [128, 8, 1024], BF16)
        fw.dma("sp", gmask.all(), self.cst["c_gmask"].all())
        fw.dma("sp", swp.all(), self.cst["c_swap"].all())
        fw.dma("sp", sgn.all(), self.cst["c_sgn"].all())
        fw.dma("sp", vec.all(), self.s5_vec[:, li * 16:(li + 1) * 16])
        fw.dma("pool", gluW.all().x(lambda a: a.rearrange("p k m -> p (k m)")), self.s5_glu[li * 128:(li + 1) * 128, :])
        if stage < 2:
            fw.pop(); return
        fw.push()
        braw = fw.sbuf("s5braw", [64, 2, 64, 16], F32)
        fw.dma("sp", braw.all().x(lambda a: a.rearrange("p a g c -> p (a g c)")),
               self.s5_b[:, li * 2048:(li + 1) * 2048])
        bbre = fw.sbuf("s5bbre", [64, 64, 16], F32)
        bbim = fw.sbuf("s5bbim", [64, 64, 16], F32)
        tmpb = fw.sbuf("s5tmpb", [64, 64, 16], F32)
        bc3 = lambda v: v.x(lambda a: a.unsqueeze(2).to_broadcast([64, 64, 16]))
        fw.tt(bbre.all(), braw[:, 0, :, :], bc3(cre[0:64, :]), ALU.mult)
        fw.tt(tmpb.all(), braw[:, 1, :, :], bc3(cim[0:64, :]), ALU.mult)
        fw.tt(bbre.all(), bbre.all(), tmpb.all(), ALU.subtract)
        fw.tt(bbim.all(), braw[:, 1, :, :], bc3(cre[0:64, :]), ALU.mult)
        fw.tt(tmpb.all(), braw[:, 0, :, :], bc3(cim[0:64, :]), ALU.mult)
        fw.tt(bbim.all(), bbim.all(), tmpb.all(), ALU.add)
        cat1 = fw.sbuf("s5cat1", [128, 128], F32)
        cat2 = fw.sbuf("s5cat2", [128, 128], F32)
        for ct in range(8 if float(os.environ.get('S5_SUB','9')) > 0.5 else 0):
            ps = fw.next("ps")
            fl = lambda v: v.x(lambda a: a.rearrange("p g c -> p (g c)"))
            fw.matmul(ps[:, 0:64], fl(bbre[:, ct * 8:(ct + 1) * 8, :]), self.ident[0:64, 0:64])
            fw.matmul(ps[:, 64:128], fl(bbim[:, ct * 8:(ct + 1) * 8, :]), self.ident[0:64, 0:64])
            fw.copy(cat1.all(), ps[:, 0:128], eng=os.environ.get("CAT_ENG","act"))
            fw.copy(cat2[:, 0:64], ps[:, 64:128])
            fw.ts(cat2[:, 64:128], ps[:, 0:64], -1.0, ALU.mult)
            for gp in range(8 if float(os.environ.get('S5_SUB','9')) > 1.5 else 0):
                half, slot = gp // 4, ct * 4 + gp % 4
                rows = slice(half * 64, half * 64 + 64)
                fw.ts(L1[rows, slot, :], cat1[rows, :], gmask[rows, gp:gp + 1], ALU.mult)
                fw.ts(L2[rows, slot, :], cat2[rows, :], gmask[rows, gp:gp + 1], ALU.mult)
        fw.pop()
        if stage < 3:
            fw.pop(); return
        fw.push()
        craw = fw.sbuf("s5craw", [128, 2, 8, 8, 16], F32)
        fw.dma("sp", craw.all().x(lambda a: a.rearrange("p a t g c -> p (a t g c)")),
               self.s5_c[:, li * 2048:(li + 1) * 2048])
        fw.memset(LY1.all(), 0.0)
        fw.memset(LY2.all(), 0.0, eng="pool")
        for gp in range(8):
            cs_ = slice((gp % 4) * 16, (gp % 4) * 16 + 16)
            fw.copy(LY1[0:64, :, gp, cs_], craw[0:64, 0, :, gp, :])
            fw.ts(LY1[64:128, :, gp, cs_], craw[64:128, 1, :, gp, :], -1.0, ALU.mult)
            fw.ts(LY2[0:64, :, gp, cs_], craw[0:64, 1, :, gp, :], -1.0, ALU.mult)
            fw.ts(LY2[64:128, :, gp, cs_], craw[64:128, 0, :, gp, :], -1.0, ALU.mult)
        fw.pop()
        if stage < 4:
            fw.pop(); return
        fw.push()
        iota = fw.sbuf("s5iota", [128, TC], F32)
        fw.dma("sp", iota.all(), self.cst["c_iota"].all())
        phi = fw.sbuf("s5phi", [128, 8, TC], F32)
        for gb in range(8):
            fw.tt(phi.all(),
                  ang[:, gb * 8:(gb + 1) * 8].x(lambda a: a.unsqueeze(2).to_broadcast([128, 8, TC])),
                  iota.all().x(lambda a: a.unsqueeze(1).to_broadcast([128, 8, TC])), ALU.mult)
            fw.push()
            self.sincos(CTb[:, gb * 8:(gb + 1) * 8, :], STb[:, gb * 8:(gb + 1) * 8, :], phi.all(), [128, 8, TC], "t")
            fw.pop()
        fw.pop()
        ssgn = fw.sbuf("s5ssgn", [128, 64], F32)
        cend = fw.sbuf("s5cend", [128, 64], F32)
        fw.ts(ssgn.all(), STb[:, :, TC - 1], sgn[:, 0:1], ALU.mult)
        fw.copy(cend.all(), CTb[:, :, TC - 1])
        if stage < 5:
            fw.pop(); return
        if self.debug and li == 0:
            self.dbg_y = self.scratch("dbg_y", [1024, L])
            self.dbg_tab = self.scratch("dbg_tab", [128, 4 * TC])
            self.dbg_par = self.scratch("dbg_par", [128, 4 * 64])
            self.dbg_L = self.scratch("dbg_L", [128, 4 * 128], BF16)
            fw.dma("sp", self.dbg_tab[:, 0:TC], CTb[:, 0, :])
            fw.dma("sp", self.dbg_tab[:, TC:2 * TC], STb[:, 0, :])
            fw.dma("sp", self.dbg_tab[:, 2 * TC:3 * TC], CTb[:, 5, :])
            fw.dma("sp", self.dbg_tab[:, 3 * TC:4 * TC], STb[:, 5, :])
            fw.dma("sp", self.dbg_par[:, 0:64], mag.all())
            fw.dma("sp", self.dbg_par[:, 64:128], ang.all())
            fw.dma("sp", self.dbg_par[:, 128:192], cre.all())
            fw.dma("sp", self.dbg_par[:, 192:256], cim.all())
            fw.dma("sp", self.dbg_L[:, 0:128], L1[:, 0, :])
            fw.dma("sp", self.dbg_L[:, 128:256], L2[:, 0, :])
            fw.dma("sp", self.dbg_L[:, 256:320], LY1[:, 0, 0, :])
            fw.dma("sp", self.dbg_L[:, 320:384], LY2[:, 0, 0, :])
            fw.dma("sp", self.dbg_L[:, 384:512], L1[:, 5, :])
        sprev = fw.sbuf("s5sprev", [128, 64], F32)
        send = fw.sbuf("s5send", [128, 64], F32)
        fw.memset(sprev.all(), 0.0)
        uT = fw.sbuf("s5uT", [128, 8, TB], F32)
        ub = fw.sbuf("s5ub", [128, 8, TB], BF16)
        ysb = fw.sbuf("s5y", [128, 8, TB], F32)
        ygb = fw.sbuf("s5yg", [128, 8, TB], BF16)
        fw.ring("s5t1", 8, [128, 2, TC], F32)
        fw.ring("s5t2", 8, [128, 2, TC], F32)
        fw.ring("s5bt", 8, [128, 2, TC], F32)
        fw.ring("s5st", 8, [128, 2, TC], F32)
        fw.ring("s5z1", 8, [128, 2, TC], BF16)
        fw.ring("s5z2", 8, [128, 2, TC], BF16)
        fw.ring("s5zT", 2, [128, TB], F32)
        fw.ring("s5sg", 2, [128, TB], F32)
        fw.ring("s5oc", 2, [128, TB], BF16)
        for tb in range(L // TB):
            t0 = tb * TB
            for ct in range(8):
                r0 = (CT_SU + ct) * 128
                fw.dma("sp", uT[:, ct, :], self.projT[r0:r0 + 128, t0:t0 + TB])
                fw.copy(ub[:, ct, :], uT[:, ct, :], eng="pool")
            for cc in range(TB // TC):
                csl = slice(cc * TC, (cc + 1) * TC)
                for ct in range(8):
                    psy = fw.next("psacc")
                    units = []
                    for pr_ in range(4):
                        g0 = ct * 8 + pr_ * 2
                        psx = fw.next("ps")
                        for u_ in range(2):
                            gp = pr_ * 2 + u_
                            half, slot = gp // 4, ct * 4 + gp % 4
                            rows = slice(half * 64, half * 64 + 64)
                            fw.matmul(psx[:, u_ * TC:(u_ + 1) * TC], L1[rows, slot, :], ub[rows, ct, csl])
                            fw.matmul(psx[:, (2 + u_) * TC:(3 + u_) * TC], L2[rows, slot, :], ub[rows, ct, csl])
                        units.append((g0, psx, fw.next("s5t1"), fw.next("s5t2"), fw.next("s5bt"), fw.next("s5st"),
                                      fw.next("s5z1"), fw.next("s5z2")))
                    f2 = lambda v: v.x(lambda a: a.rearrange("p g c -> p (g c)"))
                    for (g0, psx, a1, a2, bt, st, z1, z2) in units:
                        fw.tt(f2(a1.all()), psx[:, 0:2 * TC], f2(CTb[:, g0:g0 + 2, :]), ALU.mult)
                    for (g0, psx, a1, a2, bt, st, z1, z2) in units:
                        fw.tt(f2(a2.all()), psx[:, 2 * TC:4 * TC], f2(STb[:, g0:g0 + 2, :]), ALU.mult)
                    for (g0, psx, a1, a2, bt, st, z1, z2) in units:
                        fw.tt(f2(bt.all()), f2(a1.all()), f2(a2.all()), ALU.add)
                    for u_ in range(2):
                        for (g0, psx, a1, a2, bt, st, z1, z2) in units:
                            g = g0 + u_
                            fw.scan(st[:, u_, :], mag[:, g:g + 1].bc([128, TC]), bt[:, u_, :], sprev[:, g:g + 1])
                    for (g0, psx, a1, a2, bt, st, z1, z2) in units:
                        fw.copy(send[:, g0:g0 + 2], st[:, :, TC - 1], eng="act")
                    for (g0, psx, a1, a2, bt, st, z1, z2) in units:
                        fw.tt(f2(z1.all()), f2(st.all()), f2(CTb[:, g0:g0 + 2, :]), ALU.mult)
                        fw.tt(f2(z2.all()), f2(st.all()), f2(STb[:, g0:g0 + 2, :]), ALU.mult, eng="pool")
                    for (g0, psx, a1, a2, bt, st, z1, z2) in units:
                        for u_ in range(2):
                            gp = (g0 - ct * 8) + u_
                            half = gp // 4
                            orow = slice(half * 64, half * 64 + 64)
                            first, last = (gp % 4 == 0), (gp % 4 == 3)
                            fw.matmul(psy[orow, 0:TC], LY1[:, ct, gp, :], z1[:, u_, :], start=first, stop=False)
                            fw.matmul(psy[orow, 0:TC], LY2[:, ct, gp, :], z2[:, u_, :], start=False, stop=last)
                    fw.stt(ysb[:, ct, csl], uT[:, ct, csl], vec[:, ct:ct + 1], psy[:, 0:TC], ALU.mult, ALU.add)
                pss = fw.next("ps")
                fw.matmul(pss[:, 0:64], swp.all(), send.all())
                fw.tt(sprev.all(), send.all(), cend.all(), ALU.mult)
                fw.tt(send.all(), pss[:, 0:64], ssgn.all(), ALU.mult)
                fw.tt(sprev.all(), sprev.all(), send.all(), ALU.add)
            if self.debug and li == 0:
                for ct in range(8):
                    fw.dma("sp", self.dbg_y[ct * 128:(ct + 1) * 128, t0:t0 + TB], ysb[:, ct, :])
            for ct in range(8):
                fw.act(ysb[:, ct, :], ysb[:, ct, :], AF.Gelu_apprx_tanh)
                fw.copy(ygb[:, ct, :], ysb[:, ct, :], eng="pool")
            for mt in range(8):
                ps = fw.next("ps")
                for kt in range(8):
                    fw.matmul(ps[:, 0:TB], gluW[:, kt, mt * 128:(mt + 1) * 128], ygb[:, kt, :],
                              start=(kt == 0), stop=(kt == 7))
                sg = fw.next("s5sg")
                fw.act(sg.all(), ps[:, 0:TB], AF.Sigmoid, bias=vec[:, 8 + mt:9 + mt])
                fw.tt(sg.all(), sg.all(), ysb[:, mt, :], ALU.mult)
                zT = fw.next("s5zT")
                r0 = (CT_SZ + mt) * 128
                fw.dma("sp", zT.all(), self.projT[r0:r0 + 128, t0:t0 + TB])
                fw.act(zT.all(), zT.all(), AF.Silu)
                oc = fw.next("s5oc")
                fw.tt(oc.all(), sg.all(), zT.all(), ALU.mult)
                fw.dma("sp", self.oT[2048 + mt * 128:2048 + (mt + 1) * 128, t0:t0 + TB], oc.all())
        fw.pop()


    def neumann(self, A_T, A_, tagn):
        fw = self.fw
        X = fw.next("nm_x")
        fw.tt(X.all(), self.identb.all(), A_T.all(), ALU.subtract)
        P, Q = A_T, A_
        for l in range(1, 7):
            psq = fw.next("ps")
            fw.matmul(psq[:, 0:128], P.all(), Q.all())
            if l < 6:
                fw.matmul(psq[:, 128:256], Q.all(), P.all())
            Qn = fw.next("nm_q")
            fw.copy(Qn.all(), psq[:, 0:128], eng="act")
            if l < 6:
                Pn = fw.next("nm_p")
                fw.copy(Pn.all(), psq[:, 128:256], eng="act")
            psx = fw.next("ps")
            fw.matmul(psx[:, 0:128], Qn.all(), X.all())
            Xn = fw.next("nm_x")
            fw.tt(Xn.all(), X.all(), psx[:, 0:128], ALU.add)
            X = Xn
            Q = Qn
            if l < 6:
                P = Pn
        psr = fw.next("ps")
        fw.matmul(psr[:, 0:128], A_.all(), X.all())
        pstv = psr.all().x(lambda a: a.bitcast(BF16))
        xtv = pstv.x(lambda a: a[:, 512:640])
        fw.transpose(xtv, X.all(), self.identb.all())
        t_ = fw.next("nm_p")
        fw.tt(t_.all(), self.identb.all(), X.all(), ALU.subtract)
        Rb = fw.next("nm_q")
        fw.tt(Rb.all(), t_.all(), psr[:, 0:128], ALU.subtract)
        XT = fw.next("nm_xt")
        fw.copy(XT.all(), xtv)
        psd = fw.next("ps")
        fw.matmul(psd[:, 0:128], XT.all(), Rb.all())
        dX = fw.next("nm_dx")
        fw.copy(dX.all(), psd[:, 0:128], eng="act")
        return X, dX

    def nb_alloc(self, n, tagn):
        fw = self.fw
        mk = lambda nm: [fw.sbuf("%s_%s%d" % (tagn, nm, i), [128, n, 128], BF16) for i in range(2)]
        return dict(P=mk("nbP"), Q=mk("nbQ"), X=mk("nbX"),
                    dX=fw.sbuf(tagn + "_nbdX", [128, n, 128], BF16), XT=fw.sbuf(tagn + "_nbXT", [128, n, 128], BF16))

    def neumann_batch(self, AT, A_, n, nbuf):
        fw = self.fw
        G = 4
        ng = n // G
        Pb, Qb, Xb, dXt, XTt = nbuf["P"], nbuf["Q"], nbuf["X"], nbuf["dX"], nbuf["XT"]
        fl = lambda v: v.x(lambda a: a.rearrange("p g c -> p (g c)"))
        idb = self.identb.all().x(lambda a: a.unsqueeze(1).to_broadcast([128, G, 128]))
        for g in range(ng):
            gs = slice(g * G, (g + 1) * G)
            fw.tt(Xb[0][:, gs, :], idb, AT[:, gs, :], ALU.subtract)
        P, Q, X = AT, A_, Xb[0]
        xi = 0
        for l in range(1, 7):
            Qn, Pn, Xn = Qb[l % 2], Pb[l % 2], Xb[(xi + 1) % 2]
            for g in range(ng):
                gs = slice(g * G, (g + 1) * G)
                psQ = fw.next("ps")
                for k in range(G):
                    fw.matmul(psQ[:, k * 128:(k + 1) * 128], P[:, g * G + k, :], Q[:, g * G + k, :])
                fw.copy(fl(Qn[:, gs, :]), psQ.all(), eng="act")
                if l < 6:
                    psP = fw.next("ps")
                    for k in range(G):
                        fw.matmul(psP[:, k * 128:(k + 1) * 128], Q[:, g * G + k, :], P[:, g * G + k, :])
                    fw.copy(fl(Pn[:, gs, :]), psP.all(), eng="act")
                psX = fw.next("ps")
                for k in range(G):
                    fw.matmul(psX[:, k * 128:(k + 1) * 128], Qn[:, g * G + k, :], X[:, g * G + k, :])
                fw.tt(fl(Xn[:, gs, :]), fl(X[:, gs, :]), psX.all(), ALU.add)
            P, Q, X = Pn, Qn, Xn
            xi += 1
        tmpR = Pb[0]
        Rb = Pb[1]
        for g in range(ng):
            gs = slice(g * G, (g + 1) * G)
            psr = fw.next("ps")
            for k in range(G):
                fw.matmul(psr[:, k * 128:(k + 1) * 128], A_[:, g * G + k, :], X[:, g * G + k, :])
            pst_ = fw.next("ps")
            pstv = pst_.all().x(lambda a: a.bitcast(BF16))
            for k in range(G):
                fw.transpose(pstv.x(lambda a, k=k: a[:, k * 128:(k + 1) * 128]), X[:, g * G + k, :], self.identb.all())
            fw.tt(tmpR[:, gs, :], idb, X[:, gs, :], ALU.subtract)
            fw.tt(fl(Rb[:, gs, :]), fl(tmpR[:, gs, :]), psr.all(), ALU.subtract)
            fw.copy(fl(XTt[:, gs, :]), pstv.x(lambda a: a[:, 0:512]), eng="act")
            psd = fw.next("ps")
            for k in range(G):
                fw.matmul(psd[:, k * 128:(k + 1) * 128], XTt[:, g * G + k, :], Rb[:, g * G + k, :])
            fw.copy(fl(dXt[:, gs, :]), psd.all(), eng="act")
        return X, dXt

    def phase_GDN(self, li):
        fw, L, NL = self.fw, self.L, self.NL
        C = 128
        NCH = L // C
        fw.push()
        sel = fw.sbuf("g_sel", [64, 16, 128], F32)
        fw.dma("sp", sel.all().x(lambda a: a.rearrange("p a b -> p (a b)")), self.cst["c_sel"].all())
        msk = {}
        for nm in ("su", "sl", "iu"):
            msk[nm] = fw.sbuf("g_m" + nm, [128, 128], F32)
            fw.dma("sp", msk[nm].all(), self.cst["c_mask_" + nm].all())
        par = fw.sbuf("g_par", [64, 2], F32)
        fw.dma("sp", par.all(), self.gdn_par[:, li * 2:(li + 1) * 2])
        vec = fw.sbuf("g_vec", [128, 5, 24], F32)
        fw.dma("sp", vec.all().x(lambda a: a.rearrange("p a b -> p (a b)")), self.gdn_vec[:, li * 120:(li + 1) * 120])
        onec = fw.sbuf("g_onec", [128, 1], F32)
        fw.memset(onec.all(), 1.0)
        epsc = fw.sbuf("g_epsc", [128, 1], F32)
        fw.memset(epsc.all(), 1e-6)
        nA = fw.sbuf("g_nA", [64, 1], F32)
        fw.act(nA.all(), par[:, 1:2], AF.Exp)
        fw.ts(nA.all(), nA.all(), -1.0, ALU.mult)
        SCX = fw.sbuf("g_scx", [64, NCH, 2, C], F32)
        NSC = fw.sbuf("g_nsc", [64, NCH, C], F32)
        TSS = fw.sbuf("g_tss", [128, NCH, 24], F32)
        QT = fw.sbuf("g_qT", [128, 8, L], BF16)
        KTt = fw.sbuf("g_kT", [128, 8, L], BF16)
        VT = fw.sbuf("g_vT", [128, 8, L], BF16)
        fw.push()
        ba = fw.sbuf("g_ba", [64, L], F32)
        gg = fw.sbuf("g_g", [64, L], F32)
        fw.dma("sp", ba.all(), self.projT[CT_GBA * 128:CT_GBA * 128 + 64, :])
        fw.memset(SCX.all(), 0.0)
        fw.memset(NSC.all(), 0.0)
        scx0 = SCX[:, :, 0, :]
        v3 = lambda v: v.x(lambda a: a.rearrange("p (n c) -> p n c", c=C))
        fw.act(gg[0:8, :], ba[0:8, :], AF.Sigmoid)
        fw.act(SCX[0:8, :, 0, :], v3(gg[0:8, :]), AF.Ln)
        fw.act(gg[32:40, :], ba[32:40, :], AF.Exp, bias=par[32:40, 0:1])
        fw.act(gg[32:40, :], gg[32:40, :], AF.Ln, bias=onec[32:40, 0:1])
        fw.ts(gg[32:40, :], gg[32:40, :], nA[32:40, 0:1], ALU.mult)
        for ch in range(NCH):
            fw.scan(SCX[32:40, ch, 0, :], onec[32:40, 0:1].bc([8, C]), gg[32:40, ch * C:(ch + 1) * C], 0.0)
            fw.act(SCX[32:40, ch, 1, :], SCX[32:40, ch, 0, :], AF.Identity, scale=-1.0, bias=SCX[32:40, ch, 0, C - 1:C])
            fw.ts(NSC[32:40, ch, :], SCX[32:40, ch, 0, :], -1.0, ALU.mult)
        for ch in range(NCH):
            ps = fw.next("ps")
            fw.matmul(ps[:, 0:64], SCX[:, ch, 0, :], self.ident[0:64, 0:64])
            fw.matmul(ps[:, 64:128], SCX[:, ch, 1, :], self.ident[0:64, 0:64])
            tsr = fw.sbuf("g_tsr%d" % (ch % 2), [128, 128], F32) if ch < 2 else fw.tensors["g_tsr%d" % (ch % 2)]
            fw.copy(tsr.all(), ps[:, 0:128])
            fw.act(TSS[:, ch, 0:8], tsr[:, 0:8], AF.Exp)
            fw.tt(tsr[:, 8:16], tsr[:, 0:8], tsr[:, 32:40], ALU.add)
            fw.act(TSS[:, ch, 8:16], tsr[:, 8:16], AF.Exp)
            fw.act(TSS[:, ch, 16:24], tsr[:, 96:104], AF.Exp)
        fw.pop()
        fw.push()
        fw.ring("g_raw", 2, [128, L + 3], F32)
        fw.ring("g_acc", 2, [128, L], F32)
        fw.ring("g_sq", 2, [128, 512], BF16)
        fw.ring("g_rn", 2, [128, 512], F32)
        for j in range(24):
            raw = fw.next("g_raw")
            fw.memset(raw[:, 0:3], 0.0, eng="pool")
            fw.dma("sp", raw[:, 3:L + 3], self.projT[(CT_GQKV + j) * 128:(CT_GQKV + j + 1) * 128, :])
            acc = fw.next("g_acc")
            fw.ts(acc.all(), raw[:, 0:L], vec[:, 0, j:j + 1], ALU.mult)
            for k in range(1, 4):
                fw.stt(acc.all(), raw[:, k:k + L], vec[:, k, j:j + 1], acc.all(), ALU.mult, ALU.add)
            typ, h = j // 8, j % 8
            if typ == 2:
                fw.act(VT[:, h, :], acc.all(), AF.Silu)
                continue
            fw.act(acc.all(), acc.all(), AF.Silu)
            dst = QT if typ == 0 else KTt
            for tb in range(self.NB):
                ts_ = slice(tb * 512, (tb + 1) * 512)
                sq = fw.next("g_sq")
                fw.tt(sq.all(), acc[:, ts_], acc[:, ts_], ALU.mult, eng="pool")
                ps = fw.next("ps")
                fw.matmul(ps.all(), self.onesb.all(), sq.all())
                rn = fw.next("g_rn")
                fw.act(rn.all(), ps.all(), AF.Ln, bias=epsc[:, 0:1])
                fw.act(rn.all(), rn.all(), AF.Exp, scale=-0.5)
                if typ == 0:
                    fw.stt(dst[:, h, ts_], acc[:, ts_], float(128 ** -0.5), rn.all(), ALU.mult, ALU.mult)
                else:
                    fw.tt(dst[:, h, ts_], acc[:, ts_], rn.all(), ALU.mult)
        fw.pop()
        S32 = fw.sbuf("g_S32", [128, 8, 128], F32)
        Sb = fw.sbuf("g_Sb", [128, 8, 128], BF16)
        fw.memset(S32.all(), 0.0)
        fw.memset(Sb.all(), 0.0)
        import os
        GD = F32 if os.environ.get("GDN_F32", "0") == "1" else BF16
        for nm, n_ in (("nm_x", 3), ("nm_q", 2), ("nm_p", 2), ("nm_xt", 2), ("nm_dx", 2), ("g_AT", 2), ("g_A", 2), ("g_at", 2),
                       ("g_ktk", 2), ("g_kd", 2), ("g_vb", 2), ("g_wT", 2), ("g_vn", 2), ("g_qd", 2)):
            fw.ring(nm, n_, [128, 128], GD)
        if GD == F32:
            Sb = S32
        fw.ring("g_D", 6, [128, 128], F32)
        fw.ring("g_gam", 4, [128, 128], F32)
        obuf = fw.sbuf("g_obuf", [128, 8, C], F32)
        osq = fw.sbuf("g_osq", [128, 8, C], BF16)
        zt = fw.sbuf("g_zt", [128, 8, C], F32)
        ogb = fw.sbuf("g_ogb", [128, 8, C], BF16)
        epsn = fw.sbuf("g_epsn", [128, 1], F32)
        fw.memset(epsn.all(), NORM_EPS)
        nbuf_g = self.nb_alloc(8, "g")
        ATg = fw.sbuf("g_ATg", [128, 8, 128], BF16)
        Ag = fw.sbuf("g_Ag", [128, 8, 128], BF16)
        fw.ring("g_at8", 9, [128, 128], BF16)
        fw.ring("g_ktk8", 9, [128, 128], BF16)
        fw.ring("g_kd8", 9, [128, 128], BF16)
        fw.ring("g_vb8", 9, [128, 128], BF16)
        fw.ring("g_qd8", 9, [128, 128], BF16)
        fw.ring("g_gl8", 9, [128, 1], F32)
        for ch in range(NCH):
            cs_ = slice(ch * C, (ch + 1) * C)
            per = []
            for h in range(8):
                selBG, selG = sel[:, h, :], sel[:, 8 + h, :]
                sc, nsc = SCX[:, ch, 0, :], NSC[:, ch, :]
                kT, qT, vT = KTt[:, h, cs_], QT[:, h, cs_], VT[:, h, cs_]
                ps1 = fw.next("ps")
                fw.matmul(ps1[:, 0:128], selBG, sc, start=True, stop=False)
                fw.matmul(ps1[:, 0:128], nsc, selG, start=False, stop=False)
                fw.matmul(ps1[:, 0:128], self.ident.all(), msk["su"].all(), start=False, stop=True)
                fw.matmul(ps1[:, 128:256], sc, selBG, start=True, stop=False)
                fw.matmul(ps1[:, 128:256], selG, nsc, start=False, stop=False)
                fw.matmul(ps1[:, 128:256], self.ident.all(), msk["sl"].all(), start=False, stop=True)
                fw.matmul(ps1[:, 256:384], selG, sc, start=True, stop=False)
                fw.matmul(ps1[:, 256:384], nsc, selG, start=False, stop=False)
                fw.matmul(ps1[:, 256:384], self.ident.all(), msk["iu"].all(), start=False, stop=True)
                fw.matmul(ps1[:, 384:512], selG, sc, start=True, stop=True)
                DT_, D_, DI_ = fw.next("g_D"), fw.next("g_D"), fw.next("g_D")
                gam = fw.next("g_gam")
                fw.act(DT_.all(), ps1[:, 0:128], AF.Exp)
                fw.act(D_.all(), ps1[:, 128:256], AF.Exp)
                fw.act(DI_.all(), ps1[:, 256:384], AF.Exp)
                fw.act(gam.all(), ps1[:, 384:512], AF.Exp)
                ps2 = fw.next("ps")
                fw.matmul(ps2[:, 0:128], kT, kT)
                fw.matmul(ps2[:, 128:256], kT, qT)
                at = fw.next("g_at8")
                fw.tt(ATg[:, h, :], ps2[:, 0:128], DT_.all(), ALU.mult)
                fw.tt(Ag[:, h, :], ps2[:, 0:128], D_.all(), ALU.mult)
                fw.tt(at.all(), ps2[:, 128:256], DI_.all(), ALU.mult)
                ps3 = fw.next("ps")
                pst = ps3.all().x(lambda a: a.bitcast(BF16))
                fw.transpose(pst.x(lambda a: a[:, 0:128]), kT, self.identb.all())
                fw.transpose(pst.x(lambda a: a[:, 128:256]), vT, self.identb.all())
                ktk, kd, vb = fw.next("g_ktk8"), fw.next("g_kd8"), fw.next("g_vb8")
                kview = pst.x(lambda a: a[:, 0:128])
                vview = pst.x(lambda a: a[:, 128:256])
                fw.ts(ktk.all(), kview, TSS[:, ch, 8 + h:9 + h], ALU.mult)
                fw.ts(kd.all(), kview, TSS[:, ch, 16 + h:17 + h], ALU.mult)
                fw.ts(vb.all(), vview, TSS[:, ch, h:h + 1], ALU.mult)
                qd = fw.next("g_qd8")
                fw.tt(qd.all(), qT, gam.all(), ALU.mult, eng="pool")
                gl = fw.next("g_gl8")
                fw.copy(gl.all(), gam[:, C - 1:C], eng="pool")
                per.append((at, ktk, kd, vb, qd, gl))
            Xg, dXg = self.neumann_batch(ATg, Ag, 8, nbuf_g)
            for h in range(8):
                at, ktk, kd, vb, qd, gl = per[h]
                ps4 = fw.next("ps")
                fw.matmul(ps4[:, 0:128], ktk.all(), Xg[:, h, :], start=True, stop=False)
                fw.matmul(ps4[:, 0:128], ktk.all(), dXg[:, h, :], start=False, stop=True)
                wT = fw.next("g_wT")
                fw.ts(wT.all(), ps4[:, 0:128], -1.0, ALU.mult)
                ps5 = fw.next("ps")
                fw.matmul(ps5[:, 0:128], Xg[:, h, :], vb.all(), start=True, stop=False)
                fw.matmul(ps5[:, 0:128], dXg[:, h, :], vb.all(), start=False, stop=False)
                fw.matmul(ps5[:, 0:128], wT.all(), Sb[:, h, :], start=False, stop=True)
                vn = fw.next("g_vn")
                fw.copy(vn.all(), ps5[:, 0:128], eng="act")
                ps6 = fw.next("ps")
                fw.matmul(ps6[:, 0:128], Sb[:, h, :], qd.all(), start=True, stop=False)
                fw.matmul(ps6[:, 0:128], vn.all(), at.all(), start=False, stop=True)
                fw.matmul(ps6[:, 128:256], kd.all(), vn.all())
                fw.copy(obuf[:, h, :], ps6[:, 0:128], eng="act")
                fw.stt(S32[:, h, :], S32[:, h, :], gl[:, 0:1], ps6[:, 128:256], ALU.mult, ALU.add)
                if GD != F32:
                    fw.copy(Sb[:, h, :], S32[:, h, :], eng="pool")
            fw.tt(osq.all(), obuf.all(), obuf.all(), ALU.mult, eng="pool")
            fw.dma("sp", zt.all(), self.projT[CT_GZ * 128:(CT_GZ + 8) * 128, cs_].x(
                lambda a: a.rearrange("(h p) t -> p h t", p=128)))
            fw.act(zt.all(), zt.all(), AF.Silu)
            for hh in range(2):
                ps = fw.next("ps")
                fw.matmul(ps.all(), self.onesb.all(),
                          osq[:, hh * 4:(hh + 1) * 4, :].x(lambda a: a.rearrange("p h c -> p (h c)")))
                rn = fw.sbuf("g_rn2", [128, 512], F32) if (ch == 0 and hh == 0) else fw.tensors["g_rn2"]
                fw.ts(rn.all(), ps.all(), 1.0 / 128, ALU.mult)
                fw.act(rn.all(), rn.all(), AF.Ln, bias=epsn[:, 0:1])
                fw.act(rn.all(), rn.all(), AF.Exp, scale=-0.5)
                ov = obuf[:, hh * 4:(hh + 1) * 4, :].x(lambda a: a.rearrange("p h c -> p (h c)"))
                zv = zt[:, hh * 4:(hh + 1) * 4, :].x(lambda a: a.rearrange("p h c -> p (h c)"))
                gv_ = ogb[:, hh * 4:(hh + 1) * 4, :].x(lambda a: a.rearrange("p h c -> p (h c)"))
                fw.stt(rn.all(), ov, vec[:, 4, 0:1], rn.all(), ALU.mult, ALU.mult)
                fw.tt(gv_, rn.all(), zv, ALU.mult)
            fw.dma("sp", self.oT[0:1024, cs_].x(lambda a: a.rearrange("(h p) t -> p h t", p=128)), ogb.all())
        fw.pop()


    def phase_RWKV(self, li):
        fw, L, NL = self.fw, self.L, self.NL
        C = 128
        NCH = L // C
        MID = 63
        fw.push()
        rv = fw.sbuf("r_rv", [128, 10, 8], F32)
        fw.dma("sp", rv.all().x(lambda a: a.rearrange("p a b -> p (a b)")), self.rw_vec[:, li * 80:(li + 1) * 80])
        mul = fw.sbuf("r_mul", [128, 2], F32)
        fw.dma("sp", mul.all(), self.rw_mul[:, li * 2:(li + 1) * 2])
        ups = fw.sbuf("r_ups", [96, 2048], BF16)
        fw.dma("pool", ups.all(), self.rw_up[li * 96:(li + 1) * 96, :])
        omka = fw.sbuf("r_omka", [128, 8], F32)
        fw.ts(omka.all(), rv[:, 6, :], -1.0, ALU.mult, 1.0, ALU.add)
        m01 = {}
        for nm in ("su", "sl", "iu"):
            m01[nm] = fw.sbuf("r_m" + nm, [128, 128], F32)
            fw.dma("sp", m01[nm].all(), self.cst["c_m01_" + nm].all())
        blkf = fw.sbuf("r_blkf", [128, 128], F32)
        blkb = fw.sbuf("r_blkb", [128, 128], BF16)
        fw.dma("sp", blkf.all(), self.cst["c_blk64"].all())
        fw.dma("pool", blkb.all(), self.cst["c_blk64"].all())
        onec = fw.sbuf("r_onec", [128, 1], F32)
        fw.memset(onec.all(), 1.0)
        eps6 = fw.sbuf("r_eps6", [128, 1], F32)
        fw.memset(eps6.all(), 1e-6)
        epsl = fw.sbuf("r_epsl", [128, 1], F32)
        fw.memset(epsl.all(), 64e-5)
        S32 = fw.sbuf("r_S32", [128, 8, 64], F32)
        fw.memset(S32.all(), 0.0)
        YB = fw.sbuf("r_YB", [128, 8, C], F32)
        BON = fw.sbuf("r_BON", [128, 8, C], F32)
        ZT = fw.sbuf("r_ZT", [128, 8, C], F32)
        OB = fw.sbuf("r_OB", [128, 8, C], BF16)
        for nm, n_ in (("r_BT", 17), ("r_M1", 17),
                       ("r_M2", 17), ("r_rh", 9), ("r_kh", 9), ("r_ah", 2), ("r_kkh", 9), ("r_vb", 2),
                       ("r_khT", 9), ("r_ahT", 9), ("r_vT", 9)):
            fw.ring(nm, n_, [128, 128], BF16)
        nbuf_r = self.nb_alloc(16, "r")
        ATr = fw.sbuf("r_ATr", [128, 16, 128], BF16)
        Ar = fw.sbuf("r_Ar", [128, 16, 128], BF16)
        for nm in ("r_raw",):
            fw.ring(nm, 8, [128, C + 1], F32)
        for nm in ("r_rs", "r_ks", "r_vs", "r_d", "r_lw", "r_a", "r_kkr", "r_kk", "r_t3", "r_km", "r_at", "r_cs",
                   "r_csp", "r_Wi", "r_Wp", "r_tmp"):
            fw.ring(nm, 4, [128, C], F32)
        fw.ring("r_Wt", 9, [128, C], F32)
        fw.ring("r_sqb", 4, [128, C], BF16)
        fw.ring("r_tw", 2, [128, C], BF16)
        fw.ring("r_al", 2, [128, C], BF16)
        fw.ring("r_col", 9, [128, 4], F32)
        fw.ring("r_Sh", 2, [128, 64], BF16)
        fw.ring("r_rhs", 2, [128, 64], BF16)
        fw.ring("r_nP", 2, [128, 64], BF16)
        fw.ring("r_st", 2, [128, 64], F32)
        fw.ring("r_e512", 3, [128, 512], F32)

        def shifted(ct_row0, ch, mu_ap, out, nrows=128):
            raw = fw.next("r_raw")
            t0 = ch * C
            if ch == 0:
                fw.memset(raw[0:nrows, 0:1], 0.0, eng="pool")
                fw.dma("sp", raw[0:nrows, 1:C + 1], self.projT[ct_row0:ct_row0 + nrows, 0:C])
            else:
                fw.dma("sp", raw[0:nrows, :], self.projT[ct_row0:ct_row0 + nrows, t0 - 1:t0 + C])
            d = fw.next("r_d")
            fw.tt(d[0:nrows, :], raw[0:nrows, 0:C], raw[0:nrows, 1:C + 1], ALU.subtract, eng="pool")
            fw.stt(out, d[0:nrows, :], mu_ap, raw[0:nrows, 1:C + 1], ALU.mult, ALU.add)

        for ch in range(NCH):
            cs_ = slice(ch * C, (ch + 1) * C)
            tw, alb, tmp = fw.next("r_tw"), fw.next("r_al"), fw.next("r_tmp")
            shifted(CT_RWL * 128, ch, mul[0:96, 0:1], tmp[0:96, :], nrows=96)
            fw.act(tw[0:96, :], tmp[0:96, :], AF.Tanh)
            tmp2 = fw.next("r_tmp")
            shifted(CT_RAL * 128, ch, mul[0:96, 1:2], tmp2[0:96, :], nrows=96)
            fw.copy(alb[0:96, :], tmp2[0:96, :], eng="pool")
            fw.dma("sp", ZT.all(), self.projT[CT_RZ * 128:(CT_RZ + 8) * 128, cs_].x(
                lambda a: a.rearrange("(h p) t -> p h t", p=128)))
            fw.act(ZT.all(), ZT.all(), AF.Silu)
            perp, perh = [], []
            for p in range(8):
                rs, ks, vs = fw.next("r_rs"), fw.next("r_ks"), fw.next("r_vs")
                shifted((CT_RR + p) * 128, ch, rv[:, 0, p:p + 1], rs.all())
                shifted((CT_RK + p) * 128, ch, rv[:, 1, p:p + 1], ks.all())
                shifted((CT_RV + p) * 128, ch, rv[:, 2, p:p + 1], vs.all())
                psl = fw.next("ps")
                fw.matmul(psl[:, 0:C], ups[:, p * 128:(p + 1) * 128], tw[0:96, :])
                fw.matmul(psl[:, C:2 * C], ups[:, 1024 + p * 128:1024 + (p + 1) * 128], alb[0:96, :])
                lw, a_ = fw.next("r_lw"), fw.next("r_a")
                fw.act(lw.all(), psl[:, 0:C], AF.Sigmoid, bias=rv[:, 3, p:p + 1])
                fw.act(a_.all(), psl[:, C:2 * C], AF.Sigmoid, bias=rv[:, 4, p:p + 1])
                fw.ts(lw.all(), lw.all(), float(-np.exp(-0.5)), ALU.mult)
                kkr, sqb, kk = fw.next("r_kkr"), fw.next("r_sqb"), fw.next("r_kk")
                fw.ts(kkr.all(), ks.all(), rv[:, 5, p:p + 1], ALU.mult)
                fw.tt(sqb.all(), kkr.all(), kkr.all(), ALU.mult, eng="pool")
                t3, km, at = fw.next("r_t3"), fw.next("r_km"), fw.next("r_at")
                fw.ts(t3.all(), a_.all(), rv[:, 6, p:p + 1], ALU.mult, omka[:, p:p + 1], ALU.add)
                fw.tt(km.all(), ks.all(), t3.all(), ALU.mult)
                sqb2 = fw.next("r_sqb")
                fw.stt(sqb2.all(), rs.all(), rv[:, 7, p:p + 1], km.all(), ALU.mult, ALU.mult)
                pss = fw.next("ps")
                fw.matmul(pss[:, 0:C], blkb.all(), sqb.all())
                fw.matmul(pss[:, C:2 * C], blkb.all(), sqb2.all())
                rn = fw.next("r_tmp")
                fw.act(rn.all(), pss[:, 0:C], AF.Ln, bias=eps6[:, 0:1])
                fw.act(rn.all(), rn.all(), AF.Exp, scale=-0.5)
                fw.tt(kk.all(), kkr.all(), rn.all(), ALU.mult)
                fw.tt(BON[:, p, :], pss[:, C:2 * C], vs.all(), ALU.mult)
                fw.tt(at.all(), a_.all(), kk.all(), ALU.mult)
                csum, csp = fw.next("r_cs"), fw.next("r_csp")
                fw.scan(csum.all(), onec[:, 0:1].bc([128, C]), lw.all(), 0.0)
                fw.tt(csp.all(), csum.all(), lw.all(), ALU.subtract)
                col = fw.next("r_col")
                fw.ts(col[:, 0:1], csum[:, MID:MID + 1], -1.0, ALU.mult)
                Wt, Wi, Wp = fw.next("r_Wt"), fw.next("r_Wi"), fw.next("r_Wp")
                fw.act(Wt.all(), csum.all(), AF.Exp, bias=col[:, 0:1])
                fw.act(Wi.all(), csum.all(), AF.Exp, scale=-1.0, bias=csum[:, MID:MID + 1])
                fw.act(Wp.all(), csp.all(), AF.Exp, bias=col[:, 0:1])
                fw.op("dve", "reciprocal", [Wp[:, 0:1]], [col[:, 1:2]], col[:, 1:2], Wp[:, 0:1])
                rh, kh, ah, kkh, vb = fw.next("r_rh"), fw.next("r_kh"), fw.next("r_ah"), fw.next("r_kkh"), fw.next("r_vb")
                fw.tt(rh.all(), rs.all(), Wt.all(), ALU.mult)
                fw.tt(kh.all(), km.all(), Wi.all(), ALU.mult)
                fw.tt(ah.all(), at.all(), Wi.all(), ALU.mult)
                fw.tt(kkh.all(), kk.all(), Wp.all(), ALU.mult, eng="pool")
                fw.copy(vb.all(), vs.all(), eng="act")
                pst_ = fw.next("ps")
                pst = pst_.all().x(lambda a: a.bitcast(BF16))
                fw.transpose(pst.x(lambda a: a[:, 0:128]), kh.all(), self.identb.all())
                fw.transpose(pst.x(lambda a: a[:, 128:256]), ah.all(), self.identb.all())
                fw.transpose(pst.x(lambda a: a[:, 256:384]), vb.all(), self.identb.all())
                khT, ahT, vT = fw.next("r_khT"), fw.next("r_ahT"), fw.next("r_vT")
                fw.copy(khT.all(), pst.x(lambda a: a[:, 0:128]), eng="act")
                fw.copy(ahT.all(), pst.x(lambda a: a[:, 128:256]), eng="act")
                fw.copy(vT.all(), pst.x(lambda a: a[:, 256:384]), eng="act")
                for hh in range(2):
                    rows = slice(hh * 64, hh * 64 + 64)
                    hidx = p * 2 + hh
                    pa = fw.next("ps")
                    fw.matmul(pa[:, 0:128], ah[rows, :], kkh[rows, :])
                    fw.matmul(pa[:, 128:256], kkh[rows, :], ah[rows, :])
                    fw.matmul(pa[:, 256:384], kh[rows, :], kkh[rows, :])
                    BT = fw.next("r_BT")
                    fw.tt(ATr[:, hidx, :], pa[:, 0:128], m01["su"].all(), ALU.mult)
                    fw.tt(Ar[:, hidx, :], pa[:, 128:256], m01["sl"].all(), ALU.mult)
                    fw.tt(BT.all(), pa[:, 256:384], m01["su"].all(), ALU.mult)
                    pb = fw.next("ps")
                    fw.matmul(pb[:, 0:128], kh[rows, :], rh[rows, :])
                    fw.matmul(pb[:, 128:256], ah[rows, :], rh[rows, :])
                    M1, M2 = fw.next("r_M1"), fw.next("r_M2")
                    fw.tt(M1.all(), pb[:, 0:128], m01["iu"].all(), ALU.mult)
                    fw.tt(M2.all(), pb[:, 128:256], m01["iu"].all(), ALU.mult)
                    perh.append((BT, M1, M2))
                perp.append((rh, kh, kkh, vT, khT, ahT, col, Wt))
            Xr, dXr = self.neumann_batch(ATr, Ar, 16, nbuf_r)
            for p in range(8):
                rh, kh, kkh, vT, khT, ahT, col, Wt = perp[p]
                Sh = fw.next("r_Sh")
                fw.ts(Sh.all(), S32[:, p, :], col[:, 1:2], ALU.mult)
                psY = fw.next("psacc")
                psS = fw.next("psacc")
                for hh in range(2):
                    rows = slice(hh * 64, hh * 64 + 64)
                    hidx = p * 2 + hh
                    BT, M1, M2 = perh[hidx]
                    pr = fw.next("ps")
                    fw.matmul(pr[:, 0:64], kkh[rows, :], Sh[rows, :], start=True, stop=False)
                    fw.matmul(pr[:, 0:64], BT.all(), vT[:, rows], start=False, stop=True)
                    rhs = fw.next("r_rhs")
                    fw.copy(rhs.all(), pr[:, 0:64], eng="act")
                    fw.matmul(pr[:, 64:128], Xr[:, hidx, :], rhs.all(), start=True, stop=False)
                    fw.matmul(pr[:, 64:128], dXr[:, hidx, :], rhs.all(), start=False, stop=True)
                    nP = fw.next("r_nP")
                    fw.ts(nP.all(), pr[:, 64:128], -1.0, ALU.mult)
                    fw.matmul(psY[rows, 0:C], Sh[rows, :], rh[rows, :], start=True, stop=False)
                    fw.matmul(psY[rows, 0:C], vT[:, rows], M1.all(), start=False, stop=False)
                    fw.matmul(psY[rows, 0:C], nP.all(), M2.all(), start=False, stop=True)
                    fw.matmul(psS[rows, 0:64], khT[:, rows], vT[:, rows], start=True, stop=False)
                    fw.matmul(psS[rows, 0:64], ahT[:, rows], nP.all(), start=False, stop=True)
                fw.copy(YB[:, p, :], psY[:, 0:C], eng="act")
                st = fw.next("r_st")
                fw.stt(st.all(), S32[:, p, :], col[:, 1:2], psS[:, 0:64], ALU.mult, ALU.add)
                fw.ts(S32[:, p, :], st.all(), Wt[:, C - 1:C], ALU.mult)
            for hf in range(2):
                fl = lambda v: v.x(lambda a: a.rearrange("p h c -> p (h c)"))
                yv = fl(YB[:, hf * 4:(hf + 1) * 4, :])
                pm = fw.next("ps")
                fw.matmul(pm.all(), blkf.all(), yv)
                yc = fw.next("r_e512")
                fw.stt(yc.all(), pm.all(), -1.0 / 64, yv, ALU.mult, ALU.add)
                sq = fw.next("r_e512")
                fw.tt(sq.all(), yc.all(), yc.all(), ALU.mult, eng="pool")
                pv = fw.next("ps")
                fw.matmul(pv.all(), blkf.all(), sq.all())
                rs_ = fw.next("r_e512")
                fw.ts(rs_.all(), pv.all(), 1.0 / 64, ALU.mult)
                fw.act(rs_.all(), rs_.all(), AF.Ln, bias=epsl[:, 0:1])
                fw.act(rs_.all(), rs_.all(), AF.Exp, scale=-0.5)
                fw.tt(yc.all(), yc.all(), rs_.all(), ALU.mult)
                for q in range(4):
                    p = hf * 4 + q
                    fw.ts(YB[:, p, :], yc[:, q * C:(q + 1) * C], rv[:, 8, p:p + 1], ALU.mult, rv[:, 9, p:p + 1], ALU.add)
                fw.tt(yv, yv, fl(BON[:, hf * 4:(hf + 1) * 4, :]), ALU.add)
                fw.tt(fl(OB[:, hf * 4:(hf + 1) * 4, :]), yv, fl(ZT[:, hf * 4:(hf + 1) * 4, :]), ALU.mult)
            fw.dma("sp", self.oT[1024:2048, cs_].x(lambda a: a.rearrange("(h p) t -> p h t", p=128)), OB.all())
        fw.pop()


    def phase_C(self, li, xsrc, xdst):
        fw, L, NL = self.fw, self.L, self.NL
        fw.push()
        gb = fw.sbuf("c_gb", [128, 48], F32)
        fw.dma("sp", gb.all(), self.gate_b[:, li * 48:(li + 1) * 48])
        oall = fw.sbuf("c_o", [128, 24, 512], BF16)
        mg = fw.sbuf("c_mg", [128, 16, 512], BF16)
        fw.ring("c_wb", 3, [128, 1024], BF16)
        fw.ring("c_wo", 2, [128, 2048], BF16)
        fw.ring("c_gl", 3, [128, 512], F32)
        fw.ring("c_acc", 2, [128, 512], F32)
        fw.ring("c_tmp", 2, [128, 512], F32)
        fw.ring("c_x", 3, [128, 512], F32)
        for tb in range(self.NB):
            ts_ = slice(tb * 512, (tb + 1) * 512)
            fw.dma("sp", oall.all(), self.oT[:, ts_].x(lambda a: a.rearrange("(j p) t -> p j t", p=128)))
            for dt in range(16):
                acc = fw.next("c_acc")
                for n in range(3):
                    wb = fw.next("c_wb")
                    r0 = ((li * 3 + n) * 16 + dt) * 128
                    fw.dma("pool", wb.all(), self.w_br[r0:r0 + 128, :])
                    gl = fw.next("c_gl")
                    g0 = (CT_GATE + n * 16 + dt) * 128
                    fw.dma("sp", gl.all(), self.projT[g0:g0 + 128, ts_])
                    fw.act(gl.all(), gl.all(), AF.Sigmoid, bias=gb[:, n * 16 + dt:n * 16 + dt + 1])
                    ps = fw.next("ps")
                    for kt in range(8):
                        fw.matmul(ps.all(), wb[:, kt * 128:(kt + 1) * 128], oall[:, n * 8 + kt, :],
                                  start=(kt == 0), stop=(kt == 7))
                    if n == 0:
                        fw.tt(acc.all(), ps.all(), gl.all(), ALU.mult)
                    elif n == 1:
                        tmp = fw.next("c_tmp")
                        fw.tt(tmp.all(), ps.all(), gl.all(), ALU.mult)
                        fw.tt(acc.all(), acc.all(), tmp.all(), ALU.add, eng="pool")
                    else:
                        tmp = fw.next("c_tmp")
                        fw.tt(tmp.all(), ps.all(), gl.all(), ALU.mult)
                        fw.tt(mg[:, dt, :], acc.all(), tmp.all(), ALU.add, eng="pool")
            for dt in range(16):
                wo = fw.next("c_wo")
                r0 = (li * 16 + dt) * 128
                fw.dma("pool", wo.all(), self.w_out[r0:r0 + 128, :])
                xt = fw.next("c_x")
                fw.dma("sp", xt.all(), xsrc[dt * 128:(dt + 1) * 128, ts_])
                ps = fw.next("ps")
                for kt in range(16):
                    fw.matmul(ps.all(), wo[:, kt * 128:(kt + 1) * 128], mg[:, kt, :], start=(kt == 0), stop=(kt == 15))
                fw.tt(xt.all(), xt.all(), ps.all(), ALU.add)
                fw.dma("sp", xdst[dt * 128:(dt + 1) * 128, ts_], xt.all())
        fw.pop()

    def phase_final(self, xsrc):
        fw, L = self.fw, self.L
        fw.push()
        fnw = fw.sbuf("f_w", [128, 16], F32)
        fw.dma("sp", fnw.all(), self.fnorm_w.all())
        rstd = fw.sbuf("f_rstd", [128, 512], F32)
        fw.ring("f_xt", 3, [128, 512], F32)
        fw.ring("f_sq", 2, [128, 512], F32)
        for tb in range(self.NB):
            ts_ = slice(tb * 512, (tb + 1) * 512)
            ps = fw.next("ps")
            for kt in range(KT):
                xt = fw.next("f_xt")
                fw.dma("sp", xt.all(), xsrc[kt * 128:(kt + 1) * 128, ts_])
                sq = fw.next("f_sq")
                fw.act(sq.all(), xt.all(), AF.Square)
                fw.matmul(ps.all(), self.ones.all(), sq.all(), start=(kt == 0), stop=(kt == KT - 1))
            tmp = fw.next("f_sq")
            fw.ts(tmp.all(), ps.all(), 1.0 / D, ALU.mult, NORM_EPS, ALU.add)
            fw.act(tmp.all(), tmp.all(), AF.Ln)
            fw.act(rstd.all(), tmp.all(), AF.Exp, scale=-0.5)
            for kt in range(KT):
                xt = fw.next("f_xt")
                fw.dma("sp", xt.all(), xsrc[kt * 128:(kt + 1) * 128, ts_])
                fw.stt(xt.all(), xt.all(), fnw[:, kt:kt + 1], rstd.all(), ALU.mult, ALU.mult)
                fw.dma("sp", self.out[kt * 128:(kt + 1) * 128, ts_], xt.all())
        fw.pop()


def host_inputs(inputs, b, L, NL):
    f = np.float32
    m = {}
    m["xT"] = np.ascontiguousarray(inputs["x"][b, :L].T)
    m["w_in"] = np.concatenate([relayout_w_in(inputs["w_in"][i]) for i in range(NL)], axis=0)
    m["norm_w"] = np.ascontiguousarray(inputs["norm_w"][:NL].reshape(NL * KT, 128).T)
    m.update(make_consts())
    pg = np.zeros((128, NL, 3, 64), f)
    for i in range(NL):
        for h in range(2):
            pg[h * 64:(h + 1) * 64, i, 0] = inputs["s5_a_re"][i].T
            pg[h * 64:(h + 1) * 64, i, 1] = inputs["s5_a_im"][i].T
            pg[h * 64:(h + 1) * 64, i, 2] = inputs["s5_log_dt"][i][None, :]
    m["s5_pg"] = pg.reshape(128, -1)
    sb = np.zeros((64, NL, 2, 64, 16), f)
    sc = np.zeros((128, NL, 2, 64, 16), f)
    for i in range(NL):
        sb[:, i, 0] = inputs["s5_b_re"][i].transpose(1, 0, 2)
        sb[:, i, 1] = inputs["s5_b_im"][i].transpose(1, 0, 2)
        for h in range(2):
            sc[h * 64:(h + 1) * 64, i, 0] = inputs["s5_c_re"][i].transpose(2, 0, 1)
            sc[h * 64:(h + 1) * 64, i, 1] = inputs["s5_c_im"][i].transpose(2, 0, 1)
    m["s5_b"] = sb.reshape(64, -1)
    m["s5_c"] = sc.reshape(128, -1)
    vec = np.zeros((128, NL, 2, 8), f)
    for i in range(NL):
        vec[:, i, 0] = inputs["s5_d"][i].reshape(8, 128).T
        vec[:, i, 1] = inputs["s5_glu_b"][i].reshape(8, 128).T
    m["s5_vec"] = vec.reshape(128, -1)
    gp_ = np.zeros((64, NL, 2), f)
    gv = np.zeros((128, NL, 5, 24), f)
    for i in range(NL):
        gp_[32:40, i, 0] = inputs["gdn_dt_bias"][i]
        gp_[32:40, i, 1] = inputs["gdn_a_log"][i]
        gv[:, i, 0:4, :] = inputs["gdn_conv_w"][i].reshape(4, 24, 128).transpose(2, 0, 1)
        gv[:, i, 4, 0] = inputs["gdn_norm_w"][i]
    m["gdn_par"] = gp_.reshape(64, -1)
    m["gdn_vec"] = gv.reshape(128, -1)
    rv = np.zeros((128, NL, 10, 8), f)
    mul = np.zeros((128, NL, 2), f)
    for i in range(NL):
        mu = inputs["rwkv_mu"][i]
        t8 = lambda a: a.reshape(8, 128).T
        rv[:, i, 0] = t8(mu[0:1024]); rv[:, i, 1] = t8(mu[1024:2048]); rv[:, i, 2] = t8(mu[2048:3072])
        mul[0:96, i, 0] = mu[3072:3168]; mul[0:96, i, 1] = mu[3168:3264]
        rv[:, i, 3] = t8(inputs["rwkv_w0"][i]); rv[:, i, 4] = t8(inputs["rwkv_a0"][i])
        rv[:, i, 5] = t8(inputs["rwkv_k_k"][i]); rv[:, i, 6] = t8(inputs["rwkv_k_a"][i])
        rv[:, i, 7] = t8(inputs["rwkv_r_k"][i].reshape(1024))
        rv[:, i, 8] = t8(inputs["rwkv_lnx_w"][i]); rv[:, i, 9] = t8(inputs["rwkv_lnx_b"][i])
    m["rw_vec"] = rv.reshape(128, -1)
    m["rw_mul"] = mul.reshape(128, -1)
    m["rw_up"] = np.ascontiguousarray(np.concatenate(
        [np.concatenate([inputs["rwkv_w_up"][i], inputs["rwkv_a_up"][i]], axis=1) for i in range(NL)], axis=0))
    m["gate_b"] = np.ascontiguousarray(inputs["gate_b"][:NL].reshape(NL, 3, 16, 128).transpose(3, 0, 1, 2).reshape(128, -1))
    m["w_br"] = np.ascontiguousarray(
        inputs["w_branch"][:NL].reshape(NL, 3, 8, 128, 16, 128).transpose(0, 1, 4, 3, 2, 5).reshape(NL * 3 * 16 * 128, 1024))
    m["w_out"] = np.ascontiguousarray(
        inputs["w_out"][:NL].reshape(NL, 16, 128, 16, 128).transpose(0, 3, 2, 1, 4).reshape(NL * 16 * 128, 2048))
    m["fnorm_w"] = np.ascontiguousarray(inputs["final_norm_w"].reshape(16, 128).T)
    m["s5_glu"] = np.concatenate(
        [inputs["s5_glu_w"][i].reshape(8, 128, 1024).transpose(1, 0, 2).reshape(128, 8 * 1024) for i in range(NL)], axis=0)
    return m


def build(L, NL, debug=False, phases=("A", "S5", "GDN", "RWKV", "C", "F")):
    p = Prog(L, NL, debug)
    p.setup()
    xsrc = p.xT_in
    for li in range(NL):
        xdst = p.xT[li % 2]
        if "A" in phases:
            p.phase_A(li, xsrc)
        if "S5" in phases:
            p.phase_S5(li)
        if "GDN" in phases:
            p.phase_GDN(li)
        if "RWKV" in phases:
            p.phase_RWKV(li)
        if "C" in phases:
            p.phase_C(li, xsrc, xdst)
            xsrc = xdst
    if "F" in phases:
        p.phase_final(xsrc)
    stats = p.fw.emit()
    print("ops per engine:", stats)
    return p


_CACHE = {}


def kernel(**inputs):
    L, NL, B = 2048, 4, 4
    inputs = {k: np.asarray(v) for k, v in inputs.items()}
    if "prog" not in _CACHE:
        _CACHE["prog"] = build(L, NL)
    p = _CACHE["prog"]
    m0 = host_inputs(inputs, 0, L, NL)
    in_maps = [m0]
    for b in range(1, B):
        mb = dict(m0)
        mb["xT"] = np.ascontiguousarray(inputs["x"][b, :L].T)
        in_maps.append(mb)
    res = run_bass_kernel_spmd(p.nc, in_maps, core_ids=list(range(B)))
    out = np.stack([np.ascontiguousarray(res.results[b]["out"].T) for b in range(B)], axis=0)
    return out.astype(np.float32)
```
